# Optimizing a Trainium2 kernel written in Bass

```python
import math
import jax, jax.numpy as jnp
from jax import lax
import numpy as np

D_MODEL = 1024
BATCH = 8
SEQ = 2048
DEPTH = 4
DEC_BATCH = 128
DEC_SEQ = 1
PAST_LEN = 16384
PAGE_SIZE = 128

D_SSM = D_MODEL // 2
GROUP = 16
N_GROUPS = D_SSM // GROUP
P_STATE = 64
D_CONV = D_MODEL // 2
CONV_W = 31
D_FF = -(-8 * D_MODEL // (3 * 256)) * 256
D_IN = D_SSM + 2 * D_CONV + 2 * D_MODEL
DT_MIN = 1e-3
DT_MAX = 1e-1
EPS = 1e-6

kernel_name = "s5_conformer_gated_hybrid_step"


def rmsnorm(x, g):
    xf = x.astype(jnp.float32)
    y = xf * lax.rsqrt(jnp.mean(xf * xf, axis=-1, keepdims=True) + EPS)
    return (y * g.astype(jnp.float32)).astype(x.dtype)


def layernorm(x, g, b):
    xf = x.astype(jnp.float32)
    mu = jnp.mean(xf, axis=-1, keepdims=True)
    var = jnp.mean(jnp.square(xf - mu), axis=-1, keepdims=True)
    y = (xf - mu) * lax.rsqrt(var + EPS)
    return (y * g.astype(jnp.float32) + b.astype(jnp.float32)).astype(x.dtype)


def _cmul_combine(e1, e2):
    a1r, a1i, b1r, b1i = e1
    a2r, a2i, b2r, b2i = e2
    return (a2r * a1r - a2i * a1i,
            a2r * a1i + a2i * a1r,
            a2r * b1r - a2i * b1i + b2r,
            a2r * b1i + a2i * b1r + b2i)


def s5_layer(u, h0_re, h0_im, a_re, a_im, log_dt, b_re, b_im, c_re, c_im, d):
    f32 = jnp.float32
    bsz, seqlen = u.shape[0], u.shape[1]
    a_re, a_im = a_re.astype(f32), a_im.astype(f32)
    dt = jnp.exp(log_dt.astype(f32))[:, None]
    mag = jnp.exp(dt * a_re)
    ab_re, ab_im = mag * jnp.cos(dt * a_im), mag * jnp.sin(dt * a_im)
    nr, ni = ab_re - 1.0, ab_im
    den = a_re * a_re + a_im * a_im
    q_re = (nr * a_re + ni * a_im) / den
    q_im = (ni * a_re - nr * a_im) / den
    b_re, b_im = b_re.astype(f32), b_im.astype(f32)
    bb_re = q_re[..., None] * b_re - q_im[..., None] * b_im
    bb_im = q_re[..., None] * b_im + q_im[..., None] * b_re
    ug = u.astype(f32).reshape(bsz, seqlen, N_GROUPS, GROUP)
    bu_re = jnp.einsum('blgc,gpc->blgp', ug, bb_re)
    bu_im = jnp.einsum('blgc,gpc->blgp', ug, bb_im)
    h0r, h0i = h0_re.astype(f32), h0_im.astype(f32)
    bu_re = bu_re.at[:, 0].add(ab_re * h0r - ab_im * h0i)
    bu_im = bu_im.at[:, 0].add(ab_re * h0i + ab_im * h0r)
    el_re = jnp.broadcast_to(ab_re, bu_re.shape)
    el_im = jnp.broadcast_to(ab_im, bu_im.shape)
    _, _, h_re, h_im = lax.associative_scan(_cmul_combine, (el_re, el_im, bu_re, bu_im), axis=1)
    y = (jnp.einsum('blgp,gcp->blgc', h_re, c_re.astype(f32))
         - jnp.einsum('blgp,gcp->blgc', h_im, c_im.astype(f32)))
    y = y.reshape(bsz, seqlen, D_SSM) + d.astype(f32) * u.astype(f32)
    return y.astype(u.dtype), h_re[:, -1].astype(h0_re.dtype), h_im[:, -1].astype(h0_im.dtype)


def causal_dwconv(v, buf, w, b):
    full = jnp.concatenate([buf.astype(v.dtype), v], axis=1)
    out = lax.conv_general_dilated(
        full, w.astype(v.dtype)[:, None, :], window_strides=(1,), padding='VALID',
        dimension_numbers=('NWC', 'WIO', 'NWC'), feature_group_count=v.shape[-1])
    return out + b, full[:, -(CONV_W - 1):]


def hybrid_layer(x, h_re, h_im, cbuf, p):
    (g_mix, w_in, a_re, a_im, log_dt, b_re, b_im, c_re, c_im, d, w_glu,
     conv_w, conv_b, ln_g, ln_b, w_pw, w_out, g_ffn, w_ff_in, w_ff_out) = p
    xn = rmsnorm(x, g_mix)
    z = xn @ w_in
    u = z[..., :D_SSM]
    cv = z[..., D_SSM:D_SSM + D_CONV]
    cg = z[..., D_SSM + D_CONV:D_SSM + 2 * D_CONV]
    gates = jax.nn.sigmoid(z[..., D_SSM + 2 * D_CONV:])
    y, hr, hi = s5_layer(u, h_re, h_im, a_re, a_im, log_dt, b_re, b_im, c_re, c_im, d)
    yg = jax.nn.gelu(y) @ w_glu
    ya = yg[..., :D_MODEL] * jax.nn.sigmoid(yg[..., D_MODEL:])
    c = cv * jax.nn.sigmoid(cg)
    c, new_buf = causal_dwconv(c, cbuf, conv_w, conv_b)
    c = jax.nn.silu(layernorm(c, ln_g, ln_b))
    yb = c @ w_pw
    merged = gates[..., :D_MODEL] * ya + gates[..., D_MODEL:] * yb
    x = x + merged @ w_out
    hn = rmsnorm(x, g_ffn) @ w_ff_in
    x = x + (jax.nn.silu(hn[..., :D_FF]) * hn[..., D_FF:]) @ w_ff_out
    return x, hr, hi, new_buf


def run_trunk(x, s_re, s_im, s_conv, stacked, g_final):
    new_re, new_im, new_conv = [], [], []
    for l in range(DEPTH):
        p = tuple(w[l] for w in stacked)
        x, hr, hi, nb = hybrid_layer(x, s_re[l], s_im[l], s_conv[l], p)
        new_re.append(hr)
        new_im.append(hi)
        new_conv.append(nb)
    return rmsnorm(x, g_final), jnp.stack(new_re), jnp.stack(new_im), jnp.stack(new_conv)


def setup_inputs(seed: int = 0) -> dict:
    key = jax.random.key(seed)
    ks = jax.random.split(key, 32)
    f32 = jnp.float32
    nrm = lambda k, shape, s: jax.random.normal(k, shape, f32) * s
    x_prompt = nrm(ks[0], (BATCH, SEQ, D_MODEL), 1.0)
    x_sample = nrm(ks[1], (DEC_BATCH, DEC_SEQ, D_MODEL), 1.0)
    state_ssm_re = nrm(ks[2], (DEPTH, DEC_BATCH, N_GROUPS, P_STATE), 0.5)
    state_ssm_im = nrm(ks[3], (DEPTH, DEC_BATCH, N_GROUPS, P_STATE), 0.5)
    state_conv = nrm(ks[4], (DEPTH, DEC_BATCH, CONV_W - 1, D_CONV), 0.5)
    n_idx = jnp.arange(P_STATE, dtype=f32)
    ssm_a_re = -0.5 + nrm(ks[5], (DEPTH, N_GROUPS, P_STATE), 0.01)
    ssm_a_im = math.pi * n_idx + nrm(ks[6], (DEPTH, N_GROUPS, P_STATE), 0.01)
    ssm_log_dt = jax.random.uniform(ks[7], (DEPTH, N_GROUPS), f32,
                                    math.log(DT_MIN), math.log(DT_MAX))
    bs = (2.0 * GROUP) ** -0.5
    cs = (2.0 * P_STATE) ** -0.5
    return {
        "x_prompt": x_prompt,
        "x_sample": x_sample,
        "state_ssm_re": state_ssm_re,
        "state_ssm_im": state_ssm_im,
        "state_conv": state_conv,
        "g_mix": 1.0 + nrm(ks[8], (DEPTH, D_MODEL), 0.01),
        "w_in": nrm(ks[9], (DEPTH, D_MODEL, D_IN), D_MODEL ** -0.5),
        "ssm_a_re": ssm_a_re,
        "ssm_a_im": ssm_a_im,
        "ssm_log_dt": ssm_log_dt,
        "ssm_b_re": nrm(ks[10], (DEPTH, N_GROUPS, P_STATE, GROUP), bs),
        "ssm_b_im": nrm(ks[11], (DEPTH, N_GROUPS, P_STATE, GROUP), bs),
        "ssm_c_re": nrm(ks[12], (DEPTH, N_GROUPS, GROUP, P_STATE), cs),
        "ssm_c_im": nrm(ks[13], (DEPTH, N_GROUPS, GROUP, P_STATE), cs),
        "ssm_d": nrm(ks[14], (DEPTH, D_SSM), 1.0),
        "w_glu": nrm(ks[15], (DEPTH, D_SSM, 2 * D_MODEL), D_SSM ** -0.5),
        "conv_w": nrm(ks[16], (DEPTH, CONV_W, D_CONV), CONV_W ** -0.5),
        "conv_b": nrm(ks[17], (DEPTH, D_CONV), 0.02),
        "conv_ln_g": 1.0 + nrm(ks[18], (DEPTH, D_CONV), 0.01),
        "conv_ln_b": nrm(ks[19], (DEPTH, D_CONV), 0.02),
        "w_pw": nrm(ks[20], (DEPTH, D_CONV, D_MODEL), D_CONV ** -0.5),
        "w_out": nrm(ks[21], (DEPTH, D_MODEL, D_MODEL), D_MODEL ** -0.5),
        "g_ffn": 1.0 + nrm(ks[22], (DEPTH, D_MODEL), 0.01),
        "w_ff_in": nrm(ks[23], (DEPTH, D_MODEL, 2 * D_FF), D_MODEL ** -0.5),
        "w_ff_out": nrm(ks[24], (DEPTH, D_FF, D_MODEL), D_FF ** -0.5),
        "g_final": 1.0 + nrm(ks[25], (D_MODEL,), 0.01),
    }


def reference(x_prompt, x_sample, state_ssm_re, state_ssm_im, state_conv,
              g_mix, w_in, ssm_a_re, ssm_a_im, ssm_log_dt, ssm_b_re, ssm_b_im,
              ssm_c_re, ssm_c_im, ssm_d, w_glu, conv_w, conv_b, conv_ln_g, conv_ln_b,
              w_pw, w_out, g_ffn, w_ff_in, w_ff_out, g_final):
    stacked = (g_mix, w_in, ssm_a_re, ssm_a_im, ssm_log_dt, ssm_b_re, ssm_b_im,
               ssm_c_re, ssm_c_im, ssm_d, w_glu, conv_w, conv_b, conv_ln_g, conv_ln_b,
               w_pw, w_out, g_ffn, w_ff_in, w_ff_out)
    bp = x_prompt.shape[0]
    zero_re = jnp.zeros((DEPTH, bp, N_GROUPS, P_STATE), state_ssm_re.dtype)
    zero_im = jnp.zeros((DEPTH, bp, N_GROUPS, P_STATE), state_ssm_im.dtype)
    zero_conv = jnp.zeros((DEPTH, bp, CONV_W - 1, D_CONV), x_prompt.dtype)
    y_prompt, re_p, im_p, conv_p = run_trunk(x_prompt, zero_re, zero_im, zero_conv, stacked, g_final)
    y_sample, re_s, im_s, conv_s = run_trunk(x_sample, state_ssm_re, state_ssm_im, state_conv, stacked, g_final)
    return (y_prompt, y_sample, re_p, im_p, conv_p, re_s, im_s, conv_s)
```

```python
import math
import os
from contextlib import ExitStack
import numpy as np
import concourse.bass as bass
import concourse.mybir as mybir
from concourse.bass_utils import run_bass_kernel_spmd

F32 = mybir.dt.float32
BF16 = mybir.dt.bfloat16
ALU = mybir.AluOpType
AF = mybir.ActivationFunctionType

ENGS = ("pe", "act", "dve", "pool", "sp")
DMA_Q = ("sp", "act", "pool")
NDMA_SEM = 8

DEPTH = 4
EPS = 1e-6
TWO_PI = 2.0 * math.pi
C1 = 6.28125
C2 = TWO_PI - 6.28125
MAGIC = 12582912.0


class Res:
    __slots__ = ("name", "lw", "rd")

    def __init__(self, name):
        self.name = name
        self.lw = None
        self.rd = {}


class Prog:
    def __init__(self, nc):
        self.nc = nc
        self.ins = {e: [] for e in ENGS}
        self.ndma = {q: 0 for q in DMA_Q}
        self.dmas = []
        self.epoch = 0
        self.all_out_dmas = []
        self._dma_id_of = {}

    def _deps(self, reads, writes):
        deps = set()
        for r in reads:
            if r.lw is not None:
                deps.add(r.lw)
        for w in writes:
            if w.lw is not None:
                deps.add(w.lw)
            for k, v in w.rd.items():
                if isinstance(k, tuple):
                    deps.add(k)
                else:
                    deps.add((k, v))
        return deps

    def op(self, eng, fn, reads=(), writes=()):
        idx = len(self.ins[eng])
        deps = self._deps(reads, writes)
        self.ins[eng].append(dict(fn=fn, deps=deps, dma=None, epoch=self.epoch))
        for r in reads:
            r.rd[eng] = idx
        for w in writes:
            w.lw = (eng, idx)
            w.rd = {}
        return (eng, idx)

    def dma(self, q, fn, reads=(), writes=(), is_out=False):
        deps = self._deps(reads, writes)
        n = self.ndma[q]
        self.ndma[q] += 1
        did = len(self.dmas)
        self.dmas.append((q, n))
        if n >= NDMA_SEM:
            deps.add(("dma", self._dma_id_of[(q, n - NDMA_SEM)]))
        self._dma_id_of[(q, n)] = did
        self.ins[q].append(dict(fn=fn, deps=deps, dma=did, epoch=self.epoch))
        key = ("dma", did)
        for r in reads:
            r.rd[key] = True
        for w in writes:
            w.lw = key
            w.rd = {}
        if is_out:
            self.all_out_dmas.append(key)
        return key

    def barrier(self):
        deps = set()
        for e in ENGS:
            if self.ins[e]:
                deps.add((e, len(self.ins[e]) - 1))
        for q in DMA_Q:
            for n in range(max(0, self.ndma[q] - NDMA_SEM), self.ndma[q]):
                deps.add(("dma", self._dma_id_of[(q, n)]))
        for e in ENGS:
            self.ins[e].append(dict(fn=None, deps=set(deps), dma=None, epoch=self.epoch))

    def finish(self, eng="sp"):
        self.ins[eng].append(dict(fn=None, deps=set(self.all_out_dmas), dma=None, epoch=self.epoch))

    def emit(self, ctx):
        nc = self.nc
        signal = {e: set() for e in ENGS}
        for e in ENGS:
            for i, ins in enumerate(self.ins[e]):
                for d in ins["deps"]:
                    if d[0] != "dma":
                        if d[0] == "pe" and e == "pe":
                            continue
                        if d[0] == e and d[1] >= i:
                            continue
                        signal[d[0]].add(d[1])
        for e in ENGS:
            fixed = set()
            for i in signal[e]:
                j = i
                while j >= 0 and (self.ins[e][j]["fn"] is None or self.ins[e][j]["dma"] is not None):
                    j -= 1
                fixed.add((i, j))
            signal[e] = fixed
        rank = {}
        sigidx = {e: {} for e in ENGS}
        for e in ENGS:
            cnt = {}
            seen = {}
            for i, j in sorted(signal[e], key=lambda t: (t[1], t[0])):
                if j < 0:
                    rank[(e, i)] = None
                    continue
                ep = self.ins[e][j]["epoch"]
                if j not in seen:
                    cnt[ep] = cnt.get(ep, 0) + 1
                    seen[j] = (ep, cnt[ep])
                    sigidx[e][j] = ep
                rank[(e, i)] = seen[j]
        sems = {}
        for e in ENGS:
            for ep in sorted(set(sigidx[e].values())):
                sems[(e, ep)] = ctx.enter_context(nc.semaphore(f"s_{e}_{ep}"))
        dsem = {}
        for q in DMA_Q:
            for k in range(min(NDMA_SEM, self.ndma[q])):
                dsem[(q, k)] = ctx.enter_context(nc.semaphore(f"d_{q}_{k}"))
        prog = self

        def run_engine(e, eng):
            waited = {}
            dwaited = set()
            for i, ins in enumerate(prog.ins[e]):
                need = {}
                dneed = []
                for d in ins["deps"]:
                    if d[0] == "dma":
                        if d[1] not in dwaited:
                            dneed.append(d[1])
                    else:
                        te, ti = d
                        if te == "pe" and e == "pe":
                            continue
                        if te == e and ti >= i:
                            continue
                        if waited.get(te, -1) >= ti:
                            continue
                        need[te] = max(need.get(te, -1), ti)
                for te, ti in need.items():
                    rk = rank[(te, ti)]
                    if rk is not None:
                        eng.wait_ge(sems[(te, rk[0])], rk[1])
                    waited[te] = ti
                for did in sorted(dneed):
                    q, n = prog.dmas[did]
                    eng.wait_ge(dsem[(q, n % NDMA_SEM)], 16 * (n // NDMA_SEM + 1))
                    dwaited.add(did)
                if ins["fn"] is None:
                    continue
                r = ins["fn"](eng)
                if ins["dma"] is not None:
                    q, n = prog.dmas[ins["dma"]]
                    r.then_inc(dsem[(q, n % NDMA_SEM)], 16)
                elif i in sigidx[e]:
                    r.then_inc(sems[(e, sigidx[e][i])], 1)

        with nc.Block() as block:
            @block.tensor
            def _(eng):
                run_engine("pe", eng)

            @block.scalar
            def _(eng):
                run_engine("act", eng)

            @block.vector
            def _(eng):
                run_engine("dve", eng)

            @block.gpsimd
            def _(eng):
                run_engine("pool", eng)

            @block.sync
            def _(eng):
                run_engine("sp", eng)


def build(n_layers=DEPTH, n_halves=2):
    nc = bass.Bass("TRN2", target_bir_lowering=False)
    D = {}

    def inp(name, shape):
        D[name] = nc.dram_tensor(name, shape, F32, kind="ExternalInput").ap()

    def outp(name, shape):
        D[name] = nc.dram_tensor(name, shape, F32, kind="ExternalOutput").ap()

    inp("xp", [2048, 1024]); inp("xs", [16, 1024])
    inp("sre", [4, 16, 2048]); inp("sim", [4, 16, 2048]); inp("scv", [4, 480, 512])
    inp("g_mix", [4, 1024]); inp("w_in", [4, 1024, 3584])
    inp("a_re", [4, 2048]); inp("a_im", [4, 2048]); inp("log_dt", [4, 32])
    inp("b_re", [4, 2048, 16]); inp("b_im", [4, 2048, 16])
    inp("c_re", [4, 512, 64]); inp("c_im", [4, 512, 64]); inp("ssm_d", [4, 512])
    inp("w_glu", [4, 512, 2048]); inp("conv_w", [4, 31, 512]); inp("conv_b", [4, 512])
    inp("ln_g", [4, 512]); inp("ln_b", [4, 512]); inp("w_pw", [4, 512, 1024])
    inp("w_out", [4, 1024, 1024]); inp("g_ffn", [4, 1024]); inp("w_ff_in", [4, 1024, 5632])
    inp("w_ff_out", [4, 2816, 1024]); inp("g_final", [1, 1024])
    inp("k_ident", [128, 128]); inp("k_tri", [128, 128]); inp("k_svec", [128, 128])
    inp("k_pidx", [128, 1]); inp("k_par", [128, 2])
    outp("yp", [2048, 1024]); outp("ys", [16, 1024])
    outp("rep", [4, 16, 128]); outp("imp", [4, 16, 128]); outp("cvp", [4, 30, 512])
    outp("res", [4, 16, 2048]); outp("ims", [4, 16, 2048]); outp("cvs", [4, 16, 30, 512])
    scrB = nc.dram_tensor("scrB", [4, 128, 4096], BF16).ap()
    scrC = nc.dram_tensor("scrC", [4, 128, 4096], BF16).ap()
    scrR = nc.dram_tensor("scrR", [4, 2, 128, 2048], F32).ap()
    scrP = nc.dram_tensor("scrP", [4, 2, 128, 2048], F32).ap()

    with ExitStack() as ctx:
        def sb(name, shape, dt):
            return ctx.enter_context(nc.sbuf_tensor(name, shape, dt))

        P = Prog(nc)
        NTM = 1040
        x = sb("x", [128, 8, NTM], F32)
        xn = sb("xn", [128, 8, NTM], BF16)
        u = sb("u", [128, 4, NTM], BF16)
        cbuf = sb("cbuf", [128, 4, 1054], BF16)
        ya = sb("ya", [128, 8, NTM], BF16)
        arena = sb("arena", [128, 16448], F32)
        wsl = [sb(f"wsl{i}", [128, 4096], BF16) for i in range(3)]
        tmpf = [sb(f"tmpf{i}", [128, 1024], F32) for i in range(4)]
        sqb = sb("sqb", [128, 8, 512], BF16)
        ps = ctx.enter_context(nc.psum_tensor("ps", [128, 8, 512], F32))
        ident_f = sb("ident_f", [128, 128], F32)
        ident_b = sb("ident_b", [128, 128], BF16)
        tri_b = sb("tri_b", [128, 128], BF16)
        ones1024 = sb("ones1024", [128, 128], BF16)
        ones512 = sb("ones512", [128, 128], BF16)
        svec = sb("svec", [128, 128], F32)
        pidx = sb("pidx", [128, 1], F32)
        par = sb("par", [128, 2], F32)
        gmix = sb("gmix", [128, 8, 4], F32)
        gffn = sb("gffn", [128, 8, 4], F32)
        gfin = sb("gfin", [128, 8, 1], F32)
        dpar = sb("dpar", [128, 4, 4], F32)
        convb = sb("convb", [128, 4, 4], F32)
        lng = sb("lng", [128, 4, 4], F32)
        lnb = sb("lnb", [128, 4, 4], F32)
        convw = sb("convw", [128, 4, 4, 31], F32)
        are = sb("are", [128, 16, 4], F32)
        aim = sb("aim", [128, 16, 4], F32)
        ldt = sb("ldt", [128, 4, 16], F32)
        sm = {n: sb("sm_" + n, [128, 4, 16], F32) for n in
              ["dt", "dr", "th", "mag", "cs", "sn", "lre", "lim", "t1", "t2", "t3", "t4", "qre", "qim", "den"]}
        Hst = sb("Hst", [128, 4, 2, 16], F32)
        diagd = sb("diagd", [128, 4, 128], BF16)
        carry = sb("carry", [128, 2, 16], F32)
        hsm = [sb(f"hsm{i}", [128, 2, 16], F32) for i in range(5)]
        ctail = sb("ctail", [128, 4, 4, 30], BF16)
        cnew32 = sb("cnew32", [128, 4, 16], F32)
        ctail32 = sb("ctail32", [128, 4, 30], F32)
        hsp = ya[:, 0, 0:1024].bitcast(F32).rearrange("p (r j s) -> p r j s", r=2, j=16)
        hsn = ya[:, 1, 0:1024].bitcast(F32).rearrange("p (r j s) -> p r j s", r=2, j=16)
        hsb = ya[:, 2, 0:512].rearrange("p (r j s) -> p r j s", r=2, j=16)
        bus = sqb[:16, :, :].rearrange("p a b -> p (a b)").rearrange("p (r c) -> p r c", r=2)

        R = {}

        def res(name):
            if name not in R:
                R[name] = Res(name)
            return R[name]

        RB = [res(f"bank{i}") for i in range(8)]
        RW = [res(f"wsl{i}") for i in range(3)]
        RT = [res(f"tmpf{i}") for i in range(4)]
        RA = res("arena_generic")
        st = dict(bank=0, wsl=0, tmp=0)

        def bank():
            b = st["bank"]; st["bank"] = (b + 1) % 8
            return b

        def tmp():
            t = st["tmp"]; st["tmp"] = (t + 1) % 4
            return t

        def MM(out, lhsT, rhs, start, stop, reads, writes):
            P.op("pe", lambda e, o=out, l=lhsT, r=rhs, s=start, t=stop:
                 e.matmul(o, lhsT=l, rhs=r, start=s, stop=t), reads, writes)

        def TR(out, in_, idn, reads, writes):
            P.op("pe", lambda e, o=out, i=in_, d=idn: e.transpose(o, i, d), reads, writes)

        def ACT(out, in_, func, reads, writes, **kw):
            P.op("act", lambda e, o=out, i=in_, f=func, k=kw: e.activation(out=o, in_=i, func=f, **k), reads, writes)

        def TT(eng, out, in0, in1, op, reads, writes):
            P.op(eng, lambda e, o=out, a=in0, b=in1, p=op: e.tensor_tensor(out=o, in0=a, in1=b, op=p), reads, writes)

        def TS(eng, out, in0, s1, s2, op0, op1, reads, writes):
            if s2 is None:
                P.op(eng, lambda e, o=out, a=in0, x1=s1, p0=op0:
                     e.tensor_scalar(out=o, in0=a, scalar1=x1, scalar2=None, op0=p0), reads, writes)
            else:
                P.op(eng, lambda e, o=out, a=in0, x1=s1, x2=s2, p0=op0, p1=op1:
                     e.tensor_scalar(out=o, in0=a, scalar1=x1, scalar2=x2, op0=p0, op1=p1), reads, writes)

        def STT(eng, out, in0, scalar, in1, op0, op1, reads, writes):
            P.op(eng, lambda e, o=out, a=in0, s=scalar, b=in1, p0=op0, p1=op1:
                 e.scalar_tensor_tensor(out=o, in0=a, scalar=s, in1=b, op0=p0, op1=p1), reads, writes)

        def CP(eng, out, in_, reads, writes):
            if eng == "act":
                P.op("act", lambda e, o=out, i=in_: e.copy(out=o, in_=i), reads, writes)
            else:
                P.op(eng, lambda e, o=out, i=in_: e.tensor_copy(out=o, in_=i), reads, writes)

        def MEMSET(eng, ap, val, writes):
            P.op(eng, lambda e, a=ap, v=val: e.memset(a, v), (), writes)

        def DMA(q, out, in_, reads, writes, is_out=False, slow=False):
            def fn(e, o=out, i=in_, s=slow):
                if s:
                    with nc.allow_non_contiguous_dma(reason="small strided parameter/state transfer"):
                        return e.dma_start(out=o, in_=i)
                return e.dma_start(out=o, in_=i)
            P.dma(q, fn, reads, writes, is_out=is_out)

        def RECIP(out, in_, reads, writes):
            P.op("dve", lambda e, o=out, i=in_: e.reciprocal(out=o, in_=i), reads, writes)

        def carve(off, shape, dt):
            n = int(np.prod(shape))
            nb = n * (4 if dt == F32 else 2)
            assert off % 4 == 0 and nb % 4 == 0 and off + nb <= 65792, (off, shape)
            a = arena[:, off // 4:(off + nb) // 4]
            if dt != F32:
                a = a.bitcast(dt)
            if len(shape) == 2:
                a = a.rearrange("p (a b) -> p a b", b=shape[1])
            elif len(shape) == 3:
                a = a.rearrange("p (a b c) -> p a b c", b=shape[1], c=shape[2])
            return a

        rc = res("consts")
        DMA("sp", ident_f[:], D["k_ident"], (), [rc])
        DMA("sp", svec[:], D["k_svec"], (), [rc])
        DMA("sp", pidx[:], D["k_pidx"], (), [rc])
        DMA("sp", par[:], D["k_par"], (), [rc])
        DMA("pool", ident_b[:], D["k_ident"], (), [rc])
        DMA("pool", tri_b[:], D["k_tri"], (), [rc])
        MEMSET("dve", ones1024[:], 1.0 / 1024.0, [rc])
        MEMSET("dve", ones512[:], 1.0 / 512.0, [rc])
        MEMSET("dve", Hst[:], 0.0, [res("Hst")])
        MEMSET("dve", ctail[:], 0.0, [res("ctail")])

        def load_T(src, rows, ncols, dst_fn, wres, eng_alt=[0]):
            t = tmp()
            DMA("sp", tmpf[t][:rows, :ncols], src, (), [RT[t]])
            for b in range(ncols // 128):
                bk = bank()
                TR(ps[:, bk, :rows], tmpf[t][:rows, b * 128:(b + 1) * 128], ident_f[:rows, :rows],
                   [RT[t], rc], [RB[bk]])
                eng = "act" if (eng_alt[0] % 2 == 0) else "dve"
                eng_alt[0] += 1
                CP(eng, dst_fn(b), ps[:, bk, :rows], [RB[bk]], wres)

        KSTOP = int(os.environ.get("KSTOP", "99"))
        if KSTOP == 1:
            P.finish(); P.emit(ctx); return nc
        rp = res("params")
        load_T(D["g_mix"], 4, 1024, lambda b: gmix[:, b, :], [rp])
        load_T(D["g_ffn"], 4, 1024, lambda b: gffn[:, b, :], [rp])
        load_T(D["g_final"], 1, 1024, lambda b: gfin[:, b, :], [rp])
        load_T(D["ssm_d"], 4, 512, lambda b: dpar[:, b, :], [rp])
        load_T(D["conv_b"], 4, 512, lambda b: convb[:, b, :], [rp])
        load_T(D["ln_g"], 4, 512, lambda b: lng[:, b, :], [rp])
        load_T(D["ln_b"], 4, 512, lambda b: lnb[:, b, :], [rp])
        for l in range(4):
            load_T(D["conv_w"][l], 31, 512, lambda b, l=l: convw[:, l, b, :], [rp])
        for h2 in range(2):
            load_T(D["a_re"][:, h2 * 1024:(h2 + 1) * 1024], 4, 1024, lambda b, h2=h2: are[:, h2 * 8 + b, :], [rp])
            load_T(D["a_im"][:, h2 * 1024:(h2 + 1) * 1024], 4, 1024, lambda b, h2=h2: aim[:, h2 * 8 + b, :], [rp])
        if KSTOP == 2:
            P.finish(); P.emit(ctx); return nc
        t_ld = tmp()
        DMA("sp", tmpf[t_ld][:, 0:128], D["log_dt"].rearrange("l g -> (l g)").partition_broadcast(128), (), [RT[t_ld]])
        for g2 in range(2):
            srcv = tmpf[t_ld][64 * g2:64 * g2 + 64, 0:128].rearrange("p (l j two) -> p l j two", l=4, two=2)[:, :, :, g2]
            CP("dve", ldt[64 * g2:64 * g2 + 64, :, :], srcv, [RT[t_ld]], [rp])

        def load_x(half, rx):
            blocks = [(D["xp"][half * 1024 + b * 128: half * 1024 + (b + 1) * 128, :], 128, b * 128) for b in range(8)]
            if half == 0:
                blocks.append((D["xs"], 16, 1024))
            for bi, (src, rows, c0) in enumerate(blocks):
                t = tmp()
                DMA("sp", tmpf[t][:rows, :], src, (), [RT[t]])
                ti = c0 // 512
                for q in range(2):
                    bk = bank()
                    for k4 in range(4):
                        kt = 4 * q + k4
                        TR(ps[:, bk, k4 * rows:(k4 + 1) * rows], tmpf[t][:rows, kt * 128:(kt + 1) * 128],
                           ident_f[:rows, :rows], [RT[t], rc], [RB[bk]])
                    CP("act" if q == 0 else "dve", x[:, 4 * q:4 * q + 4, c0:c0 + rows],
                       ps[:, bk, :4 * rows].rearrange("p (k t) -> p k t", t=rows), [RB[bk]], [rx[ti]])


        rx0 = [res(f"x{ti}") for ti in range(3)]
        if n_halves > 0:
            load_x(0, rx0)
        if KSTOP == 3:
            P.finish(); P.emit(ctx); return nc
        rs = res("s5small")

        def sincos(eng, A, n, cs_out, sn_out, t_a, t_b, rA, rO):
            TS(eng, t_a, A, 1.0 / TWO_PI, MAGIC, ALU.mult, ALU.add, rA, rO)
            TS(eng, t_a, t_a, -MAGIC, None, ALU.add, None, rO, rO)
            STT(eng, t_b, t_a, -C1, A, ALU.mult, ALU.add, rA + rO, rO)
            STT(eng, t_b, t_a, -C2, t_b, ALU.mult, ALU.add, rO, rO)
            TS(eng, t_b, t_b, -math.pi, math.pi, ALU.max, ALU.min, rO, rO)
            ACT(sn_out, t_b, AF.Sin, rO, rO)
            ACT(t_a, t_b, AF.Sin, rO, rO, scale=0.5)
            TT(eng, t_a, t_a, t_a, ALU.mult, rO, rO)
            TS(eng, cs_out, t_a, -2.0, 1.0, ALU.mult, ALU.add, rO, rO)

        halfpi = sb("halfpi", [128, 1], F32)
        epsc = sb("epsc", [128, 1], F32)
        MEMSET("dve", halfpi[:], math.pi / 2.0, [rc])
        MEMSET("dve", epsc[:], EPS, [rc])

        S = {k: v[:] for k, v in sm.items()}
        are_v = are[:].rearrange("p j l -> p l j")
        aim_v = aim[:].rearrange("p j l -> p l j")
        ACT(S["dt"], ldt[:], AF.Exp, [rp], [rs])
        TT("dve", S["dr"], S["dt"], are_v, ALU.mult, [rs, rp], [rs])
        TT("dve", S["th"], S["dt"], aim_v, ALU.mult, [rs, rp], [rs])
        ACT(S["mag"], S["dr"], AF.Exp, [rs], [rs])
        sincos("dve", S["th"], 64, S["cs"], S["sn"], S["t1"], S["t2"], [rs], [rs])
        TT("dve", S["lre"], S["mag"], S["cs"], ALU.mult, [rs], [rs])
        TT("dve", S["lim"], S["mag"], S["sn"], ALU.mult, [rs], [rs])
        TS("dve", S["t1"], S["lre"], -1.0, None, ALU.add, None, [rs], [rs])
        TT("dve", S["t2"], are_v, are_v, ALU.mult, [rs, rp], [rs])
        TT("dve", S["t3"], aim_v, aim_v, ALU.mult, [rs, rp], [rs])
        TT("dve", S["den"], S["t2"], S["t3"], ALU.add, [rs], [rs])
        RECIP(S["den"], S["den"], [rs], [rs])
        TT("dve", S["t2"], S["t1"], are_v, ALU.mult, [rs, rp], [rs])
        TT("dve", S["t3"], S["lim"], aim_v, ALU.mult, [rs, rp], [rs])
        TT("dve", S["t2"], S["t2"], S["t3"], ALU.add, [rs], [rs])
        TT("dve", S["qre"], S["t2"], S["den"], ALU.mult, [rs], [rs])
        TT("dve", S["t2"], S["lim"], are_v, ALU.mult, [rs, rp], [rs])
        TT("dve", S["t3"], S["t1"], aim_v, ALU.mult, [rs, rp], [rs])
        TT("dve", S["t2"], S["t2"], S["t3"], ALU.subtract, [rs], [rs])
        TT("dve", S["qim"], S["t2"], S["den"], ALU.mult, [rs], [rs])

        A_ang = carve(0, [2048], F32)
        A_ta = carve(8192, [2048], F32)
        A_tb = carve(16384, [2048], F32)
        A_cs = carve(24576, [2048], F32)
        A_sn = carve(32768, [2048], F32)
        A_mg = carve(40960, [2048], F32)
        BpadRe = carve(49152, [16, 128], F32)
        BpadIm = carve(57344, [16, 128], F32)
        rAng, rTa, rTb, rCs, rSn, rMg, rBpR, rBpI = [res("prep_" + n_) for n_ in ("ang", "ta", "tb", "cs", "sn", "mg", "bpr", "bpi")]
        rSC = [rTa, rTb, rCs, rSn]
        MEMSET("pool", BpadRe, 0.0, [rBpR])
        MEMSET("pool", BpadIm, 0.0, [rBpI])
        n_prep = n_layers
        for l in range(n_prep):
            rl = [RA]
            ang3 = A_ang.rearrange("p (j s) -> p j s", s=128)
            sv_bc = svec[:].unsqueeze(1).to_broadcast([128, 16, 128])
            th_bc = sm["th"][:, l, :].unsqueeze(2).to_broadcast([128, 16, 128])
            dr_bc = sm["dr"][:, l, :].unsqueeze(2).to_broadcast([128, 16, 128])
            TT("dve", ang3, sv_bc, th_bc, ALU.mult, [rc, rs], [rAng])
            sincos("dve", A_ang, 2048, A_cs, A_sn, A_ta, A_tb, [rAng], rSC)
            TT("dve", A_ta.rearrange("p (j s) -> p j s", s=128), sv_bc, dr_bc, ALU.mult, [rc, rs], [rTa])
            ACT(A_mg, A_ta, AF.Exp, [rTa], [rMg])
            TT("dve", A_cs, A_cs, A_mg, ALU.mult, [rCs, rMg], [rCs])
            TT("dve", A_sn, A_sn, A_mg, ALU.mult, [rSn, rMg], [rSn])
            DMA("sp", scrP[l, 0], A_cs, [rCs], [res("scr")])
            DMA("sp", scrP[l, 1], A_sn, [rSn], [res("scr")])
            ACT(A_ang, A_mg, AF.Copy, [rMg], [rAng])
            P.op("dve", lambda e, o=A_mg, i=A_ang: e.reciprocal(out=o, in_=i), [rAng], [rMg])
            TT("dve", A_mg, A_mg, A_mg, ALU.mult, [rMg], [rMg])
            TT("dve", A_ta, A_cs, A_mg, ALU.mult, [rCs, rMg], [rTa])
            STT("dve", A_tb, A_sn, -1.0, A_mg, ALU.mult, ALU.mult, [rSn, rMg], [rTb])
            for (Qsrc, rQ, ri_) in ((A_ta, rTa, 0), (A_tb, rTb, 1)):
                for q4 in range(4):
                    bk = bank()
                    for jm in range(4):
                        j = 4 * q4 + jm
                        TR(ps[:, bk, jm * 128:(jm + 1) * 128], Qsrc[:, j * 128:(j + 1) * 128], ident_f[:], [rQ, rc], [RB[bk]])
                    CP("act" if q4 % 2 == 0 else "dve", A_ang[:, q4 * 512:(q4 + 1) * 512], ps[:, bk, :], [RB[bk]], [rAng])
                DMA("sp", scrR[l, ri_], A_ang, [rAng], [res("scr")])
            Braw_re = A_ta.rearrange("p (a b) -> p a b", b=128)[:, :, 0:16]
            Braw_im = A_tb.rearrange("p (a b) -> p a b", b=128)[:, :, 0:16]
            Bb_re = A_ta.rearrange("p (a b) -> p a b", b=128)[:, :, 16:32]
            Bb_im = A_tb.rearrange("p (a b) -> p a b", b=128)[:, :, 16:32]
            Bt1 = A_ta.rearrange("p (a b) -> p a b", b=128)[:, :, 32:48]
            Bt2 = A_tb.rearrange("p (a b) -> p a b", b=128)[:, :, 32:48]
            DMA("sp", Braw_re, D["b_re"][l].rearrange("(j i) c -> i j c", i=128), (), [rTa])
            DMA("sp", Braw_im, D["b_im"][l].rearrange("(j i) c -> i j c", i=128), (), [rTb])
            qre_bc = sm["qre"][:, l, :].unsqueeze(2).to_broadcast([128, 16, 16])
            qim_bc = sm["qim"][:, l, :].unsqueeze(2).to_broadcast([128, 16, 16])
            TT("dve", Bt1, Braw_re, qre_bc, ALU.mult, [rTa, rTb] + [rs], [rTa, rTb])
            TT("dve", Bt2, Braw_im, qim_bc, ALU.mult, [rTa, rTb] + [rs], [rTa, rTb])
            TT("dve", Bb_re, Bt1, Bt2, ALU.subtract, [rTa, rTb], [rTa, rTb])
            TT("dve", Bt1, Braw_im, qre_bc, ALU.mult, [rTa, rTb] + [rs], [rTa, rTb])
            TT("dve", Bt2, Braw_re, qim_bc, ALU.mult, [rTa, rTb] + [rs], [rTa, rTb])
            TT("dve", Bb_im, Bt1, Bt2, ALU.add, [rTa, rTb], [rTa, rTb])
            for (Bb, Bpad, rBp) in ((Bb_re, BpadRe, rBpR), (Bb_im, BpadIm, rBpI)):
                for g2 in range(2):
                    for jm in range(4):
                        c0 = 16 * (2 * jm + g2)
                        CP("dve", Bpad[64 * g2:64 * g2 + 64, jm::4, c0:c0 + 16],
                           Bb[64 * g2:64 * g2 + 64, jm::4, :], [rTa, rTb], [rBp])
            Bm_sb = A_cs.bitcast(BF16).rearrange("p (k r c) -> p k r c", k=4, r=2)
            for ri, (Bpad, rBp) in enumerate(((BpadRe, rBpR), (BpadIm, rBpI))):
                for kt in range(4):
                    bk = bank()
                    for jm in range(4):
                        TR(ps[:, bk, jm * 128:(jm + 1) * 128], Bpad[:, 4 * kt + jm, :], ident_f[:], [rBp, rc], [RB[bk]])
                    CP("act", Bm_sb[:, kt, ri, :], ps[:, bk, :], [RB[bk]], [rCs])
            DMA("sp", scrB[l], A_cs.bitcast(BF16), [rCs], [res("scr")])
            Craw = A_sn.rearrange("p (k q) -> p k q", q=512)
            Cm_sb = A_mg.bitcast(BF16).rearrange("p (r j c) -> p r j c", r=2, j=16)
            MEMSET("pool", A_mg, 0.0, [rMg])
            for ri, nm in enumerate(("c_re", "c_im")):
                DMA("sp", Craw[:, :, 0:64], D[nm][l].rearrange("(k r) q -> r k q", r=128), (), [rSn])
                for g2 in range(2):
                    TS("dve", Craw[:, :, 128 + 64 * g2:128 + 64 * g2 + 64], Craw[:, :, 0:64], par[:, g2:g2 + 1], None,
                       ALU.mult, None, [rSn, rc], [rSn])
                bk = bank()
                for kt in range(4):
                    TR(ps[:, bk, kt * 128:(kt + 1) * 128], Craw[:, kt, 128:256], ident_f[:], [rSn, rc], [RB[bk]])
                for kt in range(4):
                    for jm in range(4):
                        src = ps[:, bk, kt * 128 + 32 * jm:kt * 128 + 32 * jm + 32]
                        dst = Cm_sb[:, ri, 4 * kt + jm, 32 * jm:32 * jm + 32]
                        if ri == 0:
                            CP("act", dst, src, [RB[bk]], [rMg])
                        else:
                            P.op("act", lambda e, o=dst, i=src: e.mul(out=o, in_=i, mul=-1.0), [RB[bk]], [rMg])
            DMA("sp", scrC[l], A_mg.bitcast(BF16), [rMg], [res("scr")])
        P.barrier()

        def load_w(dst_slot, src_ap, kt_n, ncols, col_off=0, slot_cols=None):
            sc = slot_cols if slot_cols is not None else ncols
            dst = wsl[dst_slot][:, 0:kt_n * sc].rearrange("p (k c) -> p k c", c=sc)[:, :, col_off:col_off + ncols]
            DMA("pool", dst, src_ap.rearrange("(k p) c -> p k c", p=128), (), [RW[dst_slot]])

        def next_slot():
            s = st["wsl"]; st["wsl"] = (s + 1) % 3
            return s

        def wview(slot, kt_n, sc):
            return wsl[slot][:, 0:kt_n * sc].rearrange("p (k c) -> p k c", c=sc)

        def rmsnorm_stats(TTL, rx, gcol):
            out = []
            for ti, (t0, n) in enumerate(TTL):
                ACT(sqb[:, :, :n], x[:, :, t0:t0 + n], AF.Square, [rx[ti]], [res("sqb")])
                bk = bank()
                for kt in range(8):
                    MM(ps[:, bk, :n], ones1024[:], sqb[:, kt, :n], kt == 0, kt == 7, [res("sqb"), rc], [RB[bk]])
                t = tmp()
                ACT(tmpf[t][:, :n], ps[:, bk, :n], AF.Sqrt, [RB[bk], rc], [RT[t]], bias=epsc[:, 0:1])
                RECIP(tmpf[t][:, :n], tmpf[t][:, :n], [RT[t]], [RT[t]])
                out.append(t)
            return out

        def rms_apply(TTL, rx, rxn, gsel, ti, t):
            t0, n = TTL[ti]
            for kt in range(8):
                STT("dve", xn[:, kt, t0:t0 + n], x[:, kt, t0:t0 + n], gsel(kt), tmpf[t][:, :n],
                    ALU.mult, ALU.mult, [rx[ti], RT[t], rp], [rxn[ti]])

        def rmsnorm_to_xn(TTL, rx, rxn, gsel, tis=None):
            for ti, (t0, n) in enumerate(TTL):
                if tis is not None and ti not in tis:
                    continue
                (t,) = rmsnorm_stats([TTL[ti]], [rx[ti]], gsel)
                rms_apply(TTL, rx, rxn, gsel, ti, t)

        WS = dict(q=[])

        def ws_add(fn):
            WS["q"].append([fn, None])
            return len(WS["q"]) - 1

        def ws_use(i, la=2):
            for k in range(i, min(i + 1 + la, len(WS["q"]))):
                if WS["q"][k][1] is None:
                    sl = next_slot()
                    WS["q"][k][0](sl)
                    WS["q"][k][1] = sl
            return WS["q"][i][1]

        def decl_dense(wsrc, kt_n, col0, ncols, grp):
            ids = []
            for g0 in range(0, ncols, grp):
                ids.append(ws_add(lambda sl, a=wsrc[:, col0 + g0:col0 + g0 + grp], k=kt_n, g=grp: load_w(sl, a, k, g)))
            return ids

        def dense(gids, kt_n, grp, src_fn, rsrc, TTL, consumer, tis=None, mid_hook=None, la=2, hooks=None):
            if tis is None:
                tis = list(range(len(TTL)))
            for gi, gid in enumerate(gids):
                s = ws_use(gid, la)
                wv = wview(s, kt_n, grp)
                for nt in range(grp // 128):
                    for ti in tis:
                        t0, n = TTL[ti]
                        bk = bank()
                        for kt in range(kt_n):
                            MM(ps[:, bk, :n], wv[:, kt, nt * 128:(nt + 1) * 128], src_fn(kt, t0, n),
                               kt == 0, kt == kt_n - 1, [RW[s]] + rsrc[ti], [RB[bk]])
                        consumer(gi * (grp // 128) + nt, ti, t0, n, bk)
                    if hooks is not None and (gi, nt) in hooks:
                        hooks[(gi, nt)]()
                if gi == 0 and mid_hook is not None:
                    mid_hook()

        def ffin_load(l, g):
            def fn(sl):
                load_w(sl, D["w_ff_in"][l][:, 256 * g:256 * g + 256], 8, 256, col_off=0, slot_cols=512)
                load_w(sl, D["w_ff_in"][l][:, 2816 + 256 * g:2816 + 256 * g + 256], 8, 256, col_off=256, slot_cols=512)
            return fn

        WD = {}
        for half_ in range(n_halves):
            for l_ in range(n_layers):
                w = {}
                w["u"] = decl_dense(D["w_in"][l_], 8, 0, 512, 512)
                w["cv"] = decl_dense(D["w_in"][l_], 8, 512, 512, 512)
                w["cg"] = decl_dense(D["w_in"][l_], 8, 1024, 512, 512)
                w["glu1"] = decl_dense(D["w_glu"][l_], 4, 0, 1024, 1024)
                w["glu2"] = decl_dense(D["w_glu"][l_], 4, 1024, 1024, 1024)
                w["gate"] = [None] * 4
                w["gate"][0] = decl_dense(D["w_in"][l_], 8, 1536, 512, 512)
                w["gate"][1] = decl_dense(D["w_in"][l_], 8, 2048, 512, 512)
                w["pw"] = decl_dense(D["w_pw"][l_], 4, 0, 1024, 1024)
                w["gate"][2] = decl_dense(D["w_in"][l_], 8, 2560, 512, 512)
                w["gate"][3] = decl_dense(D["w_in"][l_], 8, 3072, 512, 512)
                w["outA"] = decl_dense(D["w_out"][l_], 8, 0, 1024, 512)
                w["outB"] = decl_dense(D["w_out"][l_], 8, 0, 1024, 512)
                w["ffin"] = [ws_add(ffin_load(l_, g)) for g in range(11)]
                w["ffoutA"] = decl_dense(D["w_ff_out"][l_], 22, 0, 1024, 128)
                w["ffoutB"] = decl_dense(D["w_ff_out"][l_], 22, 0, 1024, 128)
                WD[(half_, l_)] = w

        for half in range(n_halves):
            TTL = [(0, 512), (512, 512)] + ([(1024, 16)] if half == 0 else [])
            NTT = len(TTL)
            rx = [res(f"x{ti}") for ti in range(NTT)]
            rxn = [res(f"xn{ti}") for ti in range(NTT)]
            ru = [[res(f"u{c}_{k}") for k in range(4)] for c in range(9)]
            rcb = [res(f"cb{i}") for i in range(3)]
            rya = [res(f"ya{ti}") for ti in range(NTT)]
            rar = [res(f"ar{ti}") for ti in range(NTT)]
            rar2 = [res(f"ar2_{ti}") for ti in range(NTT)]

            def ru_tile(ti, kt=None):
                cks = [8] if ti == 2 else list(range(4 * ti, 4 * ti + 4))
                if kt is None:
                    return [ru[c][k] for c in cks for k in range(4)]
                return [ru[c][kt] for c in cks]

            if half > 0:
                load_x(half, rx)
            if KSTOP == 4:
                P.finish(); P.emit(ctx); return nc
            for l in range(n_layers):
                P.epoch += 1
                W = WD[(half, l)]
                Tb = [carve(4096 * i, [4, 512], BF16) for i in range(2)]
                Mb = [carve(8192 + 4096 * i, [4, 512], BF16) for i in range(2)]
                Rre = carve(16640, [2048], F32); Rim = carve(24832, [2048], F32)
                Pre = carve(33024, [16, 128], F32); Pim = carve(41216, [16, 128], F32)
                Bm = carve(49408, [4, 2, 512], BF16)
                Cm = carve(57600, [2, 16, 128], BF16)
                rTb = [res("Tb0"), res("Tb1")]; rMb = [res("Mb0"), res("Mb1")]
                rtab = res("s5tab")
                tab_w = [rtab, res("diag"), res("lnst"), res("bufT"), res("xo")] + \
                        [res(f"ar{i}") for i in range(3)] + [res(f"ar2_{i}") for i in range(3)]
                DMA("sp", Rre, scrR[l, 0], [res("scr")], tab_w)
                DMA("sp", Rim, scrR[l, 1], [res("scr")], [rtab])
                DMA("sp", Pre, scrP[l, 0].rearrange("p (j s) -> p j s", s=128), [res("scr")], [rtab])
                DMA("sp", Pim, scrP[l, 1].rearrange("p (j s) -> p j s", s=128), [res("scr")], [rtab])
                DMA("sp", Bm, scrB[l].rearrange("p (k r c) -> p k r c", k=4, r=2), [res("scr")], [rtab])
                DMA("sp", Cm, scrC[l].rearrange("p (r j c) -> p r j c", r=2, j=16), [res("scr")], [rtab])
                TT("dve", diagd[:, :, :], ident_b[:].unsqueeze(1).to_broadcast([128, 4, 128]),
                   dpar[:, :, l].unsqueeze(2).to_broadcast([128, 4, 128]), ALU.mult, [rc, rp], [res("diagd")])
                gmix_sel = lambda kt, l=l: gmix[:, kt, l:l + 1]
                if l == 0:
                    rmsnorm_to_xn(TTL, rx, rxn, gmix_sel, tis=[0])
                cv32 = carve(0, [4, NTM], F32)

                def cons_u(ntl, ti, t0, n, bk):
                    CP("act", u[:, ntl, t0:t0 + n], ps[:, bk, :n], [RB[bk]], ru_tile(ti, ntl))

                def cons_cv(ntl, ti, t0, n, bk):
                    CP("act", cv32[:, ntl, t0:t0 + n], ps[:, bk, :n], [RB[bk]], [rar[ti]])

                def cons_cg(ntl, ti, t0, n, bk, l=l, half=half):
                    t = tmp()
                    ACT(tmpf[t][:, :n], ps[:, bk, :n], AF.Sigmoid, [RB[bk]], [RT[t]])
                    TT("dve", cv32[:, ntl, t0:t0 + n], cv32[:, ntl, t0:t0 + n], tmpf[t][:, :n], ALU.mult,
                       [rar[ti], RT[t]], [rar[ti]])
                    if ti < 2:
                        CP("act", cbuf[:, ntl, 30 + t0:30 + t0 + n], cv32[:, ntl, t0:t0 + n], [rar[ti]], [rcb[1 + ti]])
                        if ti == 1:
                            CP("dve", ctail32[:, ntl, :], cv32[:, ntl, 994:1024], [rar[ti]], [res("ctail32")])
                    else:
                        CP("dve", cnew32[:, ntl, :], cv32[:, ntl, t0:t0 + n], [rar[ti]], [res("cnew32")])

                xsrc = lambda kt, t0, n: xn[:, kt, t0:t0 + n]
                rxn_l = [[r] for r in rxn]
                CP("pool", cbuf[:, :, 0:30], ctail[:, l, :, :], [res("ctail")], [rcb[0]])
                tis0 = [0]
                tis1 = list(range(1, NTT))
                dense(W["u"], 8, 512, xsrc, rxn_l, TTL, cons_u, tis=tis0, la=2)
                rmsnorm_to_xn(TTL, rx, rxn, gmix_sel, tis=tis1)
                dense(W["cv"], 8, 512, xsrc, rxn_l, TTL, cons_cv, tis=tis0, la=1)
                dense(W["cg"], 8, 512, xsrc, rxn_l, TTL, cons_cg, tis=tis0, la=0)
                dense(W["u"], 8, 512, xsrc, rxn_l, TTL, cons_u, tis=tis1)
                dense(W["cv"], 8, 512, xsrc, rxn_l, TTL, cons_cv, tis=tis1)
                dense(W["cg"], 8, 512, xsrc, rxn_l, TTL, cons_cg, tis=tis1)
                if half == 0:
                    DMA("sp", D["cvs"][l, :, 0:29, :], D["scv"][l].rearrange("(b k) c -> b k c", k=30)[:, 1:30, :],
                        (), [res("cvs")], is_out=True)
                    t = tmp()
                    bk = bank()
                    for ct in range(4):
                        TR(ps[:16, bk, ct * 128:(ct + 1) * 128], cnew32[:, ct, :], ident_f[:], [res("cnew32"), rc], [RB[bk]])
                    CP("dve", tmpf[t][:16, :512], ps[:16, bk, :], [RB[bk]], [RT[t]])
                    DMA("sp", D["cvs"][l, :, 29, :], tmpf[t][:16, :512], [RT[t]], [res("cvs")], is_out=True)
                if half == n_halves - 1:
                    t = tmp()
                    bk = bank()
                    for ct in range(4):
                        TR(ps[:30, bk, ct * 128:(ct + 1) * 128], ctail32[:, ct, :], ident_f[:], [res("ctail32"), rc], [RB[bk]])
                    CP("dve", tmpf[t][:30, :512], ps[:30, bk, :], [RB[bk]], [RT[t]])
                    DMA("sp", D["cvp"][l], tmpf[t][:30, :512], [RT[t]], [res("cvp")], is_out=True)
                P.barrier()

                rH = res("Hst"); rcar = res("carry"); rgc = res("gcol")
                lre = sm["lre"][:, l, :]; lim = sm["lim"][:, l, :]
                gcol = hsm[1]

                def cmul_small(o_re, o_im, a_re_, a_im_, b_re_, b_im_, reads, writes):
                    t1 = hsm[0][:, 0, :]; t2 = hsm[0][:, 1, :]
                    TT("pool", t1, a_re_, b_re_, ALU.mult, reads, [res("hsm")])
                    TT("pool", t2, a_im_, b_im_, ALU.mult, reads, [res("hsm")])
                    TT("pool", o_re, t1, t2, ALU.subtract, [res("hsm")], writes)
                    TT("pool", t1, a_re_, b_im_, ALU.mult, reads, [res("hsm")])
                    TT("pool", t2, a_im_, b_re_, ALU.mult, reads, [res("hsm")])
                    TT("pool", o_im, t1, t2, ALU.add, [res("hsm")], writes)

                if half == 0:
                    MEMSET("pool", Hst[:, l, :, :], 0.0, [rH])
                v3 = lambda ap: ap.rearrange("p (j s) -> p j s", s=128)
                L128 = hsm[2]
                rL = res("L128")
                cmul_small(L128[:, 0, :], L128[:, 1, :], lre, lim, Pre[:, :, 127], Pim[:, :, 127], [rs, rtab], [rL])
                cmul_small(carry[:, 0, :], carry[:, 1, :], lre, lim, Hst[:, l, 0, :], Hst[:, l, 1, :], [rs, rH], [rcar])
                items = [(ck, kt) for ck in range(8) for kt in range(4)]
                Mviews = [[v3(Mb[p_][:, i, :]) for i in range(4)] for p_ in range(2)]

                def stage_A(it):
                    ck, kt = items[it]; pp = it % 2; c0 = ck * 128
                    T = Tb[pp]
                    ruk = [ru[ck][kt]]
                    bre, bim = 0, 1
                    MM(ps[:, bre, :], u[:, kt, c0:c0 + 128], Bm[:, kt, 0, :], True, True, ruk + [rtab], [RB[bre]])
                    MM(ps[:, bim, :], u[:, kt, c0:c0 + 128], Bm[:, kt, 1, :], True, True, ruk + [rtab], [RB[bim]])
                    cs = slice(kt * 512, (kt + 1) * 512)
                    TT("dve", T[:, 0, :], ps[:, bre, :], Rre[:, cs], ALU.mult, [RB[bre], rtab], [rTb[pp]])
                    TT("dve", T[:, 2, :], ps[:, bre, :], Rim[:, cs], ALU.mult, [RB[bre], rtab], [rTb[pp]])
                    STT("dve", T[:, 1, :], ps[:, bim, :], -1.0, Rim[:, cs], ALU.mult, ALU.mult, [RB[bim], rtab], [rTb[pp]])
                    TT("dve", T[:, 3, :], ps[:, bim, :], Rre[:, cs], ALU.mult, [RB[bim], rtab], [rTb[pp]])

                GT = {}

                def stage_B1(it):
                    ck, kt = items[it]; pp = it % 2
                    T = Tb[pp]
                    gre, gim = (2, 3) if pp == 0 else (4, 5)
                    for jm in range(4):
                        MM(ps[:, gre, jm * 128:(jm + 1) * 128], T[:, 0, jm * 128:(jm + 1) * 128], tri_b[:],
                           True, False, [rTb[pp], rc], [RB[gre]])
                        MM(ps[:, gre, jm * 128:(jm + 1) * 128], T[:, 1, jm * 128:(jm + 1) * 128], tri_b[:],
                           False, True, [rTb[pp], rc], [RB[gre]])
                    for jm in range(4):
                        MM(ps[:, gim, jm * 128:(jm + 1) * 128], T[:, 2, jm * 128:(jm + 1) * 128], tri_b[:],
                           True, False, [rTb[pp], rc], [RB[gim]])
                        MM(ps[:, gim, jm * 128:(jm + 1) * 128], T[:, 3, jm * 128:(jm + 1) * 128], tri_b[:],
                           False, True, [rTb[pp], rc], [RB[gim]])
                    ta = tmp()
                    GT[it] = ta
                    gr = v3(tmpf[ta][:, 0:512]); gi = v3(tmpf[ta][:, 512:1024])
                    js = slice(4 * kt, 4 * kt + 4)
                    for jm in range(4):
                        j = 4 * kt + jm
                        ACT(gr[:, jm, :], ps[:, gre, jm * 128:(jm + 1) * 128], AF.Identity, [RB[gre], rcar], [RT[ta]],
                            bias=carry[:, 0, j:j + 1])
                    for jm in range(4):
                        j = 4 * kt + jm
                        ACT(gi[:, jm, :], ps[:, gim, jm * 128:(jm + 1) * 128], AF.Identity, [RB[gim], rcar], [RT[ta]],
                            bias=carry[:, 1, j:j + 1])
                    CP("act", gcol[:, 0, js], gr[:, :, 127], [RT[ta]], [rgc])
                    CP("act", gcol[:, 1, js], gi[:, :, 127], [RT[ta]], [rgc])
                    if kt == 3:
                        if ck < 7:
                            t1 = hsm[0][:, 0, :]; t2 = hsm[0][:, 1, :]; t3 = hsm[3][:, 0, :]; t4 = hsm[3][:, 1, :]
                            rh_ = res("hsm")
                            TT("pool", t1, L128[:, 0, :], gcol[:, 0, :], ALU.mult, [rL, rgc], [rh_])
                            TT("pool", t2, L128[:, 1, :], gcol[:, 1, :], ALU.mult, [rL, rgc], [rh_])
                            TT("pool", t3, L128[:, 0, :], gcol[:, 1, :], ALU.mult, [rL, rgc], [rh_])
                            TT("pool", t4, L128[:, 1, :], gcol[:, 0, :], ALU.mult, [rL, rgc], [rh_])
                            TT("pool", carry[:, 0, :], t1, t2, ALU.subtract, [rh_], [rcar])
                            TT("pool", carry[:, 1, :], t3, t4, ALU.add, [rh_], [rcar])
                        else:
                            cmul_small(Hst[:, l, 0, :], Hst[:, l, 1, :], gcol[:, 0, :], gcol[:, 1, :],
                                       Pre[:, :, 127], Pim[:, :, 127], [rgc, rtab], [rH])

                def stage_B2(it):
                    ck, kt = items[it]; pp = it % 2
                    Mv = Mviews[pp]
                    ta = GT.pop(it)
                    gr = v3(tmpf[ta][:, 0:512]); gi = v3(tmpf[ta][:, 512:1024])
                    js = slice(4 * kt, 4 * kt + 4)
                    TT("pool", Mv[0], gr, Pre[:, js, :], ALU.mult, [RT[ta], rtab], [rMb[pp]])
                    STT("dve", Mv[1], gi, -1.0, Pim[:, js, :], ALU.mult, ALU.mult, [RT[ta], rtab], [rMb[pp]])
                    TT("pool", Mv[2], gr, Pim[:, js, :], ALU.mult, [RT[ta], rtab], [rMb[pp]])
                    TT("pool", Mv[3], gi, Pre[:, js, :], ALU.mult, [RT[ta], rtab], [rMb[pp]])

                def stage_C(it):
                    ck, kt = items[it]; pp = it % 2; c0 = ck * 128
                    Mv = Mviews[pp]
                    ruk = [ru[ck][kt]]
                    bk = 6 + pp
                    MM(ps[:, bk, :128], diagd[:, kt, :], u[:, kt, c0:c0 + 128], True, False, ruk + [res("diagd")], [RB[bk]])
                    for jm in range(4):
                        j = 4 * kt + jm
                        MM(ps[:, bk, :128], Cm[:, 0, j, :], Mv[0][:, jm, :], False, False, [rtab, rMb[pp]], [RB[bk]])
                        MM(ps[:, bk, :128], Cm[:, 0, j, :], Mv[1][:, jm, :], False, False, [rtab, rMb[pp]], [RB[bk]])
                        MM(ps[:, bk, :128], Cm[:, 1, j, :], Mv[2][:, jm, :], False, False, [rtab, rMb[pp]], [RB[bk]])
                        MM(ps[:, bk, :128], Cm[:, 1, j, :], Mv[3][:, jm, :], False, jm == 3, [rtab, rMb[pp]], [RB[bk]])
                    ACT(u[:, kt, c0:c0 + 128], ps[:, bk, :128], AF.Gelu_apprx_tanh, [RB[bk]], ruk)

                segs = []
                if half == 0:
                    stgA = ya[:, 3:5, :].rearrange("p a b -> p (a b)").bitcast(F32)
                    sS = ya[:, 5:7, :].rearrange("p a b -> p (a b)").bitcast(F32)
                    sT = ya[:, 7, :].bitcast(F32)
                    rstg = res("s_stg"); rsS = res("s_sS"); rsT = res("s_sT")
                    rhsp = res("hsp"); rhsn = res("hsn"); rhsb = res("hsb"); rbus = res("sqb")
                    sbk = [0]

                    def sbank():
                        sbk[0] ^= 1
                        return 6 + sbk[0]

                    def seg_load(ri, nm, h2):
                        def f():
                            DMA("sp", stgA[:16, 0:1024], D[nm][l][:, h2 * 1024:(h2 + 1) * 1024], (), [rstg])
                            bk = sbank()
                            for b_ in range(8):
                                TR(ps[:, bk, b_ * 16:(b_ + 1) * 16], stgA[:16, b_ * 128:(b_ + 1) * 128], ident_f[:16, :16],
                                   [rstg, rc], [RB[bk]])
                            CP("act", hsp[:, ri, h2 * 8:(h2 + 1) * 8, :],
                               ps[:, bk, 0:128].rearrange("p (j s) -> p j s", s=16), [RB[bk]], [rhsp])
                        return f

                    for ri_, nm_ in enumerate(("sre", "sim")):
                        for h2_ in range(2):
                            segs.append(seg_load(ri_, nm_, h2_))

                    def seg_bu(kt):
                        def f():
                            for ri in range(2):
                                bk = sbank()
                                MM(ps[:16, bk, :], u[:, kt, 1024:1040], Bm[:, kt, ri, :], True, True, ru[8] + [rtab], [RB[bk]])
                                CP("act", bus[:, ri, kt * 512:(kt + 1) * 512], ps[:16, bk, :], [RB[bk]], [rbus])
                        return f

                    for kt_ in range(4):
                        segs.append(seg_bu(kt_))

                    def seg_step():
                        bk = sbank()
                        for ri in range(2):
                            for j in range(16):
                                MM(ps[:, bk, ri * 256 + j * 16:ri * 256 + (j + 1) * 16], bus[:, ri, j * 128:(j + 1) * 128],
                                   ident_b[:16, :16], True, True, [rbus, rc], [RB[bk]])
                        lre_bc = lre.unsqueeze(2).to_broadcast([128, 16, 16])
                        lim_bc = lim.unsqueeze(2).to_broadcast([128, 16, 16])
                        v16 = lambda ap: ap.rearrange("p (j s) -> p j s", s=16)
                        s1 = v16(sS[:, 0:256]); s2 = v16(sS[:, 256:512]); s3 = v16(sS[:, 512:768]); s4 = v16(sS[:, 768:1024])
                        TT("dve", s1, hsp[:, 0, :, :], lre_bc, ALU.mult, [rhsp, rs], [rsS])
                        TT("dve", s2, hsp[:, 1, :, :], lim_bc, ALU.mult, [rhsp, rs], [rsS])
                        TT("dve", s3, hsp[:, 0, :, :], lim_bc, ALU.mult, [rhsp, rs], [rsS])
                        TT("dve", s4, hsp[:, 1, :, :], lre_bc, ALU.mult, [rhsp, rs], [rsS])
                        TT("dve", s1, s1, s2, ALU.subtract, [rsS], [rsS])
                        TT("dve", s3, s3, s4, ALU.add, [rsS], [rsS])
                        TT("dve", hsn[:, 0, :, :], s1, v16(ps[:, bk, 0:256]), ALU.add, [rsS, RB[bk]], [rhsn])
                        TT("dve", hsn[:, 1, :, :], s3, v16(ps[:, bk, 256:512]), ALU.add, [rsS, RB[bk]], [rhsn])
                        CP("act", hsb, hsn, [rhsn], [rhsb])

                    segs.append(seg_step)

                    def seg_y(kt):
                        def f():
                            bk = sbank()
                            for jm in range(4):
                                j = 4 * kt + jm
                                MM(ps[:, bk, :16], Cm[:, 0, j, :], hsb[:, 0, j, :], jm == 0, False, [rtab, rhsb], [RB[bk]])
                                MM(ps[:, bk, :16], Cm[:, 1, j, :], hsb[:, 1, j, :], False, jm == 3, [rtab, rhsb], [RB[bk]])
                            STT("dve", sT[:, kt * 16:(kt + 1) * 16], u[:, kt, 1024:1040], dpar[:, kt, l:l + 1], ps[:, bk, :16],
                                ALU.mult, ALU.add, [ru[8][kt], rp, RB[bk]], [rsT])
                            ACT(u[:, kt, 1024:1040], sT[:, kt * 16:(kt + 1) * 16], AF.Gelu_apprx_tanh, [rsT], [ru[8][kt]])
                        return f

                    for kt_ in range(4):
                        segs.append(seg_y(kt_))

                    def seg_out(ri, nm, h2):
                        def f():
                            for q in range(2):
                                bk = sbank()
                                for jm in range(4):
                                    j = h2 * 8 + q * 4 + jm
                                    TR(ps[:16, bk, jm * 128:(jm + 1) * 128], hsn[:, ri, j, :], ident_f[:], [rhsn, rc], [RB[bk]])
                                CP("act", stgA[:16, q * 512:(q + 1) * 512], ps[:16, bk, :], [RB[bk]], [rstg])
                            DMA("sp", D[nm][l][:, h2 * 1024:(h2 + 1) * 1024], stgA[:16, 0:1024], [rstg], [res(nm)], is_out=True)
                        return f

                    for ri_, nm_ in enumerate(("res", "ims")):
                        for h2_ in range(2):
                            segs.append(seg_out(ri_, nm_, h2_))

                NI = len(items)
                for step in range(NI + 3):
                    if step < NI:
                        stage_A(step)
                    if 0 <= step - 2 < NI:
                        stage_B2(step - 2)
                    if 0 <= step - 1 < NI:
                        stage_B1(step - 1)
                    if 0 <= step - 3 < NI:
                        stage_C(step - 3)
                    if segs and step >= 3 and step % 2 == 1:
                        segs.pop(0)()
                while segs:
                    segs.pop(0)()
                if half == n_halves - 1:
                    for ri, nm in enumerate(("rep", "imp")):
                        bk = bank(); t = tmp()
                        TR(ps[:16, bk, :128], Hst[:, l, ri, :], ident_f[:], [rH, rc], [RB[bk]])
                        CP("dve", tmpf[t][:16, :128], ps[:16, bk, :128], [RB[bk]], [RT[t]])
                        DMA("sp", D[nm][l], tmpf[t][:16, :128], [RT[t]], [res(nm)], is_out=True)
                P.barrier()

                diag = carve(0, [4, 31, 128], BF16)
                conv32 = carve(31744, [4, NTM], F32)
                cbf = carve(48384, [4, 512], BF16)
                csq = carve(52480, [4, 512], BF16)
                bufT = carve(56576, [4, 16, 30], F32) if half == 0 else None
                rdiag = res("diag")
                idb_bc = ident_b[:].unsqueeze(1).to_broadcast([128, 31, 128])
                for ct in range(4):
                    TT("pool", diag[:, ct, :, :], idb_bc,
                       convw[:, l, ct, :].unsqueeze(2).to_broadcast([128, 31, 128]), ALU.mult, [rc, rp], [rdiag])
                usrc = lambda kt, t0, n: u[:, kt, t0:t0 + n]
                ru_l = [ru_tile(ti) for ti in range(NTT)]

                def cons_g1(ntl, ti, t0, n, bk):
                    CP("act", ya[:, ntl, t0:t0 + n], ps[:, bk, :n], [RB[bk]], [rya[ti]])

                def cons_g2(ntl, ti, t0, n, bk):
                    t = tmp()
                    ACT(tmpf[t][:, :n], ps[:, bk, :n], AF.Sigmoid, [RB[bk]], [RT[t]])
                    TT("dve", ya[:, ntl, t0:t0 + n], ya[:, ntl, t0:t0 + n], tmpf[t][:, :n], ALU.mult,
                       [rya[ti], RT[t]], [rya[ti]])

                dense(W["glu1"], 4, 1024, usrc, ru_l, TTL, cons_g1)
                dense(W["glu2"], 4, 1024, usrc, ru_l, TTL, cons_g2)
                if half == 0:
                    CP("dve", ctail[:, l, :, :], cbuf[:, :, 1024:1054], [rcb[2]], [res("ctail")])
                    for rg in range(4):
                        t = tmp()
                        DMA("sp", tmpf[t][:120, :512], D["scv"][l][rg * 120:(rg + 1) * 120, :], (), [RT[t]])
                        bk = bank()
                        for ct in range(4):
                            TR(ps[:, bk, ct * 120:(ct + 1) * 120], tmpf[t][:120, ct * 128:(ct + 1) * 128],
                               ident_f[:120, :120], [RT[t], rc], [RB[bk]])
                        CP("dve", bufT[:, :, 4 * rg:4 * rg + 4, :],
                           ps[:, bk, :480].rearrange("p (c b k) -> p c b k", c=4, b=4), [RB[bk]], [res("bufT")])
                    for ct in range(4):
                        t = tmp()
                        pv = tmpf[t][:, 0:480].rearrange("p (b k) -> p b k", k=30)
                        TT("dve", pv, bufT[:, ct, :, :], convw[:, l, ct, 0:30].unsqueeze(1).to_broadcast([128, 16, 30]),
                           ALU.mult, [res("bufT"), rp], [RT[t]])
                        P.op("dve", lambda e, o=tmpf[t][:, 512:528], i=pv: e.tensor_reduce(
                            out=o, in_=i, op=ALU.add, axis=mybir.AxisListType.X), [RT[t]], [RT[t]])
                        STT("dve", tmpf[t][:, 528:544], cnew32[:, ct, :], convw[:, l, ct, 30:31], tmpf[t][:, 512:528],
                            ALU.mult, ALU.add, [res("cnew32"), rp, RT[t]], [RT[t]])
                        ACT(conv32[:, ct, 1024:1040], tmpf[t][:, 528:544], AF.Identity, [RT[t], rp], [rar[2]],
                            bias=convb[:, ct, l:l + 1])


                cact = u

                def ln_pre(ti):
                    t0, n = TTL[ti]
                    rtmp = res("lnst")
                    ACT(cbf[:, :, :n], conv32[:, :, t0:t0 + n], AF.Copy, [rar[ti]], [rtmp])
                    ACT(csq[:, :, :n], conv32[:, :, t0:t0 + n], AF.Square, [rar[ti]], [rtmp])

                def ln_rest(ti):
                    t0, n = TTL[ti]
                    rtmp = res("lnst")
                    bm = bank()
                    for ct in range(4):
                        MM(ps[:, bm, :n], ones512[:], cbf[:, ct, :n], ct == 0, ct == 3, [rtmp, rc], [RB[bm]])
                    bq = bank()
                    for ct in range(4):
                        MM(ps[:, bq, :n], ones512[:], csq[:, ct, :n], ct == 0, ct == 3, [rtmp, rc], [RB[bq]])
                    tm, tv = tmp(), tmp()
                    CP("act", tmpf[tm][:, :n], ps[:, bm, :n], [RB[bm]], [RT[tm]])
                    TT("dve", tmpf[tv][:, :n], tmpf[tm][:, :n], tmpf[tm][:, :n], ALU.mult, [RT[tm]], [RT[tv]])
                    TT("dve", tmpf[tv][:, :n], ps[:, bq, :n], tmpf[tv][:, :n], ALU.subtract, [RB[bq], RT[tv]], [RT[tv]])
                    TS("dve", tmpf[tv][:, :n], tmpf[tv][:, :n], 0.0, None, ALU.max, None, [RT[tv]], [RT[tv]])
                    ACT(tmpf[tv][:, :n], tmpf[tv][:, :n], AF.Sqrt, [RT[tv], rc], [RT[tv]], bias=epsc[:, 0:1])
                    RECIP(tmpf[tv][:, :n], tmpf[tv][:, :n], [RT[tv]], [RT[tv]])
                    for ct in range(4):
                        tx = tm if ct % 2 == 0 else tv
                        xc = tmpf[tx][:, 512:512 + n]
                        TT("dve", xc, conv32[:, ct, t0:t0 + n], tmpf[tm][:, :n], ALU.subtract,
                           [rar[ti], RT[tm]], [RT[tx]])
                        TT("dve", xc, xc, tmpf[tv][:, :n], ALU.mult, [RT[tx], RT[tv]], [RT[tx]])
                        ACT(cact[:, ct, t0:t0 + n], xc, AF.Silu, [RT[tx], rp], ru_tile(ti, ct),
                            scale=lng[:, ct, l:l + 1], bias=lnb[:, ct, l:l + 1])

                for ti in range(2):
                    t0 = ti * 512
                    for ct in range(4):
                        bk = bank()
                        for k in range(31):
                            MM(ps[:, bk, :], diag[:, ct, k, :], cbuf[:, ct, t0 + k:t0 + k + 512], k == 0, k == 30,
                               [rdiag, rcb[ti], rcb[ti + 1]], [RB[bk]])
                        ACT(conv32[:, ct, t0:t0 + 512], ps[:, bk, :], AF.Identity, [RB[bk], rp], [rar[ti]],
                            bias=convb[:, ct, l:l + 1])
                        if ti == 1 and ct == 1:
                            ln_rest(0)
                    if ti == 0:
                        ln_pre(0)

                ybuf = carve(0, [8, NTM], BF16)
                csrc = lambda kt, t0, n: cact[:, kt, t0:t0 + n]

                def cons_yb(ntl, ti, t0, n, bk):
                    CP("act", ybuf[:, ntl, t0:t0 + n], ps[:, bk, :n], [RB[bk]], [rar2[ti], rdiag])

                def cons_gate1(g):
                    def f(ntl, ti, t0, n, bk):
                        t = tmp()
                        ACT(tmpf[t][:, :n], ps[:, bk, :n], AF.Sigmoid, [RB[bk]], [RT[t]])
                        TT("dve", ya[:, ntl + 4 * g, t0:t0 + n], ya[:, ntl + 4 * g, t0:t0 + n], tmpf[t][:, :n], ALU.mult,
                           [rya[ti], RT[t]], [rya[ti]])
                    return f

                def cons_gate2(g):
                    def f(ntl, ti, t0, n, bk):
                        t = tmp()
                        nt_ = ntl + 4 * g
                        ACT(tmpf[t][:, :n], ps[:, bk, :n], AF.Sigmoid, [RB[bk]], [RT[t]])
                        TT("dve", ybuf[:, nt_, t0:t0 + n], ybuf[:, nt_, t0:t0 + n], tmpf[t][:, :n], ALU.mult,
                           [rar2[ti], RT[t]], [rar2[ti]])
                        TT("dve", ya[:, nt_, t0:t0 + n], ya[:, nt_, t0:t0 + n], ybuf[:, nt_, t0:t0 + n], ALU.add,
                           [rya[ti], rar2[ti]], [rya[ti]])
                    return f

                ln_pre(1)
                dense(W["gate"][0], 8, 512, xsrc, rxn_l, TTL, cons_gate1(0))
                ln_rest(1)
                dense(W["gate"][1], 8, 512, xsrc, rxn_l, TTL, cons_gate1(1))
                if half == 0:
                    ln_pre(2)
                    ln_rest(2)
                dense(W["pw"], 4, 1024, csrc, ru_l, TTL, cons_yb, tis=[0, 1])
                if half == 0:
                    dense(W["pw"], 4, 1024, csrc, ru_l, TTL, cons_yb, tis=[2])
                for g in range(2):
                    dense(W["gate"][2 + g], 8, 512, xsrc, rxn_l, TTL, cons_gate2(g))
                ysrc = lambda kt, t0, n: ya[:, kt, t0:t0 + n]
                rya_l = [[r] for r in rya]

                def cons_res(ntl, ti, t0, n, bk):
                    TT("dve", x[:, ntl, t0:t0 + n], x[:, ntl, t0:t0 + n], ps[:, bk, :n], ALU.add, [rx[ti], RB[bk]], [rx[ti]])

                tisA = [0]
                tisB = list(range(1, NTT))
                gffn_sel = lambda kt, l=l: gffn[:, kt, l:l + 1]
                dense(W["outA"], 8, 512, ysrc, rya_l, TTL, cons_res, tis=tisA)
                hk = {}

                def hk_stats(sel, hk=hk):
                    (hk["t"],) = rmsnorm_stats([TTL[0]], [rx[0]], sel)

                dense(W["outB"], 8, 512, ysrc, rya_l, TTL, cons_res, tis=tisB,
                      hooks={(0, 1): (lambda: hk_stats(gffn_sel)),
                             (1, 0): (lambda: rms_apply(TTL, rx, rxn, gffn_sel, 0, hk["t"]))})

                actb = carve(0, [22, NTM], BF16)
                ffin_passes = [(0, [0], 1), ("norm", None, None), (1, [0], 1), (0, tisB, 2), (1, tisB, 2)] + \
                              [(g, list(range(NTT)), 2) for g in range(2, 11)]
                for g, tis_g, la_g in ffin_passes:
                    if g == "norm":
                        rmsnorm_to_xn(TTL, rx, rxn, gffn_sel, tis=tisB)
                        continue
                    s = ws_use(W["ffin"][g], la_g)
                    wv = wview(s, 8, 512)
                    for ti in tis_g:
                        t0, n = TTL[ti]
                        for jj in range(2):
                            j = 2 * g + jj
                            b1 = bank()
                            for kt in range(8):
                                MM(ps[:, b1, :n], wv[:, kt, jj * 128:(jj + 1) * 128], xn[:, kt, t0:t0 + n], kt == 0, kt == 7,
                                   [RW[s], rxn[ti]], [RB[b1]])
                            b2 = bank()
                            for kt in range(8):
                                MM(ps[:, b2, :n], wv[:, kt, 256 + jj * 128:256 + (jj + 1) * 128], xn[:, kt, t0:t0 + n],
                                   kt == 0, kt == 7, [RW[s], rxn[ti]], [RB[b2]])
                            t = tmp()
                            ACT(tmpf[t][:, :n], ps[:, b1, :n], AF.Silu, [RB[b1]], [RT[t]])
                            TT("dve", actb[:, j, t0:t0 + n], tmpf[t][:, :n], ps[:, b2, :n], ALU.mult,
                               [RT[t], RB[b2]], [rar[ti], rar2[ti]])
                asrc = lambda kt, t0, n: actb[:, kt, t0:t0 + n]
                rar_l = [[r] for r in rar]
                dense(W["ffoutA"], 22, 128, asrc, rar_l, TTL, cons_res, tis=tisA)
                nxt_hooks = None
                if l + 1 < n_layers:
                    gnext = lambda kt, l=l: gmix[:, kt, l + 1:l + 2]
                    hk2 = {}

                    def hk2_stats(hk2=hk2, gnext=gnext):
                        (hk2["t"],) = rmsnorm_stats([TTL[0]], [rx[0]], gnext)

                    nxt_hooks = {(1, 0): hk2_stats,
                                 (4, 0): (lambda hk2=hk2, gnext=gnext: rms_apply(TTL, rx, rxn, gnext, 0, hk2["t"]))}
                dense(W["ffoutB"], 22, 128, asrc, rar_l, TTL, cons_res, tis=tisB, hooks=nxt_hooks)

            P.epoch += 1
            xo = ya[:, :, :].rearrange("p a b -> p (a b)").bitcast(F32)[:, 0:4096].rearrange("p (k t) -> p k t", t=512)
            rxo = [res("xo")] + list(rya)
            for ti, (t0, n) in enumerate(TTL):
                (t,) = rmsnorm_stats([TTL[ti]], [rx[ti]], None)
                for kt in range(8):
                    STT("dve", xo[:, kt, :n], x[:, kt, t0:t0 + n], gfin[:, kt, 0:1], tmpf[t][:, :n],
                        ALU.mult, ALU.mult, [rx[ti], RT[t], rp], rxo)
                nb = (n + 127) // 128
                for b in range(nb):
                    rows = min(128, n - b * 128)
                    to = tmp()
                    for q in range(2):
                        bk = bank()
                        for k4 in range(4):
                            kt = 4 * q + k4
                            TR(ps[:rows, bk, k4 * 128:(k4 + 1) * 128], xo[:, kt, b * 128:b * 128 + rows], ident_f[:],
                               rxo + [rc], [RB[bk]])
                        CP("act" if q == 0 else "dve", tmpf[to][:rows, q * 512:(q + 1) * 512], ps[:rows, bk, :], [RB[bk]], [RT[to]])
                    if ti < 2:
                        r0 = half * 1024 + t0 + b * 128
                        DMA("sp", D["yp"][r0:r0 + 128, :], tmpf[to][:rows, :], [RT[to]], [res("yp")], is_out=True)
                    else:
                        DMA("sp", D["ys"], tmpf[to][:rows, :], [RT[to]], [res("ys")], is_out=True)

        P.finish()
        P.emit(ctx)
    return nc


_NC_CACHE = {}


def _consts():
    k = {}
    k["k_ident"] = np.eye(128, dtype=np.float32)
    k["k_tri"] = np.triu(np.ones((128, 128), dtype=np.float32))
    k["k_svec"] = np.tile(np.arange(128, dtype=np.float32)[None, :], (128, 1))
    k["k_pidx"] = np.arange(128, dtype=np.float32)[:, None].copy()
    par = np.zeros((128, 2), dtype=np.float32)
    gl = (np.arange(128) // 16) % 2
    par[:, 0] = (gl == 0)
    par[:, 1] = (gl == 1)
    k["k_par"] = par
    return k


def kernel(x_prompt, x_sample, state_ssm_re, state_ssm_im, state_conv,
           g_mix, w_in, ssm_a_re, ssm_a_im, ssm_log_dt, ssm_b_re, ssm_b_im,
           ssm_c_re, ssm_c_im, ssm_d, w_glu, conv_w, conv_b, conv_ln_g, conv_ln_b,
           w_pw, w_out, g_ffn, w_ff_in, w_ff_out, g_final):
    f = lambda a: np.ascontiguousarray(np.asarray(a, dtype=np.float32))
    if "nc" not in _NC_CACHE:
        _NC_CACHE["nc"] = build()
    nc = _NC_CACHE["nc"]
    shared = dict(
        g_mix=f(g_mix), w_in=f(w_in), a_re=f(ssm_a_re).reshape(4, 2048), a_im=f(ssm_a_im).reshape(4, 2048),
        log_dt=f(ssm_log_dt), b_re=f(ssm_b_re).reshape(4, 2048, 16), b_im=f(ssm_b_im).reshape(4, 2048, 16),
        c_re=f(ssm_c_re).reshape(4, 512, 64), c_im=f(ssm_c_im).reshape(4, 512, 64), ssm_d=f(ssm_d),
        w_glu=f(w_glu), conv_w=f(conv_w), conv_b=f(conv_b), ln_g=f(conv_ln_g), ln_b=f(conv_ln_b),
        w_pw=f(w_pw), w_out=f(w_out), g_ffn=f(g_ffn), w_ff_in=f(w_ff_in), w_ff_out=f(w_ff_out),
        g_final=f(g_final).reshape(1, 1024),
    )
    shared.update(_consts())
    xp = f(x_prompt); xs = f(x_sample).reshape(128, 1024)
    sre = f(state_ssm_re).reshape(4, 128, 2048); sim = f(state_ssm_im).reshape(4, 128, 2048)
    scv = f(state_conv).reshape(4, 128, 30, 512)
    in_maps = []
    for c in range(8):
        m = dict(shared)
        m["xp"] = xp[c]
        m["xs"] = np.ascontiguousarray(xs[16 * c:16 * c + 16])
        m["sre"] = np.ascontiguousarray(sre[:, 16 * c:16 * c + 16])
        m["sim"] = np.ascontiguousarray(sim[:, 16 * c:16 * c + 16])
        m["scv"] = np.ascontiguousarray(scv[:, 16 * c:16 * c + 16].reshape(4, 480, 512))
        in_maps.append(m)
    res = run_bass_kernel_spmd(nc, in_maps, core_ids=list(range(8)))
    r = res.results
    y_prompt = np.stack([r[c]["yp"] for c in range(8)], 0).astype(np.float32)
    y_sample = np.concatenate([r[c]["ys"] for c in range(8)], 0).reshape(128, 1, 1024).astype(np.float32)
    re_p = np.stack([r[c]["rep"].reshape(4, 32, 64) for c in range(8)], 1).astype(np.float32)
    im_p = np.stack([r[c]["imp"].reshape(4, 32, 64) for c in range(8)], 1).astype(np.float32)
    conv_p = np.stack([r[c]["cvp"] for c in range(8)], 1).astype(np.float32)
    re_s = np.concatenate([r[c]["res"].reshape(4, 16, 32, 64) for c in range(8)], 1).astype(np.float32)
    im_s = np.concatenate([r[c]["ims"].reshape(4, 16, 32, 64) for c in range(8)], 1).astype(np.float32)
    conv_s = np.concatenate([r[c]["cvs"] for c in range(8)], 1).astype(np.float32)
    return (y_prompt, y_sample, re_p, im_p, conv_p, re_s, im_s, conv_s)
```

```python
import math
import os
from contextlib import ExitStack
import numpy as np
import concourse.bass as bass
import concourse.mybir as mybir
from concourse.bass_utils import run_bass_kernel_spmd

F32 = mybir.dt.float32
BF16 = mybir.dt.bfloat16
ALU = mybir.AluOpType
AF = mybir.ActivationFunctionType

ENGS = ("pe", "act", "dve", "pool", "sp")
DMA_Q = ("sp", "act", "pool")
NDMA_SEM = 8

DEPTH = 4
EPS = 1e-6
TWO_PI = 2.0 * math.pi
C1 = 6.28125
C2 = TWO_PI - 6.28125
MAGIC = 12582912.0


class Res:
    __slots__ = ("name", "lw", "rd")

    def __init__(self, name):
        self.name = name
        self.lw = None
        self.rd = {}


class Prog:
    def __init__(self, nc):
        self.nc = nc
        self.ins = {e: [] for e in ENGS}
        self.ndma = {q: 0 for q in DMA_Q}
        self.dmas = []
        self.epoch = 0
        self.all_out_dmas = []
        self._dma_id_of = {}

    def _deps(self, reads, writes):
        deps = set()
        for r in reads:
            if r.lw is not None:
                deps.add(r.lw)
        for w in writes:
            if w.lw is not None:
                deps.add(w.lw)
            for k, v in w.rd.items():
                if isinstance(k, tuple):
                    deps.add(k)
                else:
                    deps.add((k, v))
        return deps

    def op(self, eng, fn, reads=(), writes=()):
        idx = len(self.ins[eng])
        deps = self._deps(reads, writes)
        self.ins[eng].append(dict(fn=fn, deps=deps, dma=None, epoch=self.epoch))
        for r in reads:
            r.rd[eng] = idx
        for w in writes:
            w.lw = (eng, idx)
            w.rd = {}
        return (eng, idx)

    def dma(self, q, fn, reads=(), writes=(), is_out=False):
        deps = self._deps(reads, writes)
        n = self.ndma[q]
        self.ndma[q] += 1
        did = len(self.dmas)
        self.dmas.append((q, n))
        if n >= NDMA_SEM:
            deps.add(("dma", self._dma_id_of[(q, n - NDMA_SEM)]))
        self._dma_id_of[(q, n)] = did
        self.ins[q].append(dict(fn=fn, deps=deps, dma=did, epoch=self.epoch))
        key = ("dma", did)
        for r in reads:
            r.rd[key] = True
        for w in writes:
            w.lw = key
            w.rd = {}
        if is_out:
            self.all_out_dmas.append(key)
        return key

    def barrier(self):
        deps = set()
        for e in ENGS:
            if self.ins[e]:
                deps.add((e, len(self.ins[e]) - 1))
        for q in DMA_Q:
            for n in range(max(0, self.ndma[q] - NDMA_SEM), self.ndma[q]):
                deps.add(("dma", self._dma_id_of[(q, n)]))
        for e in ENGS:
            self.ins[e].append(dict(fn=None, deps=set(deps), dma=None, epoch=self.epoch))

    def finish(self, eng="sp"):
        self.ins[eng].append(dict(fn=None, deps=set(self.all_out_dmas), dma=None, epoch=self.epoch))

    def emit(self, ctx):
        nc = self.nc
        signal = {e: set() for e in ENGS}
        for e in ENGS:
            for i, ins in enumerate(self.ins[e]):
                for d in ins["deps"]:
                    if d[0] != "dma":
                        if d[0] == "pe" and e == "pe":
                            continue
                        if d[0] == e and d[1] >= i:
                            continue
                        signal[d[0]].add(d[1])
        for e in ENGS:
            fixed = set()
            for i in signal[e]:
                j = i
                while j >= 0 and (self.ins[e][j]["fn"] is None or self.ins[e][j]["dma"] is not None):
                    j -= 1
                fixed.add((i, j))
            signal[e] = fixed
        rank = {}
        sigidx = {e: {} for e in ENGS}
        for e in ENGS:
            cnt = {}
            seen = {}
            for i, j in sorted(signal[e], key=lambda t: (t[1], t[0])):
                if j < 0:
                    rank[(e, i)] = None
                    continue
                ep = self.ins[e][j]["epoch"]
                if j not in seen:
                    cnt[ep] = cnt.get(ep, 0) + 1
                    seen[j] = (ep, cnt[ep])
                    sigidx[e][j] = ep
                rank[(e, i)] = seen[j]
        sems = {}
        for e in ENGS:
            for ep in sorted(set(sigidx[e].values())):
                sems[(e, ep)] = ctx.enter_context(nc.semaphore(f"s_{e}_{ep}"))
        dsem = {}
        for q in DMA_Q:
            for k in range(min(NDMA_SEM, self.ndma[q])):
                dsem[(q, k)] = ctx.enter_context(nc.semaphore(f"d_{q}_{k}"))
        prog = self

        def run_engine(e, eng):
            waited = {}
            dwaited = set()
            for i, ins in enumerate(prog.ins[e]):
                need = {}
                dneed = []
                for d in ins["deps"]:
                    if d[0] == "dma":
                        if d[1] not in dwaited:
                            dneed.append(d[1])
                    else:
                        te, ti = d
                        if te == "pe" and e == "pe":
                            continue
                        if te == e and ti >= i:
                            continue
                        if waited.get(te, -1) >= ti:
                            continue
                        need[te] = max(need.get(te, -1), ti)
                for te, ti in need.items():
                    rk = rank[(te, ti)]
                    if rk is not None:
                        eng.wait_ge(sems[(te, rk[0])], rk[1])
                    waited[te] = ti
                for did in sorted(dneed):
                    q, n = prog.dmas[did]
                    eng.wait_ge(dsem[(q, n % NDMA_SEM)], 16 * (n // NDMA_SEM + 1))
                    dwaited.add(did)
                if ins["fn"] is None:
                    continue
                r = ins["fn"](eng)
                if ins["dma"] is not None:
                    q, n = prog.dmas[ins["dma"]]
                    r.then_inc(dsem[(q, n % NDMA_SEM)], 16)
                elif i in sigidx[e]:
                    r.then_inc(sems[(e, sigidx[e][i])], 1)

        with nc.Block() as block:
            @block.tensor
            def _(eng):
                run_engine("pe", eng)

            @block.scalar
            def _(eng):
                run_engine("act", eng)

            @block.vector
            def _(eng):
                run_engine("dve", eng)

            @block.gpsimd
            def _(eng):
                run_engine("pool", eng)

            @block.sync
            def _(eng):
                run_engine("sp", eng)


def build(n_layers=DEPTH, n_halves=2):
    nc = bass.Bass("TRN2", target_bir_lowering=False)
    D = {}

    def inp(name, shape):
        D[name] = nc.dram_tensor(name, shape, F32, kind="ExternalInput").ap()

    def outp(name, shape):
        D[name] = nc.dram_tensor(name, shape, F32, kind="ExternalOutput").ap()

    inp("xp", [2048, 1024]); inp("xs", [16, 1024])
    inp("sre", [4, 16, 2048]); inp("sim", [4, 16, 2048]); inp("scv", [4, 480, 512])
    inp("g_mix", [4, 1024]); inp("w_in", [4, 1024, 3584])
    inp("a_re", [4, 2048]); inp("a_im", [4, 2048]); inp("log_dt", [4, 32])
    inp("b_re", [4, 2048, 16]); inp("b_im", [4, 2048, 16])
    inp("c_re", [4, 512, 64]); inp("c_im", [4, 512, 64]); inp("ssm_d", [4, 512])
    inp("w_glu", [4, 512, 2048]); inp("conv_w", [4, 31, 512]); inp("conv_b", [4, 512])
    inp("ln_g", [4, 512]); inp("ln_b", [4, 512]); inp("w_pw", [4, 512, 1024])
    inp("w_out", [4, 1024, 1024]); inp("g_ffn", [4, 1024]); inp("w_ff_in", [4, 1024, 5632])
    inp("w_ff_out", [4, 2816, 1024]); inp("g_final", [1, 1024])
    inp("k_ident", [128, 128]); inp("k_tri", [128, 128]); inp("k_svec", [128, 128])
    inp("k_pidx", [128, 1]); inp("k_par", [128, 2])
    outp("yp", [2048, 1024]); outp("ys", [16, 1024])
    outp("rep", [4, 16, 128]); outp("imp", [4, 16, 128]); outp("cvp", [4, 30, 512])
    outp("res", [4, 16, 2048]); outp("ims", [4, 16, 2048]); outp("cvs", [4, 16, 30, 512])
    scrB = nc.dram_tensor("scrB", [4, 128, 4096], BF16).ap()
    scrC = nc.dram_tensor("scrC", [4, 128, 4096], BF16).ap()
    scrR = nc.dram_tensor("scrR", [4, 2, 128, 2048], F32).ap()
    scrP = nc.dram_tensor("scrP", [4, 2, 128, 2048], F32).ap()

    with ExitStack() as ctx:
        def sb(name, shape, dt):
            return ctx.enter_context(nc.sbuf_tensor(name, shape, dt))

        P = Prog(nc)
        NTM = 1040
        x = sb("x", [128, 8, NTM], F32)
        xn = sb("xn", [128, 8, NTM], BF16)
        u = sb("u", [128, 4, NTM], BF16)
        cbuf = sb("cbuf", [128, 4, 1054], BF16)
        ya = sb("ya", [128, 8, NTM], BF16)
        arena = sb("arena", [128, 16448], F32)
        wsl = [sb(f"wsl{i}", [128, 4096], BF16) for i in range(3)]
        tmpf = [sb(f"tmpf{i}", [128, 1024], F32) for i in range(4)]
        sqb = sb("sqb", [128, 8, 512], BF16)
        ps = ctx.enter_context(nc.psum_tensor("ps", [128, 8, 512], F32))
        ident_f = sb("ident_f", [128, 128], F32)
        ident_b = sb("ident_b", [128, 128], BF16)
        tri_b = sb("tri_b", [128, 128], BF16)
        ones1024 = sb("ones1024", [128, 128], BF16)
        ones512 = sb("ones512", [128, 128], BF16)
        svec = sb("svec", [128, 128], F32)
        pidx = sb("pidx", [128, 1], F32)
        par = sb("par", [128, 2], F32)
        gmix = sb("gmix", [128, 8, 4], F32)
        gffn = sb("gffn", [128, 8, 4], F32)
        gfin = sb("gfin", [128, 8, 1], F32)
        dpar = sb("dpar", [128, 4, 4], F32)
        convb = sb("convb", [128, 4, 4], F32)
        lng = sb("lng", [128, 4, 4], F32)
        lnb = sb("lnb", [128, 4, 4], F32)
        convw = sb("convw", [128, 4, 4, 31], F32)
        are = sb("are", [128, 16, 4], F32)
        aim = sb("aim", [128, 16, 4], F32)
        ldt = sb("ldt", [128, 4, 16], F32)
        sm = {n: sb("sm_" + n, [128, 4, 16], F32) for n in
              ["dt", "dr", "th", "mag", "cs", "sn", "lre", "lim", "t1", "t2", "t3", "t4", "qre", "qim", "den"]}
        Hst = sb("Hst", [128, 4, 2, 16], F32)
        carry = sb("carry", [128, 2, 16], F32)
        hsm = [sb(f"hsm{i}", [128, 2, 16], F32) for i in range(5)]
        ctail = sb("ctail", [128, 4, 4, 30], BF16)
        cnew32 = sb("cnew32", [128, 4, 16], F32)
        ctail32 = sb("ctail32", [128, 4, 30], F32)
        hsp = ya[:, 0, 0:1024].bitcast(F32).rearrange("p (r j s) -> p r j s", r=2, j=16)
        hsn = ya[:, 1, 0:1024].bitcast(F32).rearrange("p (r j s) -> p r j s", r=2, j=16)
        hsb = ya[:, 2, 0:512].rearrange("p (r j s) -> p r j s", r=2, j=16)
        bus = sqb[:16, :, :].rearrange("p a b -> p (a b)").rearrange("p (r c) -> p r c", r=2)

        R = {}

        def res(name):
            if name not in R:
                R[name] = Res(name)
            return R[name]

        RB = [res(f"bank{i}") for i in range(8)]
        RW = [res(f"wsl{i}") for i in range(3)]
        RT = [res(f"tmpf{i}") for i in range(4)]
        RA = res("arena_generic")
        st = dict(bank=0, wsl=0, tmp=0)

        def bank():
            b = st["bank"]; st["bank"] = (b + 1) % 8
            return b

        def tmp():
            t = st["tmp"]; st["tmp"] = (t + 1) % 4
            return t

        def MM(out, lhsT, rhs, start, stop, reads, writes):
            P.op("pe", lambda e, o=out, l=lhsT, r=rhs, s=start, t=stop:
                 e.matmul(o, lhsT=l, rhs=r, start=s, stop=t), reads, writes)

        def TR(out, in_, idn, reads, writes):
            P.op("pe", lambda e, o=out, i=in_, d=idn: e.transpose(o, i, d), reads, writes)

        def ACT(out, in_, func, reads, writes, **kw):
            P.op("act", lambda e, o=out, i=in_, f=func, k=kw: e.activation(out=o, in_=i, func=f, **k), reads, writes)

        def TT(eng, out, in0, in1, op, reads, writes):
            P.op(eng, lambda e, o=out, a=in0, b=in1, p=op: e.tensor_tensor(out=o, in0=a, in1=b, op=p), reads, writes)

        def TS(eng, out, in0, s1, s2, op0, op1, reads, writes):
            if s2 is None:
                P.op(eng, lambda e, o=out, a=in0, x1=s1, p0=op0:
                     e.tensor_scalar(out=o, in0=a, scalar1=x1, scalar2=None, op0=p0), reads, writes)
            else:
                P.op(eng, lambda e, o=out, a=in0, x1=s1, x2=s2, p0=op0, p1=op1:
                     e.tensor_scalar(out=o, in0=a, scalar1=x1, scalar2=x2, op0=p0, op1=p1), reads, writes)

        def STT(eng, out, in0, scalar, in1, op0, op1, reads, writes):
            P.op(eng, lambda e, o=out, a=in0, s=scalar, b=in1, p0=op0, p1=op1:
                 e.scalar_tensor_tensor(out=o, in0=a, scalar=s, in1=b, op0=p0, op1=p1), reads, writes)

        def CP(eng, out, in_, reads, writes):
            if eng == "act":
                P.op("act", lambda e, o=out, i=in_: e.copy(out=o, in_=i), reads, writes)
            else:
                P.op(eng, lambda e, o=out, i=in_: e.tensor_copy(out=o, in_=i), reads, writes)

        def MEMSET(eng, ap, val, writes):
            P.op(eng, lambda e, a=ap, v=val: e.memset(a, v), (), writes)

        def DMA(q, out, in_, reads, writes, is_out=False, slow=False):
            def fn(e, o=out, i=in_, s=slow):
                if s:
                    with nc.allow_non_contiguous_dma(reason="small strided parameter/state transfer"):
                        return e.dma_start(out=o, in_=i)
                return e.dma_start(out=o, in_=i)
            P.dma(q, fn, reads, writes, is_out=is_out)

        def RECIP(out, in_, reads, writes):
            P.op("dve", lambda e, o=out, i=in_: e.reciprocal(out=o, in_=i), reads, writes)

        def carve(off, shape, dt):
            n = int(np.prod(shape))
            nb = n * (4 if dt == F32 else 2)
            assert off % 4 == 0 and nb % 4 == 0 and off + nb <= 65792, (off, shape)
            a = arena[:, off // 4:(off + nb) // 4]
            if dt != F32:
                a = a.bitcast(dt)
            if len(shape) == 2:
                a = a.rearrange("p (a b) -> p a b", b=shape[1])
            elif len(shape) == 3:
                a = a.rearrange("p (a b c) -> p a b c", b=shape[1], c=shape[2])
            return a

        rc = res("consts")
        DMA("sp", ident_f[:], D["k_ident"], (), [rc])
        DMA("sp", svec[:], D["k_svec"], (), [rc])
        DMA("sp", pidx[:], D["k_pidx"], (), [rc])
        DMA("sp", par[:], D["k_par"], (), [rc])
        DMA("pool", ident_b[:], D["k_ident"], (), [rc])
        DMA("pool", tri_b[:], D["k_tri"], (), [rc])
        MEMSET("dve", ones1024[:], 1.0 / 1024.0, [rc])
        MEMSET("dve", ones512[:], 1.0 / 512.0, [rc])
        MEMSET("dve", Hst[:], 0.0, [res("Hst")])
        MEMSET("dve", ctail[:], 0.0, [res("ctail")])

        def load_T(src, rows, ncols, dst_fn, wres, eng_alt=[0]):
            t = tmp()
            DMA("sp", tmpf[t][:rows, :ncols], src, (), [RT[t]])
            for b in range(ncols // 128):
                bk = bank()
                TR(ps[:, bk, :rows], tmpf[t][:rows, b * 128:(b + 1) * 128], ident_f[:rows, :rows],
                   [RT[t], rc], [RB[bk]])
                eng = "act" if (eng_alt[0] % 2 == 0) else "dve"
                eng_alt[0] += 1
                CP(eng, dst_fn(b), ps[:, bk, :rows], [RB[bk]], wres)

        KSTOP = int(os.environ.get("KSTOP", "99"))
        if KSTOP == 1:
            P.finish(); P.emit(ctx); return nc
        rp = res("params")
        load_T(D["g_mix"], 4, 1024, lambda b: gmix[:, b, :], [rp])
        load_T(D["g_ffn"], 4, 1024, lambda b: gffn[:, b, :], [rp])
        load_T(D["g_final"], 1, 1024, lambda b: gfin[:, b, :], [rp])
        load_T(D["ssm_d"], 4, 512, lambda b: dpar[:, b, :], [rp])
        load_T(D["conv_b"], 4, 512, lambda b: convb[:, b, :], [rp])
        load_T(D["ln_g"], 4, 512, lambda b: lng[:, b, :], [rp])
        load_T(D["ln_b"], 4, 512, lambda b: lnb[:, b, :], [rp])
        for l in range(4):
            load_T(D["conv_w"][l], 31, 512, lambda b, l=l: convw[:, l, b, :], [rp])
        for h2 in range(2):
            load_T(D["a_re"][:, h2 * 1024:(h2 + 1) * 1024], 4, 1024, lambda b, h2=h2: are[:, h2 * 8 + b, :], [rp])
            load_T(D["a_im"][:, h2 * 1024:(h2 + 1) * 1024], 4, 1024, lambda b, h2=h2: aim[:, h2 * 8 + b, :], [rp])
        if KSTOP == 2:
            P.finish(); P.emit(ctx); return nc
        t_ld = tmp()
        DMA("sp", tmpf[t_ld][:, 0:128], D["log_dt"].rearrange("l g -> (l g)").partition_broadcast(128), (), [RT[t_ld]])
        for g2 in range(2):
            srcv = tmpf[t_ld][64 * g2:64 * g2 + 64, 0:128].rearrange("p (l j two) -> p l j two", l=4, two=2)[:, :, :, g2]
            CP("dve", ldt[64 * g2:64 * g2 + 64, :, :], srcv, [RT[t_ld]], [rp])

        def load_x(half, rx):
            blocks = [(D["xp"][half * 1024 + b * 128: half * 1024 + (b + 1) * 128, :], 128, b * 128) for b in range(8)]
            if half == 0:
                blocks.append((D["xs"], 16, 1024))
            for bi, (src, rows, c0) in enumerate(blocks):
                t = tmp()
                DMA("sp", tmpf[t][:rows, :], src, (), [RT[t]])
                ti = c0 // 512
                for q in range(2):
                    bk = bank()
                    for k4 in range(4):
                        kt = 4 * q + k4
                        TR(ps[:, bk, k4 * rows:(k4 + 1) * rows], tmpf[t][:rows, kt * 128:(kt + 1) * 128],
                           ident_f[:rows, :rows], [RT[t], rc], [RB[bk]])
                    CP("act" if q == 0 else "dve", x[:, 4 * q:4 * q + 4, c0:c0 + rows],
                       ps[:, bk, :4 * rows].rearrange("p (k t) -> p k t", t=rows), [RB[bk]], [rx[ti]])


        rx0 = [res(f"x{ti}") for ti in range(3)]
        if n_halves > 0:
            load_x(0, rx0)
        if KSTOP == 3:
            P.finish(); P.emit(ctx); return nc
        rs = res("s5small")

        def sincos(eng, A, n, cs_out, sn_out, t_a, t_b, rA, rO):
            TS(eng, t_a, A, 1.0 / TWO_PI, MAGIC, ALU.mult, ALU.add, rA, rO)
            TS(eng, t_a, t_a, -MAGIC, None, ALU.add, None, rO, rO)
            STT(eng, t_b, t_a, -C1, A, ALU.mult, ALU.add, rA + rO, rO)
            STT(eng, t_b, t_a, -C2, t_b, ALU.mult, ALU.add, rO, rO)
            TS(eng, t_b, t_b, -math.pi, math.pi, ALU.max, ALU.min, rO, rO)
            ACT(sn_out, t_b, AF.Sin, rO, rO)
            ACT(t_a, t_b, AF.Sin, rO, rO, scale=0.5)
            TT(eng, t_a, t_a, t_a, ALU.mult, rO, rO)
            TS(eng, cs_out, t_a, -2.0, 1.0, ALU.mult, ALU.add, rO, rO)

        halfpi = sb("halfpi", [128, 1], F32)
        epsc = sb("epsc", [128, 1], F32)
        MEMSET("dve", halfpi[:], math.pi / 2.0, [rc])
        MEMSET("dve", epsc[:], EPS, [rc])

        S = {k: v[:] for k, v in sm.items()}
        are_v = are[:].rearrange("p j l -> p l j")
        aim_v = aim[:].rearrange("p j l -> p l j")
        ACT(S["dt"], ldt[:], AF.Exp, [rp], [rs])
        TT("dve", S["dr"], S["dt"], are_v, ALU.mult, [rs, rp], [rs])
        TT("dve", S["th"], S["dt"], aim_v, ALU.mult, [rs, rp], [rs])
        ACT(S["mag"], S["dr"], AF.Exp, [rs], [rs])
        sincos("dve", S["th"], 64, S["cs"], S["sn"], S["t1"], S["t2"], [rs], [rs])
        TT("dve", S["lre"], S["mag"], S["cs"], ALU.mult, [rs], [rs])
        TT("dve", S["lim"], S["mag"], S["sn"], ALU.mult, [rs], [rs])
        TS("dve", S["t1"], S["lre"], -1.0, None, ALU.add, None, [rs], [rs])
        TT("dve", S["t2"], are_v, are_v, ALU.mult, [rs, rp], [rs])
        TT("dve", S["t3"], aim_v, aim_v, ALU.mult, [rs, rp], [rs])
        TT("dve", S["den"], S["t2"], S["t3"], ALU.add, [rs], [rs])
        RECIP(S["den"], S["den"], [rs], [rs])
        TT("dve", S["t2"], S["t1"], are_v, ALU.mult, [rs, rp], [rs])
        TT("dve", S["t3"], S["lim"], aim_v, ALU.mult, [rs, rp], [rs])
        TT("dve", S["t2"], S["t2"], S["t3"], ALU.add, [rs], [rs])
        TT("dve", S["qre"], S["t2"], S["den"], ALU.mult, [rs], [rs])
        TT("dve", S["t2"], S["lim"], are_v, ALU.mult, [rs, rp], [rs])
        TT("dve", S["t3"], S["t1"], aim_v, ALU.mult, [rs, rp], [rs])
        TT("dve", S["t2"], S["t2"], S["t3"], ALU.subtract, [rs], [rs])
        TT("dve", S["qim"], S["t2"], S["den"], ALU.mult, [rs], [rs])

        A_ang = carve(0, [2048], F32)
        A_ta = carve(8192, [2048], F32)
        A_tb = carve(16384, [2048], F32)
        A_cs = carve(24576, [2048], F32)
        A_sn = carve(32768, [2048], F32)
        A_mg = carve(40960, [2048], F32)
        BpadRe = carve(49152, [16, 128], F32)
        BpadIm = carve(57344, [16, 128], F32)
        rAng, rTa, rTb, rCs, rSn, rMg, rBpR, rBpI = [res("prep_" + n_) for n_ in ("ang", "ta", "tb", "cs", "sn", "mg", "bpr", "bpi")]
        rSC = [rTa, rTb, rCs, rSn]
        MEMSET("pool", BpadRe, 0.0, [rBpR])
        MEMSET("pool", BpadIm, 0.0, [rBpI])
        n_prep = n_layers
        for l in range(n_prep):
            rl = [RA]
            ang3 = A_ang.rearrange("p (j s) -> p j s", s=128)
            sv_bc = svec[:].unsqueeze(1).to_broadcast([128, 16, 128])
            th_bc = sm["th"][:, l, :].unsqueeze(2).to_broadcast([128, 16, 128])
            dr_bc = sm["dr"][:, l, :].unsqueeze(2).to_broadcast([128, 16, 128])
            TT("dve", ang3, sv_bc, th_bc, ALU.mult, [rc, rs], [rAng])
            sincos("dve", A_ang, 2048, A_cs, A_sn, A_ta, A_tb, [rAng], rSC)
            TT("dve", A_ta.rearrange("p (j s) -> p j s", s=128), sv_bc, dr_bc, ALU.mult, [rc, rs], [rTa])
            ACT(A_mg, A_ta, AF.Exp, [rTa], [rMg])
            TT("dve", A_cs, A_cs, A_mg, ALU.mult, [rCs, rMg], [rCs])
            TT("dve", A_sn, A_sn, A_mg, ALU.mult, [rSn, rMg], [rSn])
            DMA("sp", scrP[l, 0], A_cs, [rCs], [res("scr")])
            DMA("sp", scrP[l, 1], A_sn, [rSn], [res("scr")])
            ACT(A_ang, A_mg, AF.Copy, [rMg], [rAng])
            P.op("dve", lambda e, o=A_mg, i=A_ang: e.reciprocal(out=o, in_=i), [rAng], [rMg])
            TT("dve", A_mg, A_mg, A_mg, ALU.mult, [rMg], [rMg])
            TT("dve", A_ta, A_cs, A_mg, ALU.mult, [rCs, rMg], [rTa])
            STT("dve", A_tb, A_sn, -1.0, A_mg, ALU.mult, ALU.mult, [rSn, rMg], [rTb])
            for (Qsrc, rQ, ri_) in ((A_ta, rTa, 0), (A_tb, rTb, 1)):
                for q4 in range(4):
                    bk = bank()
                    for jm in range(4):
                        j = 4 * q4 + jm
                        TR(ps[:, bk, jm * 128:(jm + 1) * 128], Qsrc[:, j * 128:(j + 1) * 128], ident_f[:], [rQ, rc], [RB[bk]])
                    CP("act" if q4 % 2 == 0 else "dve", A_ang[:, q4 * 512:(q4 + 1) * 512], ps[:, bk, :], [RB[bk]], [rAng])
                DMA("sp", scrR[l, ri_], A_ang, [rAng], [res("scr")])
            Braw_re = A_ta.rearrange("p (a b) -> p a b", b=128)[:, :, 0:16]
            Braw_im = A_tb.rearrange("p (a b) -> p a b", b=128)[:, :, 0:16]
            Bb_re = A_ta.rearrange("p (a b) -> p a b", b=128)[:, :, 16:32]
            Bb_im = A_tb.rearrange("p (a b) -> p a b", b=128)[:, :, 16:32]
            Bt1 = A_ta.rearrange("p (a b) -> p a b", b=128)[:, :, 32:48]
            Bt2 = A_tb.rearrange("p (a b) -> p a b", b=128)[:, :, 32:48]
            DMA("sp", Braw_re, D["b_re"][l].rearrange("(j i) c -> i j c", i=128), (), [rTa])
            DMA("sp", Braw_im, D["b_im"][l].rearrange("(j i) c -> i j c", i=128), (), [rTb])
            qre_bc = sm["qre"][:, l, :].unsqueeze(2).to_broadcast([128, 16, 16])
            qim_bc = sm["qim"][:, l, :].unsqueeze(2).to_broadcast([128, 16, 16])
            TT("dve", Bt1, Braw_re, qre_bc, ALU.mult, [rTa, rTb] + [rs], [rTa, rTb])
            TT("dve", Bt2, Braw_im, qim_bc, ALU.mult, [rTa, rTb] + [rs], [rTa, rTb])
            TT("dve", Bb_re, Bt1, Bt2, ALU.subtract, [rTa, rTb], [rTa, rTb])
            TT("dve", Bt1, Braw_im, qre_bc, ALU.mult, [rTa, rTb] + [rs], [rTa, rTb])
            TT("dve", Bt2, Braw_re, qim_bc, ALU.mult, [rTa, rTb] + [rs], [rTa, rTb])
            TT("dve", Bb_im, Bt1, Bt2, ALU.add, [rTa, rTb], [rTa, rTb])
            for (Bb, Bpad, rBp) in ((Bb_re, BpadRe, rBpR), (Bb_im, BpadIm, rBpI)):
                for g2 in range(2):
                    for jm in range(4):
                        c0 = 16 * (2 * jm + g2)
                        CP("dve", Bpad[64 * g2:64 * g2 + 64, jm::4, c0:c0 + 16],
                           Bb[64 * g2:64 * g2 + 64, jm::4, :], [rTa, rTb], [rBp])
            Bm_sb = A_cs.bitcast(BF16).rearrange("p (k r c) -> p k r c", k=4, r=2)
            for ri, (Bpad, rBp) in enumerate(((BpadRe, rBpR), (BpadIm, rBpI))):
                for kt in range(4):
                    bk = bank()
                    for jm in range(4):
                        TR(ps[:, bk, jm * 128:(jm + 1) * 128], Bpad[:, 4 * kt + jm, :], ident_f[:], [rBp, rc], [RB[bk]])
                    CP("act", Bm_sb[:, kt, ri, :], ps[:, bk, :], [RB[bk]], [rCs])
            DMA("sp", scrB[l], A_cs.bitcast(BF16), [rCs], [res("scr")])
            Craw = A_sn.rearrange("p (k q) -> p k q", q=512)
            Cm_sb = A_mg.bitcast(BF16).rearrange("p (r j c) -> p r j c", r=2, j=16)
            MEMSET("pool", A_mg, 0.0, [rMg])
            for ri, nm in enumerate(("c_re", "c_im")):
                DMA("sp", Craw[:, :, 0:64], D[nm][l].rearrange("(k r) q -> r k q", r=128), (), [rSn])
                for g2 in range(2):
                    TS("dve", Craw[:, :, 128 + 64 * g2:128 + 64 * g2 + 64], Craw[:, :, 0:64], par[:, g2:g2 + 1], None,
                       ALU.mult, None, [rSn, rc], [rSn])
                bk = bank()
                for kt in range(4):
                    TR(ps[:, bk, kt * 128:(kt + 1) * 128], Craw[:, kt, 128:256], ident_f[:], [rSn, rc], [RB[bk]])
                for kt in range(4):
                    for jm in range(4):
                        src = ps[:, bk, kt * 128 + 32 * jm:kt * 128 + 32 * jm + 32]
                        dst = Cm_sb[:, ri, 4 * kt + jm, 32 * jm:32 * jm + 32]
                        if ri == 0:
                            CP("act", dst, src, [RB[bk]], [rMg])
                        else:
                            P.op("act", lambda e, o=dst, i=src: e.mul(out=o, in_=i, mul=-1.0), [RB[bk]], [rMg])
            DMA("sp", scrC[l], A_mg.bitcast(BF16), [rMg], [res("scr")])
        P.barrier()

        def load_w(dst_slot, src_ap, kt_n, ncols, col_off=0, slot_cols=None):
            sc = slot_cols if slot_cols is not None else ncols
            dst = wsl[dst_slot][:, 0:kt_n * sc].rearrange("p (k c) -> p k c", c=sc)[:, :, col_off:col_off + ncols]
            DMA("pool", dst, src_ap.rearrange("(k p) c -> p k c", p=128), (), [RW[dst_slot]])

        def next_slot():
            s = st["wsl"]; st["wsl"] = (s + 1) % 3
            return s

        def wview(slot, kt_n, sc):
            return wsl[slot][:, 0:kt_n * sc].rearrange("p (k c) -> p k c", c=sc)

        def rmsnorm_stats(TTL, rx, gcol):
            out = []
            for ti, (t0, n) in enumerate(TTL):
                ACT(sqb[:, :, :n], x[:, :, t0:t0 + n], AF.Square, [rx[ti]], [res("sqb")])
                bk = bank()
                for kt in range(8):
                    MM(ps[:, bk, :n], ones1024[:], sqb[:, kt, :n], kt == 0, kt == 7, [res("sqb"), rc], [RB[bk]])
                t = tmp()
                ACT(tmpf[t][:, :n], ps[:, bk, :n], AF.Sqrt, [RB[bk], rc], [RT[t]], bias=epsc[:, 0:1])
                RECIP(tmpf[t][:, :n], tmpf[t][:, :n], [RT[t]], [RT[t]])
                out.append(t)
            return out

        def rms_apply(TTL, rx, rxn, gsel, ti, t):
            t0, n = TTL[ti]
            for kt in range(8):
                STT("dve", xn[:, kt, t0:t0 + n], x[:, kt, t0:t0 + n], gsel(kt), tmpf[t][:, :n],
                    ALU.mult, ALU.mult, [rx[ti], RT[t], rp], [rxn[ti]])

        def rmsnorm_to_xn(TTL, rx, rxn, gsel, tis=None):
            for ti, (t0, n) in enumerate(TTL):
                if tis is not None and ti not in tis:
                    continue
                (t,) = rmsnorm_stats([TTL[ti]], [rx[ti]], gsel)
                rms_apply(TTL, rx, rxn, gsel, ti, t)

        WS = dict(q=[])

        def ws_add(fn):
            WS["q"].append([fn, None])
            return len(WS["q"]) - 1

        def ws_use(i, la=2):
            for k in range(i, min(i + 1 + la, len(WS["q"]))):
                if WS["q"][k][1] is None:
                    sl = next_slot()
                    WS["q"][k][0](sl)
                    WS["q"][k][1] = sl
            return WS["q"][i][1]

        def decl_dense(wsrc, kt_n, col0, ncols, grp):
            ids = []
            for g0 in range(0, ncols, grp):
                ids.append(ws_add(lambda sl, a=wsrc[:, col0 + g0:col0 + g0 + grp], k=kt_n, g=grp: load_w(sl, a, k, g)))
            return ids

        def dense(gids, kt_n, grp, src_fn, rsrc, TTL, consumer, tis=None, mid_hook=None, la=2, hooks=None):
            if tis is None:
                tis = list(range(len(TTL)))
            for gi, gid in enumerate(gids):
                s = ws_use(gid, la)
                wv = wview(s, kt_n, grp)
                for nt in range(grp // 128):
                    for ti in tis:
                        t0, n = TTL[ti]
                        bk = bank()
                        for kt in range(kt_n):
                            MM(ps[:, bk, :n], wv[:, kt, nt * 128:(nt + 1) * 128], src_fn(kt, t0, n),
                               kt == 0, kt == kt_n - 1, [RW[s]] + rsrc[ti], [RB[bk]])
                        consumer(gi * (grp // 128) + nt, ti, t0, n, bk)
                    if hooks is not None and (gi, nt) in hooks:
                        hooks[(gi, nt)]()
                if gi == 0 and mid_hook is not None:
                    mid_hook()

        def ffin_load(l, g):
            def fn(sl):
                load_w(sl, D["w_ff_in"][l][:, 256 * g:256 * g + 256], 8, 256, col_off=0, slot_cols=512)
                load_w(sl, D["w_ff_in"][l][:, 2816 + 256 * g:2816 + 256 * g + 256], 8, 256, col_off=256, slot_cols=512)
            return fn

        WD = {}
        for half_ in range(n_halves):
            for l_ in range(n_layers):
                w = {}
                w["u"] = decl_dense(D["w_in"][l_], 8, 0, 512, 512)
                w["cv"] = decl_dense(D["w_in"][l_], 8, 512, 512, 512)
                w["cg"] = decl_dense(D["w_in"][l_], 8, 1024, 512, 512)
                w["glu1"] = decl_dense(D["w_glu"][l_], 4, 0, 1024, 1024)
                w["glu2"] = decl_dense(D["w_glu"][l_], 4, 1024, 1024, 1024)
                w["gate"] = [None] * 4
                w["gate"][0] = decl_dense(D["w_in"][l_], 8, 1536, 512, 512)
                w["gate"][1] = decl_dense(D["w_in"][l_], 8, 2048, 512, 512)
                w["pw"] = decl_dense(D["w_pw"][l_], 4, 0, 1024, 1024)
                w["gate"][2] = decl_dense(D["w_in"][l_], 8, 2560, 512, 512)
                w["gate"][3] = decl_dense(D["w_in"][l_], 8, 3072, 512, 512)
                w["outA"] = decl_dense(D["w_out"][l_], 8, 0, 1024, 512)
                w["outB"] = decl_dense(D["w_out"][l_], 8, 0, 1024, 512)
                w["ffin"] = [ws_add(ffin_load(l_, g)) for g in range(11)]
                w["ffoutA"] = decl_dense(D["w_ff_out"][l_], 22, 0, 1024, 128)
                w["ffoutB"] = decl_dense(D["w_ff_out"][l_], 22, 0, 1024, 128)
                WD[(half_, l_)] = w

        for half in range(n_halves):
            TTL = [(0, 512), (512, 512)] + ([(1024, 16)] if half == 0 else [])
            NTT = len(TTL)
            rx = [res(f"x{ti}") for ti in range(NTT)]
            rxn = [res(f"xn{ti}") for ti in range(NTT)]
            ru = [[res(f"u{c}_{k}") for k in range(4)] for c in range(9)]
            rcb = [res(f"cb{i}") for i in range(3)]
            rya = [res(f"ya{ti}") for ti in range(NTT)]
            rar = [res(f"ar{ti}") for ti in range(NTT)]
            rar2 = [res(f"ar2_{ti}") for ti in range(NTT)]

            def ru_tile(ti, kt=None):
                cks = [8] if ti == 2 else list(range(4 * ti, 4 * ti + 4))
                if kt is None:
                    return [ru[c][k] for c in cks for k in range(4)]
                return [ru[c][kt] for c in cks]

            if half > 0:
                load_x(half, rx)
            if KSTOP == 4:
                P.finish(); P.emit(ctx); return nc
            for l in range(n_layers):
                P.epoch += 1
                W = WD[(half, l)]
                Tb = [carve(4096 * i, [4, 512], BF16) for i in range(2)]
                Mb = [carve(8192 + 4096 * i, [4, 512], BF16) for i in range(2)]
                Rre = carve(16640, [2048], F32); Rim = carve(24832, [2048], F32)
                Pre = carve(33024, [16, 128], F32); Pim = carve(41216, [16, 128], F32)
                Bm = carve(49408, [4, 2, 512], BF16)
                Cm = carve(57600, [2, 16, 128], BF16)
                rTb = [res("Tb0"), res("Tb1")]; rMb = [res("Mb0"), res("Mb1")]
                rtab = res("s5tab")
                tab_w = [rtab, res("diag"), res("lnst"), res("bufT"), res("xo")] + \
                        [res(f"ar{i}") for i in range(3)] + [res(f"ar2_{i}") for i in range(3)]
                DMA("sp", Rre, scrR[l, 0], [res("scr")], tab_w)
                DMA("sp", Rim, scrR[l, 1], [res("scr")], [rtab])
                DMA("sp", Pre, scrP[l, 0].rearrange("p (j s) -> p j s", s=128), [res("scr")], [rtab])
                DMA("sp", Pim, scrP[l, 1].rearrange("p (j s) -> p j s", s=128), [res("scr")], [rtab])
                DMA("sp", Bm, scrB[l].rearrange("p (k r c) -> p k r c", k=4, r=2), [res("scr")], [rtab])
                DMA("sp", Cm, scrC[l].rearrange("p (r j c) -> p r j c", r=2, j=16), [res("scr")], [rtab])
                gmix_sel = lambda kt, l=l: gmix[:, kt, l:l + 1]
                if l == 0:
                    rmsnorm_to_xn(TTL, rx, rxn, gmix_sel, tis=[0])
                cv32 = carve(0, [4, NTM], F32)

                def cons_u(ntl, ti, t0, n, bk):
                    CP("act", u[:, ntl, t0:t0 + n], ps[:, bk, :n], [RB[bk]], ru_tile(ti, ntl))

                def cons_cv(ntl, ti, t0, n, bk):
                    CP("act", cv32[:, ntl, t0:t0 + n], ps[:, bk, :n], [RB[bk]], [rar[ti]])

                def cons_cg(ntl, ti, t0, n, bk, l=l, half=half):
                    t = tmp()
                    ACT(tmpf[t][:, :n], ps[:, bk, :n], AF.Sigmoid, [RB[bk]], [RT[t]])
                    TT("dve", cv32[:, ntl, t0:t0 + n], cv32[:, ntl, t0:t0 + n], tmpf[t][:, :n], ALU.mult,
                       [rar[ti], RT[t]], [rar[ti]])
                    if ti < 2:
                        CP("act", cbuf[:, ntl, 30 + t0:30 + t0 + n], cv32[:, ntl, t0:t0 + n], [rar[ti]], [rcb[1 + ti]])
                        if ti == 1:
                            CP("dve", ctail32[:, ntl, :], cv32[:, ntl, 994:1024], [rar[ti]], [res("ctail32")])
                    else:
                        CP("dve", cnew32[:, ntl, :], cv32[:, ntl, t0:t0 + n], [rar[ti]], [res("cnew32")])

                xsrc = lambda kt, t0, n: xn[:, kt, t0:t0 + n]
                rxn_l = [[r] for r in rxn]
                CP("pool", cbuf[:, :, 0:30], ctail[:, l, :, :], [res("ctail")], [rcb[0]])
                tis0 = [0]
                tis1 = list(range(1, NTT))
                dense(W["u"], 8, 512, xsrc, rxn_l, TTL, cons_u, tis=tis0, la=2)
                rmsnorm_to_xn(TTL, rx, rxn, gmix_sel, tis=tis1)
                dense(W["cv"], 8, 512, xsrc, rxn_l, TTL, cons_cv, tis=tis0, la=1)
                dense(W["cg"], 8, 512, xsrc, rxn_l, TTL, cons_cg, tis=tis0, la=0)
                dense(W["u"], 8, 512, xsrc, rxn_l, TTL, cons_u, tis=tis1)
                dense(W["cv"], 8, 512, xsrc, rxn_l, TTL, cons_cv, tis=tis1)
                dense(W["cg"], 8, 512, xsrc, rxn_l, TTL, cons_cg, tis=tis1)
                if half == 0:
                    DMA("sp", D["cvs"][l, :, 0:29, :], D["scv"][l].rearrange("(b k) c -> b k c", k=30)[:, 1:30, :],
                        (), [res("cvs")], is_out=True)
                    t = tmp()
                    bk = bank()
                    for ct in range(4):
                        TR(ps[:16, bk, ct * 128:(ct + 1) * 128], cnew32[:, ct, :], ident_f[:], [res("cnew32"), rc], [RB[bk]])
                    CP("dve", tmpf[t][:16, :512], ps[:16, bk, :], [RB[bk]], [RT[t]])
                    DMA("sp", D["cvs"][l, :, 29, :], tmpf[t][:16, :512], [RT[t]], [res("cvs")], is_out=True)
                if half == n_halves - 1:
                    t = tmp()
                    bk = bank()
                    for ct in range(4):
                        TR(ps[:30, bk, ct * 128:(ct + 1) * 128], ctail32[:, ct, :], ident_f[:], [res("ctail32"), rc], [RB[bk]])
                    CP("dve", tmpf[t][:30, :512], ps[:30, bk, :], [RB[bk]], [RT[t]])
                    DMA("sp", D["cvp"][l], tmpf[t][:30, :512], [RT[t]], [res("cvp")], is_out=True)
                P.barrier()

                rH = res("Hst"); rcar = res("carry"); rgc = res("gcol")
                lre = sm["lre"][:, l, :]; lim = sm["lim"][:, l, :]
                gcol = hsm[1]

                def cmul_small(o_re, o_im, a_re_, a_im_, b_re_, b_im_, reads, writes):
                    t1 = hsm[0][:, 0, :]; t2 = hsm[0][:, 1, :]
                    TT("pool", t1, a_re_, b_re_, ALU.mult, reads, [res("hsm")])
                    TT("pool", t2, a_im_, b_im_, ALU.mult, reads, [res("hsm")])
                    TT("pool", o_re, t1, t2, ALU.subtract, [res("hsm")], writes)
                    TT("pool", t1, a_re_, b_im_, ALU.mult, reads, [res("hsm")])
                    TT("pool", t2, a_im_, b_re_, ALU.mult, reads, [res("hsm")])
                    TT("pool", o_im, t1, t2, ALU.add, [res("hsm")], writes)

                if half == 0:
                    MEMSET("pool", Hst[:, l, :, :], 0.0, [rH])
                v3 = lambda ap: ap.rearrange("p (j s) -> p j s", s=128)
                L128 = hsm[2]
                rL = res("L128")
                cmul_small(L128[:, 0, :], L128[:, 1, :], lre, lim, Pre[:, :, 127], Pim[:, :, 127], [rs, rtab], [rL])
                cmul_small(carry[:, 0, :], carry[:, 1, :], lre, lim, Hst[:, l, 0, :], Hst[:, l, 1, :], [rs, rH], [rcar])
                items = [(ck, kt) for ck in range(8) for kt in range(4)]
                Mviews = [[v3(Mb[p_][:, i, :]) for i in range(4)] for p_ in range(2)]

                def stage_A(it):
                    ck, kt = items[it]; pp = it % 2; c0 = ck * 128
                    T = Tb[pp]
                    ruk = [ru[ck][kt]]
                    bre, bim = 0, 1
                    MM(ps[:, bre, :], u[:, kt, c0:c0 + 128], Bm[:, kt, 0, :], True, True, ruk + [rtab], [RB[bre]])
                    MM(ps[:, bim, :], u[:, kt, c0:c0 + 128], Bm[:, kt, 1, :], True, True, ruk + [rtab], [RB[bim]])
                    cs = slice(kt * 512, (kt + 1) * 512)
                    TT("dve", T[:, 0, :], ps[:, bre, :], Rre[:, cs], ALU.mult, [RB[bre], rtab], [rTb[pp]])
                    TT("dve", T[:, 2, :], ps[:, bre, :], Rim[:, cs], ALU.mult, [RB[bre], rtab], [rTb[pp]])
                    STT("dve", T[:, 1, :], ps[:, bim, :], -1.0, Rim[:, cs], ALU.mult, ALU.mult, [RB[bim], rtab], [rTb[pp]])
                    TT("dve", T[:, 3, :], ps[:, bim, :], Rre[:, cs], ALU.mult, [RB[bim], rtab], [rTb[pp]])

                GT = {}

                def stage_B1(it):
                    ck, kt = items[it]; pp = it % 2
                    T = Tb[pp]
                    gre, gim = (2, 3) if pp == 0 else (4, 5)
                    for jm in range(4):
                        MM(ps[:, gre, jm * 128:(jm + 1) * 128], T[:, 0, jm * 128:(jm + 1) * 128], tri_b[:],
                           True, False, [rTb[pp], rc], [RB[gre]])
                        MM(ps[:, gre, jm * 128:(jm + 1) * 128], T[:, 1, jm * 128:(jm + 1) * 128], tri_b[:],
                           False, True, [rTb[pp], rc], [RB[gre]])
                    for jm in range(4):
                        MM(ps[:, gim, jm * 128:(jm + 1) * 128], T[:, 2, jm * 128:(jm + 1) * 128], tri_b[:],
                           True, False, [rTb[pp], rc], [RB[gim]])
                        MM(ps[:, gim, jm * 128:(jm + 1) * 128], T[:, 3, jm * 128:(jm + 1) * 128], tri_b[:],
                           False, True, [rTb[pp], rc], [RB[gim]])
                    ta = tmp()
                    GT[it] = ta
                    gr = v3(tmpf[ta][:, 0:512]); gi = v3(tmpf[ta][:, 512:1024])
                    js = slice(4 * kt, 4 * kt + 4)
                    for jm in range(4):
                        j = 4 * kt + jm
                        ACT(gr[:, jm, :], ps[:, gre, jm * 128:(jm + 1) * 128], AF.Identity, [RB[gre], rcar], [RT[ta]],
                            bias=carry[:, 0, j:j + 1])
                    for jm in range(4):
                        j = 4 * kt + jm
                        ACT(gi[:, jm, :], ps[:, gim, jm * 128:(jm + 1) * 128], AF.Identity, [RB[gim], rcar], [RT[ta]],
                            bias=carry[:, 1, j:j + 1])
                    CP("act", gcol[:, 0, js], gr[:, :, 127], [RT[ta]], [rgc])
                    CP("act", gcol[:, 1, js], gi[:, :, 127], [RT[ta]], [rgc])
                    if kt == 3:
                        if ck < 7:
                            t1 = hsm[0][:, 0, :]; t2 = hsm[0][:, 1, :]; t3 = hsm[3][:, 0, :]; t4 = hsm[3][:, 1, :]
                            rh_ = res("hsm")
                            TT("pool", t1, L128[:, 0, :], gcol[:, 0, :], ALU.mult, [rL, rgc], [rh_])
                            TT("pool", t2, L128[:, 1, :], gcol[:, 1, :], ALU.mult, [rL, rgc], [rh_])
                            TT("pool", t3, L128[:, 0, :], gcol[:, 1, :], ALU.mult, [rL, rgc], [rh_])
                            TT("pool", t4, L128[:, 1, :], gcol[:, 0, :], ALU.mult, [rL, rgc], [rh_])
                            TT("pool", carry[:, 0, :], t1, t2, ALU.subtract, [rh_], [rcar])
                            TT("pool", carry[:, 1, :], t3, t4, ALU.add, [rh_], [rcar])
                        else:
                            cmul_small(Hst[:, l, 0, :], Hst[:, l, 1, :], gcol[:, 0, :], gcol[:, 1, :],
                                       Pre[:, :, 127], Pim[:, :, 127], [rgc, rtab], [rH])

                def stage_B2(it):
                    ck, kt = items[it]; pp = it % 2
                    Mv = Mviews[pp]
                    ta = GT.pop(it)
                    gr = v3(tmpf[ta][:, 0:512]); gi = v3(tmpf[ta][:, 512:1024])
                    js = slice(4 * kt, 4 * kt + 4)
                    TT("pool", Mv[0], gr, Pre[:, js, :], ALU.mult, [RT[ta], rtab], [rMb[pp]])
                    STT("dve", Mv[1], gi, -1.0, Pim[:, js, :], ALU.mult, ALU.mult, [RT[ta], rtab], [rMb[pp]])
                    TT("pool", Mv[2], gr, Pim[:, js, :], ALU.mult, [RT[ta], rtab], [rMb[pp]])
                    TT("pool", Mv[3], gi, Pre[:, js, :], ALU.mult, [RT[ta], rtab], [rMb[pp]])

                def stage_C(it):
                    ck, kt = items[it]; pp = it % 2; c0 = ck * 128
                    Mv = Mviews[pp]
                    ruk = [ru[ck][kt]]
                    bk = 6 + pp
                    for jm in range(4):
                        j = 4 * kt + jm
                        MM(ps[:, bk, :128], Cm[:, 0, j, :], Mv[0][:, jm, :], jm == 0, False, [rtab, rMb[pp]], [RB[bk]])
                        MM(ps[:, bk, :128], Cm[:, 0, j, :], Mv[1][:, jm, :], False, False, [rtab, rMb[pp]], [RB[bk]])
                        MM(ps[:, bk, :128], Cm[:, 1, j, :], Mv[2][:, jm, :], False, False, [rtab, rMb[pp]], [RB[bk]])
                        MM(ps[:, bk, :128], Cm[:, 1, j, :], Mv[3][:, jm, :], False, jm == 3, [rtab, rMb[pp]], [RB[bk]])
                    t = tmp()
                    STT("dve", tmpf[t][:, :128], u[:, kt, c0:c0 + 128], dpar[:, kt, l:l + 1], ps[:, bk, :128],
                        ALU.mult, ALU.add, ruk + [rp, RB[bk]], [RT[t]])
                    ACT(u[:, kt, c0:c0 + 128], tmpf[t][:, :128], AF.Gelu_apprx_tanh, [RT[t]], ruk)

                segs = []
                if half == 0:
                    stgA = ya[:, 3:5, :].rearrange("p a b -> p (a b)").bitcast(F32)
                    sS = ya[:, 5:7, :].rearrange("p a b -> p (a b)").bitcast(F32)
                    sT = ya[:, 7, :].bitcast(F32)
                    rstg = res("s_stg"); rsS = res("s_sS"); rsT = res("s_sT")
                    rhsp = res("hsp"); rhsn = res("hsn"); rhsb = res("hsb"); rbus = res("sqb")
                    sbk = [0]

                    def sbank():
                        sbk[0] ^= 1
                        return 6 + sbk[0]

                    def seg_load(ri, nm, h2):
                        def f():
                            DMA("sp", stgA[:16, 0:1024], D[nm][l][:, h2 * 1024:(h2 + 1) * 1024], (), [rstg])
                            bk = sbank()
                            for b_ in range(8):
                                TR(ps[:, bk, b_ * 16:(b_ + 1) * 16], stgA[:16, b_ * 128:(b_ + 1) * 128], ident_f[:16, :16],
                                   [rstg, rc], [RB[bk]])
                            CP("act", hsp[:, ri, h2 * 8:(h2 + 1) * 8, :],
                               ps[:, bk, 0:128].rearrange("p (j s) -> p j s", s=16), [RB[bk]], [rhsp])
                        return f

                    for ri_, nm_ in enumerate(("sre", "sim")):
                        for h2_ in range(2):
                            segs.append(seg_load(ri_, nm_, h2_))

                    def seg_bu(kt):
                        def f():
                            for ri in range(2):
                                bk = sbank()
                                MM(ps[:16, bk, :], u[:, kt, 1024:1040], Bm[:, kt, ri, :], True, True, ru[8] + [rtab], [RB[bk]])
                                CP("act", bus[:, ri, kt * 512:(kt + 1) * 512], ps[:16, bk, :], [RB[bk]], [rbus])
                        return f

                    for kt_ in range(4):
                        segs.append(seg_bu(kt_))

                    def seg_step():
                        bk = sbank()
                        for ri in range(2):
                            for j in range(16):
                                MM(ps[:, bk, ri * 256 + j * 16:ri * 256 + (j + 1) * 16], bus[:, ri, j * 128:(j + 1) * 128],
                                   ident_b[:16, :16], True, True, [rbus, rc], [RB[bk]])
                        lre_bc = lre.unsqueeze(2).to_broadcast([128, 16, 16])
                        lim_bc = lim.unsqueeze(2).to_broadcast([128, 16, 16])
                        v16 = lambda ap: ap.rearrange("p (j s) -> p j s", s=16)
                        s1 = v16(sS[:, 0:256]); s2 = v16(sS[:, 256:512]); s3 = v16(sS[:, 512:768]); s4 = v16(sS[:, 768:1024])
                        TT("dve", s1, hsp[:, 0, :, :], lre_bc, ALU.mult, [rhsp, rs], [rsS])
                        TT("dve", s2, hsp[:, 1, :, :], lim_bc, ALU.mult, [rhsp, rs], [rsS])
                        TT("dve", s3, hsp[:, 0, :, :], lim_bc, ALU.mult, [rhsp, rs], [rsS])
                        TT("dve", s4, hsp[:, 1, :, :], lre_bc, ALU.mult, [rhsp, rs], [rsS])
                        TT("dve", s1, s1, s2, ALU.subtract, [rsS], [rsS])
                        TT("dve", s3, s3, s4, ALU.add, [rsS], [rsS])
                        TT("dve", hsn[:, 0, :, :], s1, v16(ps[:, bk, 0:256]), ALU.add, [rsS, RB[bk]], [rhsn])
                        TT("dve", hsn[:, 1, :, :], s3, v16(ps[:, bk, 256:512]), ALU.add, [rsS, RB[bk]], [rhsn])
                        CP("act", hsb, hsn, [rhsn], [rhsb])

                    segs.append(seg_step)

                    def seg_y(kt):
                        def f():
                            bk = sbank()
                            for jm in range(4):
                                j = 4 * kt + jm
                                MM(ps[:, bk, :16], Cm[:, 0, j, :], hsb[:, 0, j, :], jm == 0, False, [rtab, rhsb], [RB[bk]])
                                MM(ps[:, bk, :16], Cm[:, 1, j, :], hsb[:, 1, j, :], False, jm == 3, [rtab, rhsb], [RB[bk]])
                            STT("dve", sT[:, kt * 16:(kt + 1) * 16], u[:, kt, 1024:1040], dpar[:, kt, l:l + 1], ps[:, bk, :16],
                                ALU.mult, ALU.add, [ru[8][kt], rp, RB[bk]], [rsT])
                            ACT(u[:, kt, 1024:1040], sT[:, kt * 16:(kt + 1) * 16], AF.Gelu_apprx_tanh, [rsT], [ru[8][kt]])
                        return f

                    for kt_ in range(4):
                        segs.append(seg_y(kt_))

                    def seg_out(ri, nm, h2):
                        def f():
                            for q in range(2):
                                bk = sbank()
                                for jm in range(4):
                                    j = h2 * 8 + q * 4 + jm
                                    TR(ps[:16, bk, jm * 128:(jm + 1) * 128], hsn[:, ri, j, :], ident_f[:], [rhsn, rc], [RB[bk]])
                                CP("act", stgA[:16, q * 512:(q + 1) * 512], ps[:16, bk, :], [RB[bk]], [rstg])
                            DMA("sp", D[nm][l][:, h2 * 1024:(h2 + 1) * 1024], stgA[:16, 0:1024], [rstg], [res(nm)], is_out=True)
                        return f

                    for ri_, nm_ in enumerate(("res", "ims")):
                        for h2_ in range(2):
                            segs.append(seg_out(ri_, nm_, h2_))

                NI = len(items)
                for step in range(NI + 3):
                    if step < NI:
                        stage_A(step)
                    if 0 <= step - 2 < NI:
                        stage_B2(step - 2)
                    if 0 <= step - 1 < NI:
                        stage_B1(step - 1)
                    if 0 <= step - 3 < NI:
                        stage_C(step - 3)
                    if segs and step >= 3 and step % 2 == 1:
                        segs.pop(0)()
                while segs:
                    segs.pop(0)()
                if half == n_halves - 1:
                    for ri, nm in enumerate(("rep", "imp")):
                        bk = bank(); t = tmp()
                        TR(ps[:16, bk, :128], Hst[:, l, ri, :], ident_f[:], [rH, rc], [RB[bk]])
                        CP("dve", tmpf[t][:16, :128], ps[:16, bk, :128], [RB[bk]], [RT[t]])
                        DMA("sp", D[nm][l], tmpf[t][:16, :128], [RT[t]], [res(nm)], is_out=True)

                diag = carve(0, [4, 31, 128], BF16)
                conv32 = carve(31744, [4, NTM], F32)
                cbf = carve(48384, [4, 512], BF16)
                csq = carve(52480, [4, 512], BF16)
                bufT = carve(56576, [4, 16, 30], F32) if half == 0 else None
                rdiag = res("diag")
                s5_alias = [res("Tb0"), res("Tb1"), res("Mb0"), res("Mb1"), res("s5tab")]
                ya_alias = [res(n_) for n_ in ("hsp", "hsn", "hsb", "s_stg", "s_sS", "s_sT")] if half == 0 else []
                idb_bc = ident_b[:].unsqueeze(1).to_broadcast([128, 31, 128])
                for ct in range(4):
                    TT("pool", diag[:, ct, :, :], idb_bc,
                       convw[:, l, ct, :].unsqueeze(2).to_broadcast([128, 31, 128]), ALU.mult, [rc, rp],
                       [rdiag] + (s5_alias if ct == 0 else []))
                usrc = lambda kt, t0, n: u[:, kt, t0:t0 + n]
                ru_l = [ru_tile(ti) for ti in range(NTT)]

                def cons_g1(ntl, ti, t0, n, bk):
                    CP("act", ya[:, ntl, t0:t0 + n], ps[:, bk, :n], [RB[bk]], [rya[ti]] + ya_alias)

                def cons_g2(ntl, ti, t0, n, bk):
                    t = tmp()
                    ACT(tmpf[t][:, :n], ps[:, bk, :n], AF.Sigmoid, [RB[bk]], [RT[t]])
                    TT("dve", ya[:, ntl, t0:t0 + n], ya[:, ntl, t0:t0 + n], tmpf[t][:, :n], ALU.mult,
                       [rya[ti], RT[t]], [rya[ti]])

                dense(W["glu1"], 4, 1024, usrc, ru_l, TTL, cons_g1)
                dense(W["glu2"], 4, 1024, usrc, ru_l, TTL, cons_g2)
                if half == 0:
                    CP("dve", ctail[:, l, :, :], cbuf[:, :, 1024:1054], [rcb[2]], [res("ctail")])
                    for rg in range(4):
                        t = tmp()
                        DMA("sp", tmpf[t][:120, :512], D["scv"][l][rg * 120:(rg + 1) * 120, :], (), [RT[t]])
                        bk = bank()
                        for ct in range(4):
                            TR(ps[:, bk, ct * 120:(ct + 1) * 120], tmpf[t][:120, ct * 128:(ct + 1) * 128],
                               ident_f[:120, :120], [RT[t], rc], [RB[bk]])
                        CP("dve", bufT[:, :, 4 * rg:4 * rg + 4, :],
                           ps[:, bk, :480].rearrange("p (c b k) -> p c b k", c=4, b=4), [RB[bk]],
                           [res("bufT")] + s5_alias)
                    for ct in range(4):
                        t = tmp()
                        pv = tmpf[t][:, 0:480].rearrange("p (b k) -> p b k", k=30)
                        TT("dve", pv, bufT[:, ct, :, :], convw[:, l, ct, 0:30].unsqueeze(1).to_broadcast([128, 16, 30]),
                           ALU.mult, [res("bufT"), rp], [RT[t]])
                        P.op("dve", lambda e, o=tmpf[t][:, 512:528], i=pv: e.tensor_reduce(
                            out=o, in_=i, op=ALU.add, axis=mybir.AxisListType.X), [RT[t]], [RT[t]])
                        STT("dve", tmpf[t][:, 528:544], cnew32[:, ct, :], convw[:, l, ct, 30:31], tmpf[t][:, 512:528],
                            ALU.mult, ALU.add, [res("cnew32"), rp, RT[t]], [RT[t]])
                        ACT(conv32[:, ct, 1024:1040], tmpf[t][:, 528:544], AF.Identity, [RT[t], rp], [rar[2]] + s5_alias,
                            bias=convb[:, ct, l:l + 1])


                cact = u

                def ln_pre(ti):
                    t0, n = TTL[ti]
                    rtmp = res("lnst")
                    ACT(cbf[:, :, :n], conv32[:, :, t0:t0 + n], AF.Copy, [rar[ti]], [rtmp])
                    ACT(csq[:, :, :n], conv32[:, :, t0:t0 + n], AF.Square, [rar[ti]], [rtmp])

                def ln_rest(ti):
                    t0, n = TTL[ti]
                    rtmp = res("lnst")
                    bm = bank()
                    for ct in range(4):
                        MM(ps[:, bm, :n], ones512[:], cbf[:, ct, :n], ct == 0, ct == 3, [rtmp, rc], [RB[bm]])
                    bq = bank()
                    for ct in range(4):
                        MM(ps[:, bq, :n], ones512[:], csq[:, ct, :n], ct == 0, ct == 3, [rtmp, rc], [RB[bq]])
                    tm, tv = tmp(), tmp()
                    CP("act", tmpf[tm][:, :n], ps[:, bm, :n], [RB[bm]], [RT[tm]])
                    TT("dve", tmpf[tv][:, :n], tmpf[tm][:, :n], tmpf[tm][:, :n], ALU.mult, [RT[tm]], [RT[tv]])
                    TT("dve", tmpf[tv][:, :n], ps[:, bq, :n], tmpf[tv][:, :n], ALU.subtract, [RB[bq], RT[tv]], [RT[tv]])
                    TS("dve", tmpf[tv][:, :n], tmpf[tv][:, :n], 0.0, None, ALU.max, None, [RT[tv]], [RT[tv]])
                    ACT(tmpf[tv][:, :n], tmpf[tv][:, :n], AF.Sqrt, [RT[tv], rc], [RT[tv]], bias=epsc[:, 0:1])
                    RECIP(tmpf[tv][:, :n], tmpf[tv][:, :n], [RT[tv]], [RT[tv]])
                    for ct in range(4):
                        tx = tm if ct % 2 == 0 else tv
                        xc = tmpf[tx][:, 512:512 + n]
                        TT("dve", xc, conv32[:, ct, t0:t0 + n], tmpf[tm][:, :n], ALU.subtract,
                           [rar[ti], RT[tm]], [RT[tx]])
                        TT("dve", xc, xc, tmpf[tv][:, :n], ALU.mult, [RT[tx], RT[tv]], [RT[tx]])
                        ACT(cact[:, ct, t0:t0 + n], xc, AF.Silu, [RT[tx], rp], ru_tile(ti, ct),
                            scale=lng[:, ct, l:l + 1], bias=lnb[:, ct, l:l + 1])

                for ti in range(2):
                    t0 = ti * 512
                    for ct in range(4):
                        bk = bank()
                        for k in range(31):
                            MM(ps[:, bk, :], diag[:, ct, k, :], cbuf[:, ct, t0 + k:t0 + k + 512], k == 0, k == 30,
                               [rdiag, rcb[ti], rcb[ti + 1]], [RB[bk]])
                        ACT(conv32[:, ct, t0:t0 + 512], ps[:, bk, :], AF.Identity, [RB[bk], rp], [rar[ti]],
                            bias=convb[:, ct, l:l + 1])
                        if ti == 1 and ct == 1:
                            ln_rest(0)
                    if ti == 0:
                        ln_pre(0)

                ybuf = carve(0, [8, NTM], BF16)
                csrc = lambda kt, t0, n: cact[:, kt, t0:t0 + n]

                def cons_yb(ntl, ti, t0, n, bk):
                    CP("act", ybuf[:, ntl, t0:t0 + n], ps[:, bk, :n], [RB[bk]], [rar2[ti], rdiag])

                def cons_gate1(g):
                    def f(ntl, ti, t0, n, bk):
                        t = tmp()
                        ACT(tmpf[t][:, :n], ps[:, bk, :n], AF.Sigmoid, [RB[bk]], [RT[t]])
                        TT("dve", ya[:, ntl + 4 * g, t0:t0 + n], ya[:, ntl + 4 * g, t0:t0 + n], tmpf[t][:, :n], ALU.mult,
                           [rya[ti], RT[t]], [rya[ti]])
                    return f

                def cons_gate2(g):
                    def f(ntl, ti, t0, n, bk):
                        t = tmp()
                        nt_ = ntl + 4 * g
                        ACT(tmpf[t][:, :n], ps[:, bk, :n], AF.Sigmoid, [RB[bk]], [RT[t]])
                        TT("dve", ybuf[:, nt_, t0:t0 + n], ybuf[:, nt_, t0:t0 + n], tmpf[t][:, :n], ALU.mult,
                           [rar2[ti], RT[t]], [rar2[ti]])
                        TT("dve", ya[:, nt_, t0:t0 + n], ya[:, nt_, t0:t0 + n], ybuf[:, nt_, t0:t0 + n], ALU.add,
                           [rya[ti], rar2[ti]], [rya[ti]])
                    return f

                ln_pre(1)
                dense(W["gate"][0], 8, 512, xsrc, rxn_l, TTL, cons_gate1(0))
                ln_rest(1)
                dense(W["gate"][1], 8, 512, xsrc, rxn_l, TTL, cons_gate1(1))
                if half == 0:
                    ln_pre(2)
                    ln_rest(2)
                dense(W["pw"], 4, 1024, csrc, ru_l, TTL, cons_yb, tis=[0, 1])
                if half == 0:
                    dense(W["pw"], 4, 1024, csrc, ru_l, TTL, cons_yb, tis=[2])
                for g in range(2):
                    dense(W["gate"][2 + g], 8, 512, xsrc, rxn_l, TTL, cons_gate2(g))
                ysrc = lambda kt, t0, n: ya[:, kt, t0:t0 + n]
                rya_l = [[r] for r in rya]

                def cons_res(ntl, ti, t0, n, bk):
                    TT("dve", x[:, ntl, t0:t0 + n], x[:, ntl, t0:t0 + n], ps[:, bk, :n], ALU.add, [rx[ti], RB[bk]], [rx[ti]])

                tisA = [0]
                tisB = list(range(1, NTT))
                gffn_sel = lambda kt, l=l: gffn[:, kt, l:l + 1]
                dense(W["outA"], 8, 512, ysrc, rya_l, TTL, cons_res, tis=tisA)
                hk = {}

                def hk_stats(sel, hk=hk):
                    (hk["t"],) = rmsnorm_stats([TTL[0]], [rx[0]], sel)

                dense(W["outB"], 8, 512, ysrc, rya_l, TTL, cons_res, tis=tisB,
                      hooks={(0, 1): (lambda: hk_stats(gffn_sel)),
                             (1, 0): (lambda: rms_apply(TTL, rx, rxn, gffn_sel, 0, hk["t"]))})

                actb = carve(0, [22, NTM], BF16)
                ffin_passes = [(0, [0], 1), ("norm", None, None), (1, [0], 1), (0, tisB, 2), (1, tisB, 2)] + \
                              [(g, list(range(NTT)), 2) for g in range(2, 11)]
                for g, tis_g, la_g in ffin_passes:
                    if g == "norm":
                        rmsnorm_to_xn(TTL, rx, rxn, gffn_sel, tis=tisB)
                        continue
                    s = ws_use(W["ffin"][g], la_g)
                    wv = wview(s, 8, 512)
                    for ti in tis_g:
                        t0, n = TTL[ti]
                        for jj in range(2):
                            j = 2 * g + jj
                            b1 = bank()
                            for kt in range(8):
                                MM(ps[:, b1, :n], wv[:, kt, jj * 128:(jj + 1) * 128], xn[:, kt, t0:t0 + n], kt == 0, kt == 7,
                                   [RW[s], rxn[ti]], [RB[b1]])
                            b2 = bank()
                            for kt in range(8):
                                MM(ps[:, b2, :n], wv[:, kt, 256 + jj * 128:256 + (jj + 1) * 128], xn[:, kt, t0:t0 + n],
                                   kt == 0, kt == 7, [RW[s], rxn[ti]], [RB[b2]])
                            t = tmp()
                            ACT(tmpf[t][:, :n], ps[:, b1, :n], AF.Silu, [RB[b1]], [RT[t]])
                            TT("dve", actb[:, j, t0:t0 + n], tmpf[t][:, :n], ps[:, b2, :n], ALU.mult,
                               [RT[t], RB[b2]], [rar[ti], rar2[ti]])
                asrc = lambda kt, t0, n: actb[:, kt, t0:t0 + n]
                rar_l = [[r] for r in rar]
                dense(W["ffoutA"], 22, 128, asrc, rar_l, TTL, cons_res, tis=tisA)
                nxt_hooks = None
                if l + 1 < n_layers:
                    gnext = lambda kt, l=l: gmix[:, kt, l + 1:l + 2]
                    hk2 = {}

                    def hk2_stats(hk2=hk2, gnext=gnext):
                        (hk2["t"],) = rmsnorm_stats([TTL[0]], [rx[0]], gnext)

                    nxt_hooks = {(1, 0): hk2_stats,
                                 (4, 0): (lambda hk2=hk2, gnext=gnext: rms_apply(TTL, rx, rxn, gnext, 0, hk2["t"]))}
                dense(W["ffoutB"], 22, 128, asrc, rar_l, TTL, cons_res, tis=tisB, hooks=nxt_hooks)

            P.epoch += 1
            xo = ya[:, :, :].rearrange("p a b -> p (a b)").bitcast(F32)[:, 0:4096].rearrange("p (k t) -> p k t", t=512)
            rxo = [res("xo")] + list(rya)
            for ti, (t0, n) in enumerate(TTL):
                (t,) = rmsnorm_stats([TTL[ti]], [rx[ti]], None)
                for kt in range(8):
                    STT("dve", xo[:, kt, :n], x[:, kt, t0:t0 + n], gfin[:, kt, 0:1], tmpf[t][:, :n],
                        ALU.mult, ALU.mult, [rx[ti], RT[t], rp], rxo)
                nb = (n + 127) // 128
                for b in range(nb):
                    rows = min(128, n - b * 128)
                    to = tmp()
                    for q in range(2):
                        bk = bank()
                        for k4 in range(4):
                            kt = 4 * q + k4
                            TR(ps[:rows, bk, k4 * 128:(k4 + 1) * 128], xo[:, kt, b * 128:b * 128 + rows], ident_f[:],
                               rxo + [rc], [RB[bk]])
                        CP("act" if q == 0 else "dve", tmpf[to][:rows, q * 512:(q + 1) * 512], ps[:rows, bk, :], [RB[bk]], [RT[to]])
                    if ti < 2:
                        r0 = half * 1024 + t0 + b * 128
                        DMA("sp", D["yp"][r0:r0 + 128, :], tmpf[to][:rows, :], [RT[to]], [res("yp")], is_out=True)
                    else:
                        DMA("sp", D["ys"], tmpf[to][:rows, :], [RT[to]], [res("ys")], is_out=True)

        P.finish()
        P.emit(ctx)
    return nc


_NC_CACHE = {}


def _consts():
    k = {}
    k["k_ident"] = np.eye(128, dtype=np.float32)
    k["k_tri"] = np.triu(np.ones((128, 128), dtype=np.float32))
    k["k_svec"] = np.tile(np.arange(128, dtype=np.float32)[None, :], (128, 1))
    k["k_pidx"] = np.arange(128, dtype=np.float32)[:, None].copy()
    par = np.zeros((128, 2), dtype=np.float32)
    gl = (np.arange(128) // 16) % 2
    par[:, 0] = (gl == 0)
    par[:, 1] = (gl == 1)
    k["k_par"] = par
    return k


def kernel(x_prompt, x_sample, state_ssm_re, state_ssm_im, state_conv,
           g_mix, w_in, ssm_a_re, ssm_a_im, ssm_log_dt, ssm_b_re, ssm_b_im,
           ssm_c_re, ssm_c_im, ssm_d, w_glu, conv_w, conv_b, conv_ln_g, conv_ln_b,
           w_pw, w_out, g_ffn, w_ff_in, w_ff_out, g_final):
    f = lambda a: np.ascontiguousarray(np.asarray(a, dtype=np.float32))
    if "nc" not in _NC_CACHE:
        _NC_CACHE["nc"] = build()
    nc = _NC_CACHE["nc"]
    shared = dict(
        g_mix=f(g_mix), w_in=f(w_in), a_re=f(ssm_a_re).reshape(4, 2048), a_im=f(ssm_a_im).reshape(4, 2048),
        log_dt=f(ssm_log_dt), b_re=f(ssm_b_re).reshape(4, 2048, 16), b_im=f(ssm_b_im).reshape(4, 2048, 16),
        c_re=f(ssm_c_re).reshape(4, 512, 64), c_im=f(ssm_c_im).reshape(4, 512, 64), ssm_d=f(ssm_d),
        w_glu=f(w_glu), conv_w=f(conv_w), conv_b=f(conv_b), ln_g=f(conv_ln_g), ln_b=f(conv_ln_b),
        w_pw=f(w_pw), w_out=f(w_out), g_ffn=f(g_ffn), w_ff_in=f(w_ff_in), w_ff_out=f(w_ff_out),
        g_final=f(g_final).reshape(1, 1024),
    )
    shared.update(_consts())
    xp = f(x_prompt); xs = f(x_sample).reshape(128, 1024)
    sre = f(state_ssm_re).reshape(4, 128, 2048); sim = f(state_ssm_im).reshape(4, 128, 2048)
    scv = f(state_conv).reshape(4, 128, 30, 512)
    in_maps = []
    for c in range(8):
        m = dict(shared)
        m["xp"] = xp[c]
        m["xs"] = np.ascontiguousarray(xs[16 * c:16 * c + 16])
        m["sre"] = np.ascontiguousarray(sre[:, 16 * c:16 * c + 16])
        m["sim"] = np.ascontiguousarray(sim[:, 16 * c:16 * c + 16])
        m["scv"] = np.ascontiguousarray(scv[:, 16 * c:16 * c + 16].reshape(4, 480, 512))
        in_maps.append(m)
    res = run_bass_kernel_spmd(nc, in_maps, core_ids=list(range(8)))
    r = res.results
    y_prompt = np.stack([r[c]["yp"] for c in range(8)], 0).astype(np.float32)
    y_sample = np.concatenate([r[c]["ys"] for c in range(8)], 0).reshape(128, 1, 1024).astype(np.float32)
    re_p = np.stack([r[c]["rep"].reshape(4, 32, 64) for c in range(8)], 1).astype(np.float32)
    im_p = np.stack([r[c]["imp"].reshape(4, 32, 64) for c in range(8)], 1).astype(np.float32)
    conv_p = np.stack([r[c]["cvp"] for c in range(8)], 1).astype(np.float32)
    re_s = np.concatenate([r[c]["res"].reshape(4, 16, 32, 64) for c in range(8)], 1).astype(np.float32)
    im_s = np.concatenate([r[c]["ims"].reshape(4, 16, 32, 64) for c in range(8)], 1).astype(np.float32)
    conv_s = np.concatenate([r[c]["cvs"] for c in range(8)], 1).astype(np.float32)
    return (y_prompt, y_sample, re_p, im_p, conv_p, re_s, im_s, conv_s)
```

```python
import math
import os
from contextlib import ExitStack
import numpy as np
import concourse.bass as bass
import concourse.mybir as mybir
from concourse.bass_utils import run_bass_kernel_spmd

F32 = mybir.dt.float32
BF16 = mybir.dt.bfloat16
ALU = mybir.AluOpType
AF = mybir.ActivationFunctionType

ENGS = ("pe", "act", "dve", "pool", "sp")
DMA_Q = ("sp", "act", "pool")
NDMA_SEM = 8

DEPTH = 4
EPS = 1e-6
TWO_PI = 2.0 * math.pi
C1 = 6.28125
C2 = TWO_PI - 6.28125
MAGIC = 12582912.0


class Res:
    __slots__ = ("name", "lw", "rd")

    def __init__(self, name):
        self.name = name
        self.lw = None
        self.rd = {}


class Prog:
    def __init__(self, nc):
        self.nc = nc
        self.ins = {e: [] for e in ENGS}
        self.ndma = {q: 0 for q in DMA_Q}
        self.dmas = []
        self.epoch = 0
        self.all_out_dmas = []
        self._dma_id_of = {}

    def _deps(self, reads, writes):
        deps = set()
        for r in reads:
            if r.lw is not None:
                deps.add(r.lw)
        for w in writes:
            if w.lw is not None:
                deps.add(w.lw)
            for k, v in w.rd.items():
                if isinstance(k, tuple):
                    deps.add(k)
                else:
                    deps.add((k, v))
        return deps

    def op(self, eng, fn, reads=(), writes=()):
        idx = len(self.ins[eng])
        deps = self._deps(reads, writes)
        self.ins[eng].append(dict(fn=fn, deps=deps, dma=None, epoch=self.epoch))
        for r in reads:
            r.rd[eng] = idx
        for w in writes:
            w.lw = (eng, idx)
            w.rd = {}
        return (eng, idx)

    def dma(self, q, fn, reads=(), writes=(), is_out=False):
        deps = self._deps(reads, writes)
        n = self.ndma[q]
        self.ndma[q] += 1
        did = len(self.dmas)
        self.dmas.append((q, n))
        if n >= NDMA_SEM:
            deps.add(("dma", self._dma_id_of[(q, n - NDMA_SEM)]))
        self._dma_id_of[(q, n)] = did
        self.ins[q].append(dict(fn=fn, deps=deps, dma=did, epoch=self.epoch))
        key = ("dma", did)
        for r in reads:
            r.rd[key] = True
        for w in writes:
            w.lw = key
            w.rd = {}
        if is_out:
            self.all_out_dmas.append(key)
        return key

    def barrier(self):
        deps = set()
        for e in ENGS:
            if self.ins[e]:
                deps.add((e, len(self.ins[e]) - 1))
        for q in DMA_Q:
            for n in range(max(0, self.ndma[q] - NDMA_SEM), self.ndma[q]):
                deps.add(("dma", self._dma_id_of[(q, n)]))
        for e in ENGS:
            self.ins[e].append(dict(fn=None, deps=set(deps), dma=None, epoch=self.epoch))

    def finish(self, eng="sp"):
        self.ins[eng].append(dict(fn=None, deps=set(self.all_out_dmas), dma=None, epoch=self.epoch))

    def emit(self, ctx):
        nc = self.nc
        signal = {e: set() for e in ENGS}
        for e in ENGS:
            for i, ins in enumerate(self.ins[e]):
                for d in ins["deps"]:
                    if d[0] != "dma":
                        if d[0] == "pe" and e == "pe":
                            continue
                        if d[0] == e and d[1] >= i:
                            continue
                        signal[d[0]].add(d[1])
        for e in ENGS:
            fixed = set()
            for i in signal[e]:
                j = i
                while j >= 0 and (self.ins[e][j]["fn"] is None or self.ins[e][j]["dma"] is not None):
                    j -= 1
                fixed.add((i, j))
            signal[e] = fixed
        rank = {}
        sigidx = {e: {} for e in ENGS}
        for e in ENGS:
            cnt = {}
            seen = {}
            for i, j in sorted(signal[e], key=lambda t: (t[1], t[0])):
                if j < 0:
                    rank[(e, i)] = None
                    continue
                ep = self.ins[e][j]["epoch"]
                if j not in seen:
                    cnt[ep] = cnt.get(ep, 0) + 1
                    seen[j] = (ep, cnt[ep])
                    sigidx[e][j] = ep
                rank[(e, i)] = seen[j]
        sems = {}
        for e in ENGS:
            for ep in sorted(set(sigidx[e].values())):
                sems[(e, ep)] = ctx.enter_context(nc.semaphore(f"s_{e}_{ep}"))
        dsem = {}
        for q in DMA_Q:
            for k in range(min(NDMA_SEM, self.ndma[q])):
                dsem[(q, k)] = ctx.enter_context(nc.semaphore(f"d_{q}_{k}"))
        prog = self

        def run_engine(e, eng):
            waited = {}
            dwaited = set()
            for i, ins in enumerate(prog.ins[e]):
                need = {}
                dneed = []
                for d in ins["deps"]:
                    if d[0] == "dma":
                        if d[1] not in dwaited:
                            dneed.append(d[1])
                    else:
                        te, ti = d
                        if te == "pe" and e == "pe":
                            continue
                        if te == e and ti >= i:
                            continue
                        if waited.get(te, -1) >= ti:
                            continue
                        need[te] = max(need.get(te, -1), ti)
                for te, ti in need.items():
                    rk = rank[(te, ti)]
                    if rk is not None:
                        eng.wait_ge(sems[(te, rk[0])], rk[1])
                    waited[te] = ti
                for did in sorted(dneed):
                    q, n = prog.dmas[did]
                    eng.wait_ge(dsem[(q, n % NDMA_SEM)], 16 * (n // NDMA_SEM + 1))
                    dwaited.add(did)
                if ins["fn"] is None:
                    continue
                r = ins["fn"](eng)
                if ins["dma"] is not None:
                    q, n = prog.dmas[ins["dma"]]
                    r.then_inc(dsem[(q, n % NDMA_SEM)], 16)
                elif i in sigidx[e]:
                    r.then_inc(sems[(e, sigidx[e][i])], 1)

        with nc.Block() as block:
            @block.tensor
            def _(eng):
                run_engine("pe", eng)

            @block.scalar
            def _(eng):
                run_engine("act", eng)

            @block.vector
            def _(eng):
                run_engine("dve", eng)

            @block.gpsimd
            def _(eng):
                run_engine("pool", eng)

            @block.sync
            def _(eng):
                run_engine("sp", eng)


def build(n_layers=DEPTH, n_halves=2):
    nc = bass.Bass("TRN2", target_bir_lowering=False)
    D = {}

    def inp(name, shape):
        D[name] = nc.dram_tensor(name, shape, F32, kind="ExternalInput").ap()

    def outp(name, shape):
        D[name] = nc.dram_tensor(name, shape, F32, kind="ExternalOutput").ap()

    inp("xp", [2048, 1024]); inp("xs", [16, 1024])
    inp("sre", [4, 16, 2048]); inp("sim", [4, 16, 2048]); inp("scv", [4, 480, 512])
    inp("g_mix", [4, 1024]); inp("w_in", [4, 1024, 3584])
    inp("a_re", [4, 2048]); inp("a_im", [4, 2048]); inp("log_dt", [4, 32])
    inp("b_re", [4, 2048, 16]); inp("b_im", [4, 2048, 16])
    inp("c_re", [4, 512, 64]); inp("c_im", [4, 512, 64]); inp("ssm_d", [4, 512])
    inp("w_glu", [4, 512, 2048]); inp("conv_w", [4, 31, 512]); inp("conv_b", [4, 512])
    inp("ln_g", [4, 512]); inp("ln_b", [4, 512]); inp("w_pw", [4, 512, 1024])
    inp("w_out", [4, 1024, 1024]); inp("g_ffn", [4, 1024]); inp("w_ff_in", [4, 1024, 5632])
    inp("w_ff_out", [4, 2816, 1024]); inp("g_final", [1, 1024])
    inp("k_ident", [128, 128]); inp("k_tri", [128, 128]); inp("k_svec", [128, 128])
    inp("k_pidx", [128, 1]); inp("k_par", [128, 2])
    outp("yp", [2048, 1024]); outp("ys", [16, 1024])
    outp("rep", [4, 16, 128]); outp("imp", [4, 16, 128]); outp("cvp", [4, 30, 512])
    outp("res", [4, 16, 2048]); outp("ims", [4, 16, 2048]); outp("cvs", [4, 16, 30, 512])
    scrB = nc.dram_tensor("scrB", [4, 128, 4096], BF16).ap()
    scrC = nc.dram_tensor("scrC", [4, 128, 4096], BF16).ap()
    scrR = nc.dram_tensor("scrR", [4, 2, 128, 2048], F32).ap()
    scrP = nc.dram_tensor("scrP", [4, 2, 128, 2048], F32).ap()

    with ExitStack() as ctx:
        def sb(name, shape, dt):
            return ctx.enter_context(nc.sbuf_tensor(name, shape, dt))

        P = Prog(nc)
        NTM = 1040
        x = sb("x", [128, 8, NTM], F32)
        xn = sb("xn", [128, 8, NTM], BF16)
        u = sb("u", [128, 4, NTM], BF16)
        cbuf = sb("cbuf", [128, 4, 1054], BF16)
        ya = sb("ya", [128, 8, NTM], BF16)
        arena = sb("arena", [128, 16448], F32)
        wsl = [sb(f"wsl{i}", [128, 4096], BF16) for i in range(3)]
        tmpf = [sb(f"tmpf{i}", [128, 1024], F32) for i in range(4)]
        sqb = sb("sqb", [128, 8, 512], BF16)
        ps = ctx.enter_context(nc.psum_tensor("ps", [128, 8, 512], F32))
        ident_f = sb("ident_f", [128, 128], F32)
        ident_b = sb("ident_b", [128, 128], BF16)
        tri_b = sb("tri_b", [128, 128], BF16)
        ones1024 = sb("ones1024", [128, 128], BF16)
        ones512 = sb("ones512", [128, 128], BF16)
        svec = sb("svec", [128, 128], F32)
        pidx = sb("pidx", [128, 1], F32)
        par = sb("par", [128, 2], F32)
        gmix = sb("gmix", [128, 8, 4], F32)
        gffn = sb("gffn", [128, 8, 4], F32)
        gfin = sb("gfin", [128, 8, 1], F32)
        dpar = sb("dpar", [128, 4, 4], F32)
        convb = sb("convb", [128, 4, 4], F32)
        lng = sb("lng", [128, 4, 4], F32)
        lnb = sb("lnb", [128, 4, 4], F32)
        convw = sb("convw", [128, 4, 4, 31], F32)
        are = sb("are", [128, 16, 4], F32)
        aim = sb("aim", [128, 16, 4], F32)
        ldt = sb("ldt", [128, 4, 16], F32)
        sm = {n: sb("sm_" + n, [128, 4, 16], F32) for n in
              ["dt", "dr", "th", "mag", "cs", "sn", "lre", "lim", "t1", "t2", "t3", "t4", "qre", "qim", "den"]}
        Hst = sb("Hst", [128, 4, 2, 16], F32)
        carry = sb("carry", [128, 2, 16], F32)
        hsm = [sb(f"hsm{i}", [128, 2, 16], F32) for i in range(5)]
        ctail = sb("ctail", [128, 4, 4, 30], BF16)
        cnew32 = sb("cnew32", [128, 4, 16], F32)
        ctail32 = sb("ctail32", [128, 4, 30], F32)
        hsp = ya[:, 0, 0:1024].bitcast(F32).rearrange("p (r j s) -> p r j s", r=2, j=16)
        hsn = ya[:, 1, 0:1024].bitcast(F32).rearrange("p (r j s) -> p r j s", r=2, j=16)
        hsb = ya[:, 2, 0:512].rearrange("p (r j s) -> p r j s", r=2, j=16)
        bus = sqb[:16, :, :].rearrange("p a b -> p (a b)").rearrange("p (r c) -> p r c", r=2)

        R = {}

        def res(name):
            if name not in R:
                R[name] = Res(name)
            return R[name]

        RB = [res(f"bank{i}") for i in range(8)]
        RW = [res(f"wsl{i}") for i in range(3)]
        RT = [res(f"tmpf{i}") for i in range(4)]
        RA = res("arena_generic")
        st = dict(bank=0, wsl=0, tmp=0)

        def bank():
            b = st["bank"]; st["bank"] = (b + 1) % 8
            return b

        def tmp():
            t = st["tmp"]; st["tmp"] = (t + 1) % 4
            return t

        def MM(out, lhsT, rhs, start, stop, reads, writes):
            P.op("pe", lambda e, o=out, l=lhsT, r=rhs, s=start, t=stop:
                 e.matmul(o, lhsT=l, rhs=r, start=s, stop=t), reads, writes)

        def TR(out, in_, idn, reads, writes):
            P.op("pe", lambda e, o=out, i=in_, d=idn: e.transpose(o, i, d), reads, writes)

        def ACT(out, in_, func, reads, writes, **kw):
            P.op("act", lambda e, o=out, i=in_, f=func, k=kw: e.activation(out=o, in_=i, func=f, **k), reads, writes)

        def TT(eng, out, in0, in1, op, reads, writes):
            P.op(eng, lambda e, o=out, a=in0, b=in1, p=op: e.tensor_tensor(out=o, in0=a, in1=b, op=p), reads, writes)

        def TS(eng, out, in0, s1, s2, op0, op1, reads, writes):
            if s2 is None:
                P.op(eng, lambda e, o=out, a=in0, x1=s1, p0=op0:
                     e.tensor_scalar(out=o, in0=a, scalar1=x1, scalar2=None, op0=p0), reads, writes)
            else:
                P.op(eng, lambda e, o=out, a=in0, x1=s1, x2=s2, p0=op0, p1=op1:
                     e.tensor_scalar(out=o, in0=a, scalar1=x1, scalar2=x2, op0=p0, op1=p1), reads, writes)

        def STT(eng, out, in0, scalar, in1, op0, op1, reads, writes):
            P.op(eng, lambda e, o=out, a=in0, s=scalar, b=in1, p0=op0, p1=op1:
                 e.scalar_tensor_tensor(out=o, in0=a, scalar=s, in1=b, op0=p0, op1=p1), reads, writes)

        def CP(eng, out, in_, reads, writes):
            if eng == "act":
                P.op("act", lambda e, o=out, i=in_: e.copy(out=o, in_=i), reads, writes)
            else:
                P.op(eng, lambda e, o=out, i=in_: e.tensor_copy(out=o, in_=i), reads, writes)

        def MEMSET(eng, ap, val, writes):
            P.op(eng, lambda e, a=ap, v=val: e.memset(a, v), (), writes)

        def DMA(q, out, in_, reads, writes, is_out=False, slow=False):
            def fn(e, o=out, i=in_, s=slow):
                if s:
                    with nc.allow_non_contiguous_dma(reason="small strided parameter/state transfer"):
                        return e.dma_start(out=o, in_=i)
                return e.dma_start(out=o, in_=i)
            P.dma(q, fn, reads, writes, is_out=is_out)

        def RECIP(out, in_, reads, writes):
            P.op("dve", lambda e, o=out, i=in_: e.reciprocal(out=o, in_=i), reads, writes)

        def carve(off, shape, dt):
            n = int(np.prod(shape))
            nb = n * (4 if dt == F32 else 2)
            assert off % 4 == 0 and nb % 4 == 0 and off + nb <= 65792, (off, shape)
            a = arena[:, off // 4:(off + nb) // 4]
            if dt != F32:
                a = a.bitcast(dt)
            if len(shape) == 2:
                a = a.rearrange("p (a b) -> p a b", b=shape[1])
            elif len(shape) == 3:
                a = a.rearrange("p (a b c) -> p a b c", b=shape[1], c=shape[2])
            return a

        rc = res("consts")
        DMA("sp", ident_f[:], D["k_ident"], (), [rc])
        DMA("sp", svec[:], D["k_svec"], (), [rc])
        DMA("sp", pidx[:], D["k_pidx"], (), [rc])
        DMA("sp", par[:], D["k_par"], (), [rc])
        DMA("pool", ident_b[:], D["k_ident"], (), [rc])
        DMA("pool", tri_b[:], D["k_tri"], (), [rc])
        MEMSET("dve", ones1024[:], 1.0 / 1024.0, [rc])
        MEMSET("dve", ones512[:], 1.0 / 512.0, [rc])
        MEMSET("dve", Hst[:], 0.0, [res("Hst")])
        MEMSET("dve", ctail[:], 0.0, [res("ctail")])

        def load_T(src, rows, ncols, dst_fn, wres, eng_alt=[0]):
            t = tmp()
            DMA("sp", tmpf[t][:rows, :ncols], src, (), [RT[t]])
            for b in range(ncols // 128):
                bk = bank()
                TR(ps[:, bk, :rows], tmpf[t][:rows, b * 128:(b + 1) * 128], ident_f[:rows, :rows],
                   [RT[t], rc], [RB[bk]])
                eng = "act" if (eng_alt[0] % 2 == 0) else "dve"
                eng_alt[0] += 1
                CP(eng, dst_fn(b), ps[:, bk, :rows], [RB[bk]], wres)

        KSTOP = int(os.environ.get("KSTOP", "99"))
        if KSTOP == 1:
            P.finish(); P.emit(ctx); return nc
        rp = res("params")
        load_T(D["g_mix"], 4, 1024, lambda b: gmix[:, b, :], [rp])
        load_T(D["g_ffn"], 4, 1024, lambda b: gffn[:, b, :], [rp])
        load_T(D["g_final"], 1, 1024, lambda b: gfin[:, b, :], [rp])
        load_T(D["ssm_d"], 4, 512, lambda b: dpar[:, b, :], [rp])
        load_T(D["conv_b"], 4, 512, lambda b: convb[:, b, :], [rp])
        load_T(D["ln_g"], 4, 512, lambda b: lng[:, b, :], [rp])
        load_T(D["ln_b"], 4, 512, lambda b: lnb[:, b, :], [rp])
        for l in range(4):
            load_T(D["conv_w"][l], 31, 512, lambda b, l=l: convw[:, l, b, :], [rp])
        for h2 in range(2):
            load_T(D["a_re"][:, h2 * 1024:(h2 + 1) * 1024], 4, 1024, lambda b, h2=h2: are[:, h2 * 8 + b, :], [rp])
            load_T(D["a_im"][:, h2 * 1024:(h2 + 1) * 1024], 4, 1024, lambda b, h2=h2: aim[:, h2 * 8 + b, :], [rp])
        if KSTOP == 2:
            P.finish(); P.emit(ctx); return nc
        t_ld = tmp()
        DMA("sp", tmpf[t_ld][:, 0:128], D["log_dt"].rearrange("l g -> (l g)").partition_broadcast(128), (), [RT[t_ld]])
        for g2 in range(2):
            srcv = tmpf[t_ld][64 * g2:64 * g2 + 64, 0:128].rearrange("p (l j two) -> p l j two", l=4, two=2)[:, :, :, g2]
            CP("dve", ldt[64 * g2:64 * g2 + 64, :, :], srcv, [RT[t_ld]], [rp])

        def load_x(half, rx):
            blocks = [(D["xp"][half * 1024 + b * 128: half * 1024 + (b + 1) * 128, :], 128, b * 128) for b in range(8)]
            if half == 0:
                blocks.append((D["xs"], 16, 1024))
            for bi, (src, rows, c0) in enumerate(blocks):
                t = tmp()
                DMA("sp", tmpf[t][:rows, :], src, (), [RT[t]])
                ti = c0 // 512
                for q in range(2):
                    bk = bank()
                    for k4 in range(4):
                        kt = 4 * q + k4
                        TR(ps[:, bk, k4 * rows:(k4 + 1) * rows], tmpf[t][:rows, kt * 128:(kt + 1) * 128],
                           ident_f[:rows, :rows], [RT[t], rc], [RB[bk]])
                    CP("act" if q == 0 else "dve", x[:, 4 * q:4 * q + 4, c0:c0 + rows],
                       ps[:, bk, :4 * rows].rearrange("p (k t) -> p k t", t=rows), [RB[bk]], [rx[ti]])


        rx0 = [res(f"x{ti}") for ti in range(3)]
        if n_halves > 0:
            load_x(0, rx0)
        if KSTOP == 3:
            P.finish(); P.emit(ctx); return nc
        rs = res("s5small")

        def sincos(eng, A, n, cs_out, sn_out, t_a, t_b, rA, rO):
            TS(eng, t_a, A, 1.0 / TWO_PI, MAGIC, ALU.mult, ALU.add, rA, rO)
            TS(eng, t_a, t_a, -MAGIC, None, ALU.add, None, rO, rO)
            STT(eng, t_b, t_a, -C1, A, ALU.mult, ALU.add, rA + rO, rO)
            STT(eng, t_b, t_a, -C2, t_b, ALU.mult, ALU.add, rO, rO)
            TS(eng, t_b, t_b, -math.pi, math.pi, ALU.max, ALU.min, rO, rO)
            ACT(sn_out, t_b, AF.Sin, rO, rO)
            ACT(t_a, t_b, AF.Sin, rO, rO, scale=0.5)
            TT(eng, t_a, t_a, t_a, ALU.mult, rO, rO)
            TS(eng, cs_out, t_a, -2.0, 1.0, ALU.mult, ALU.add, rO, rO)

        halfpi = sb("halfpi", [128, 1], F32)
        epsc = sb("epsc", [128, 1], F32)
        MEMSET("dve", halfpi[:], math.pi / 2.0, [rc])
        MEMSET("dve", epsc[:], EPS, [rc])

        S = {k: v[:] for k, v in sm.items()}
        are_v = are[:].rearrange("p j l -> p l j")
        aim_v = aim[:].rearrange("p j l -> p l j")
        ACT(S["dt"], ldt[:], AF.Exp, [rp], [rs])
        TT("dve", S["dr"], S["dt"], are_v, ALU.mult, [rs, rp], [rs])
        TT("dve", S["th"], S["dt"], aim_v, ALU.mult, [rs, rp], [rs])
        ACT(S["mag"], S["dr"], AF.Exp, [rs], [rs])
        sincos("dve", S["th"], 64, S["cs"], S["sn"], S["t1"], S["t2"], [rs], [rs])
        TT("dve", S["lre"], S["mag"], S["cs"], ALU.mult, [rs], [rs])
        TT("dve", S["lim"], S["mag"], S["sn"], ALU.mult, [rs], [rs])
        TS("dve", S["t1"], S["lre"], -1.0, None, ALU.add, None, [rs], [rs])
        TT("dve", S["t2"], are_v, are_v, ALU.mult, [rs, rp], [rs])
        TT("dve", S["t3"], aim_v, aim_v, ALU.mult, [rs, rp], [rs])
        TT("dve", S["den"], S["t2"], S["t3"], ALU.add, [rs], [rs])
        RECIP(S["den"], S["den"], [rs], [rs])
        TT("dve", S["t2"], S["t1"], are_v, ALU.mult, [rs, rp], [rs])
        TT("dve", S["t3"], S["lim"], aim_v, ALU.mult, [rs, rp], [rs])
        TT("dve", S["t2"], S["t2"], S["t3"], ALU.add, [rs], [rs])
        TT("dve", S["qre"], S["t2"], S["den"], ALU.mult, [rs], [rs])
        TT("dve", S["t2"], S["lim"], are_v, ALU.mult, [rs, rp], [rs])
        TT("dve", S["t3"], S["t1"], aim_v, ALU.mult, [rs, rp], [rs])
        TT("dve", S["t2"], S["t2"], S["t3"], ALU.subtract, [rs], [rs])
        TT("dve", S["qim"], S["t2"], S["den"], ALU.mult, [rs], [rs])

        A_ang = carve(0, [2048], F32)
        A_ta = carve(8192, [2048], F32)
        A_tb = carve(16384, [2048], F32)
        A_cs = carve(24576, [2048], F32)
        A_sn = carve(32768, [2048], F32)
        A_mg = carve(40960, [2048], F32)
        BpadRe = carve(49152, [16, 128], F32)
        BpadIm = carve(57344, [16, 128], F32)
        rAng, rTa, rTb, rCs, rSn, rMg, rBpR, rBpI = [res("prep_" + n_) for n_ in ("ang", "ta", "tb", "cs", "sn", "mg", "bpr", "bpi")]
        rSC = [rTa, rTb, rCs, rSn]
        MEMSET("pool", BpadRe, 0.0, [rBpR])
        MEMSET("pool", BpadIm, 0.0, [rBpI])
        n_prep = n_layers
        for l in range(n_prep):
            rl = [RA]
            ang3 = A_ang.rearrange("p (j s) -> p j s", s=128)
            sv_bc = svec[:].unsqueeze(1).to_broadcast([128, 16, 128])
            th_bc = sm["th"][:, l, :].unsqueeze(2).to_broadcast([128, 16, 128])
            dr_bc = sm["dr"][:, l, :].unsqueeze(2).to_broadcast([128, 16, 128])
            TT("dve", ang3, sv_bc, th_bc, ALU.mult, [rc, rs], [rAng])
            sincos("dve", A_ang, 2048, A_cs, A_sn, A_ta, A_tb, [rAng], rSC)
            TT("dve", A_ta.rearrange("p (j s) -> p j s", s=128), sv_bc, dr_bc, ALU.mult, [rc, rs], [rTa])
            ACT(A_mg, A_ta, AF.Exp, [rTa], [rMg])
            TT("dve", A_cs, A_cs, A_mg, ALU.mult, [rCs, rMg], [rCs])
            TT("dve", A_sn, A_sn, A_mg, ALU.mult, [rSn, rMg], [rSn])
            DMA("sp", scrP[l, 0], A_cs, [rCs], [res("scr")])
            DMA("sp", scrP[l, 1], A_sn, [rSn], [res("scr")])
            ACT(A_ang, A_mg, AF.Copy, [rMg], [rAng])
            P.op("dve", lambda e, o=A_mg, i=A_ang: e.reciprocal(out=o, in_=i), [rAng], [rMg])
            TT("dve", A_mg, A_mg, A_mg, ALU.mult, [rMg], [rMg])
            TT("dve", A_ta, A_cs, A_mg, ALU.mult, [rCs, rMg], [rTa])
            STT("dve", A_tb, A_sn, -1.0, A_mg, ALU.mult, ALU.mult, [rSn, rMg], [rTb])
            for (Qsrc, rQ, ri_) in ((A_ta, rTa, 0), (A_tb, rTb, 1)):
                for q4 in range(4):
                    bk = bank()
                    for jm in range(4):
                        j = 4 * q4 + jm
                        TR(ps[:, bk, jm * 128:(jm + 1) * 128], Qsrc[:, j * 128:(j + 1) * 128], ident_f[:], [rQ, rc], [RB[bk]])
                    CP("act" if q4 % 2 == 0 else "dve", A_ang[:, q4 * 512:(q4 + 1) * 512], ps[:, bk, :], [RB[bk]], [rAng])
                DMA("sp", scrR[l, ri_], A_ang, [rAng], [res("scr")])
            Braw_re = A_ta.rearrange("p (a b) -> p a b", b=128)[:, :, 0:16]
            Braw_im = A_tb.rearrange("p (a b) -> p a b", b=128)[:, :, 0:16]
            Bb_re = A_ta.rearrange("p (a b) -> p a b", b=128)[:, :, 16:32]
            Bb_im = A_tb.rearrange("p (a b) -> p a b", b=128)[:, :, 16:32]
            Bt1 = A_ta.rearrange("p (a b) -> p a b", b=128)[:, :, 32:48]
            Bt2 = A_tb.rearrange("p (a b) -> p a b", b=128)[:, :, 32:48]
            DMA("sp", Braw_re, D["b_re"][l].rearrange("(j i) c -> i j c", i=128), (), [rTa])
            DMA("sp", Braw_im, D["b_im"][l].rearrange("(j i) c -> i j c", i=128), (), [rTb])
            qre_bc = sm["qre"][:, l, :].unsqueeze(2).to_broadcast([128, 16, 16])
            qim_bc = sm["qim"][:, l, :].unsqueeze(2).to_broadcast([128, 16, 16])
            TT("dve", Bt1, Braw_re, qre_bc, ALU.mult, [rTa, rTb] + [rs], [rTa, rTb])
            TT("dve", Bt2, Braw_im, qim_bc, ALU.mult, [rTa, rTb] + [rs], [rTa, rTb])
            TT("dve", Bb_re, Bt1, Bt2, ALU.subtract, [rTa, rTb], [rTa, rTb])
            TT("dve", Bt1, Braw_im, qre_bc, ALU.mult, [rTa, rTb] + [rs], [rTa, rTb])
            TT("dve", Bt2, Braw_re, qim_bc, ALU.mult, [rTa, rTb] + [rs], [rTa, rTb])
            TT("dve", Bb_im, Bt1, Bt2, ALU.add, [rTa, rTb], [rTa, rTb])
            for (Bb, Bpad, rBp) in ((Bb_re, BpadRe, rBpR), (Bb_im, BpadIm, rBpI)):
                for g2 in range(2):
                    for jm in range(4):
                        c0 = 16 * (2 * jm + g2)
                        CP("dve", Bpad[64 * g2:64 * g2 + 64, jm::4, c0:c0 + 16],
                           Bb[64 * g2:64 * g2 + 64, jm::4, :], [rTa, rTb], [rBp])
            Bm_sb = A_cs.bitcast(BF16).rearrange("p (k r c) -> p k r c", k=4, r=2)
            for ri, (Bpad, rBp) in enumerate(((BpadRe, rBpR), (BpadIm, rBpI))):
                for kt in range(4):
                    bk = bank()
                    for jm in range(4):
                        TR(ps[:, bk, jm * 128:(jm + 1) * 128], Bpad[:, 4 * kt + jm, :], ident_f[:], [rBp, rc], [RB[bk]])
                    CP("act", Bm_sb[:, kt, ri, :], ps[:, bk, :], [RB[bk]], [rCs])
            DMA("sp", scrB[l], A_cs.bitcast(BF16), [rCs], [res("scr")])
            Craw = A_sn.rearrange("p (k q) -> p k q", q=512)
            Cm_sb = A_mg.bitcast(BF16).rearrange("p (r j c) -> p r j c", r=2, j=16)
            MEMSET("pool", A_mg, 0.0, [rMg])
            for ri, nm in enumerate(("c_re", "c_im")):
                DMA("sp", Craw[:, :, 0:64], D[nm][l].rearrange("(k r) q -> r k q", r=128), (), [rSn])
                for g2 in range(2):
                    TS("dve", Craw[:, :, 128 + 64 * g2:128 + 64 * g2 + 64], Craw[:, :, 0:64], par[:, g2:g2 + 1], None,
                       ALU.mult, None, [rSn, rc], [rSn])
                bk = bank()
                for kt in range(4):
                    TR(ps[:, bk, kt * 128:(kt + 1) * 128], Craw[:, kt, 128:256], ident_f[:], [rSn, rc], [RB[bk]])
                for kt in range(4):
                    for jm in range(4):
                        src = ps[:, bk, kt * 128 + 32 * jm:kt * 128 + 32 * jm + 32]
                        dst = Cm_sb[:, ri, 4 * kt + jm, 32 * jm:32 * jm + 32]
                        if ri == 0:
                            CP("act", dst, src, [RB[bk]], [rMg])
                        else:
                            P.op("act", lambda e, o=dst, i=src: e.mul(out=o, in_=i, mul=-1.0), [RB[bk]], [rMg])
            DMA("sp", scrC[l], A_mg.bitcast(BF16), [rMg], [res("scr")])
        P.barrier()

        def load_w(dst_slot, src_ap, kt_n, ncols, col_off=0, slot_cols=None):
            sc = slot_cols if slot_cols is not None else ncols
            dst = wsl[dst_slot][:, 0:kt_n * sc].rearrange("p (k c) -> p k c", c=sc)[:, :, col_off:col_off + ncols]
            DMA("pool", dst, src_ap.rearrange("(k p) c -> p k c", p=128), (), [RW[dst_slot]])

        def next_slot():
            s = st["wsl"]; st["wsl"] = (s + 1) % 3
            return s

        def wview(slot, kt_n, sc):
            return wsl[slot][:, 0:kt_n * sc].rearrange("p (k c) -> p k c", c=sc)

        def rmsnorm_stats(TTL, rx, gcol):
            out = []
            for ti, (t0, n) in enumerate(TTL):
                ACT(sqb[:, :, :n], x[:, :, t0:t0 + n], AF.Square, [rx[ti]], [res("sqb")])
                bk = bank()
                for kt in range(8):
                    MM(ps[:, bk, :n], ones1024[:], sqb[:, kt, :n], kt == 0, kt == 7, [res("sqb"), rc], [RB[bk]])
                t = tmp()
                ACT(tmpf[t][:, :n], ps[:, bk, :n], AF.Sqrt, [RB[bk], rc], [RT[t]], bias=epsc[:, 0:1])
                RECIP(tmpf[t][:, :n], tmpf[t][:, :n], [RT[t]], [RT[t]])
                out.append(t)
            return out

        def rms_apply(TTL, rx, rxn, gsel, ti, t):
            t0, n = TTL[ti]
            for kt in range(8):
                STT("dve", xn[:, kt, t0:t0 + n], x[:, kt, t0:t0 + n], gsel(kt), tmpf[t][:, :n],
                    ALU.mult, ALU.mult, [rx[ti], RT[t], rp], [rxn[ti]])

        def rmsnorm_to_xn(TTL, rx, rxn, gsel, tis=None):
            for ti, (t0, n) in enumerate(TTL):
                if tis is not None and ti not in tis:
                    continue
                (t,) = rmsnorm_stats([TTL[ti]], [rx[ti]], gsel)
                rms_apply(TTL, rx, rxn, gsel, ti, t)

        WS = dict(q=[])

        def ws_add(fn):
            WS["q"].append([fn, None])
            return len(WS["q"]) - 1

        def ws_use(i, la=2):
            for k in range(i, min(i + 1 + la, len(WS["q"]))):
                if WS["q"][k][1] is None:
                    sl = next_slot()
                    WS["q"][k][0](sl)
                    WS["q"][k][1] = sl
            return WS["q"][i][1]

        def decl_dense(wsrc, kt_n, col0, ncols, grp):
            ids = []
            for g0 in range(0, ncols, grp):
                ids.append(ws_add(lambda sl, a=wsrc[:, col0 + g0:col0 + g0 + grp], k=kt_n, g=grp: load_w(sl, a, k, g)))
            return ids

        def dense(gids, kt_n, grp, src_fn, rsrc, TTL, consumer, tis=None, mid_hook=None, la=2, hooks=None):
            if tis is None:
                tis = list(range(len(TTL)))
            for gi, gid in enumerate(gids):
                s = ws_use(gid, la)
                wv = wview(s, kt_n, grp)
                for nt in range(grp // 128):
                    for ti in tis:
                        t0, n = TTL[ti]
                        bk = bank()
                        for kt in range(kt_n):
                            MM(ps[:, bk, :n], wv[:, kt, nt * 128:(nt + 1) * 128], src_fn(kt, t0, n),
                               kt == 0, kt == kt_n - 1, [RW[s]] + rsrc[ti], [RB[bk]])
                        consumer(gi * (grp // 128) + nt, ti, t0, n, bk)
                    if hooks is not None and (gi, nt) in hooks:
                        hooks[(gi, nt)]()
                if gi == 0 and mid_hook is not None:
                    mid_hook()

        def ffin_load(l, g):
            def fn(sl):
                load_w(sl, D["w_ff_in"][l][:, 256 * g:256 * g + 256], 8, 256, col_off=0, slot_cols=512)
                load_w(sl, D["w_ff_in"][l][:, 2816 + 256 * g:2816 + 256 * g + 256], 8, 256, col_off=256, slot_cols=512)
            return fn

        WD = {}
        for half_ in range(n_halves):
            for l_ in range(n_layers):
                w = {}
                w["u"] = decl_dense(D["w_in"][l_], 8, 0, 512, 512)
                w["cv"] = decl_dense(D["w_in"][l_], 8, 512, 512, 512)
                w["cg"] = decl_dense(D["w_in"][l_], 8, 1024, 512, 512)
                w["glu1"] = decl_dense(D["w_glu"][l_], 4, 0, 1024, 1024)
                w["glu2"] = decl_dense(D["w_glu"][l_], 4, 1024, 1024, 1024)
                w["gate"] = [None] * 4
                w["gate"][0] = decl_dense(D["w_in"][l_], 8, 1536, 512, 512)
                w["gate"][1] = decl_dense(D["w_in"][l_], 8, 2048, 512, 512)
                w["pw"] = decl_dense(D["w_pw"][l_], 4, 0, 1024, 1024)
                w["gate"][2] = decl_dense(D["w_in"][l_], 8, 2560, 512, 512)
                w["gate"][3] = decl_dense(D["w_in"][l_], 8, 3072, 512, 512)
                w["outA"] = decl_dense(D["w_out"][l_], 8, 0, 1024, 512)
                w["outB"] = decl_dense(D["w_out"][l_], 8, 0, 1024, 512)
                w["ffin"] = [ws_add(ffin_load(l_, g)) for g in range(11)]
                w["ffoutA"] = decl_dense(D["w_ff_out"][l_], 22, 0, 1024, 128)
                w["ffoutB"] = decl_dense(D["w_ff_out"][l_], 22, 0, 1024, 128)
                WD[(half_, l_)] = w

        for half in range(n_halves):
            TTL = [(0, 512), (512, 512)] + ([(1024, 16)] if half == 0 else [])
            NTT = len(TTL)
            rx = [res(f"x{ti}") for ti in range(NTT)]
            rxn = [res(f"xn{ti}") for ti in range(NTT)]
            ru = [[res(f"u{c}_{k}") for k in range(4)] for c in range(9)]
            rcb = [res(f"cb{i}") for i in range(3)]
            rya = [res(f"ya{ti}") for ti in range(NTT)]
            rar = [res(f"ar{ti}") for ti in range(NTT)]
            rar2 = [res(f"ar2_{ti}") for ti in range(NTT)]

            def ru_tile(ti, kt=None):
                cks = [8] if ti == 2 else list(range(4 * ti, 4 * ti + 4))
                if kt is None:
                    return [ru[c][k] for c in cks for k in range(4)]
                return [ru[c][kt] for c in cks]

            if half > 0:
                load_x(half, rx)
            if KSTOP == 4:
                P.finish(); P.emit(ctx); return nc
            for l in range(n_layers):
                P.epoch += 1
                W = WD[(half, l)]
                Tb = [carve(4096 * i, [4, 512], BF16) for i in range(2)]
                Mb = [carve(8192 + 4096 * i, [4, 512], BF16) for i in range(2)]
                Rre = carve(16640, [2048], F32); Rim = carve(24832, [2048], F32)
                Pre = carve(33024, [16, 128], F32); Pim = carve(41216, [16, 128], F32)
                Bm = carve(49408, [4, 2, 512], BF16)
                Cm = carve(57600, [2, 16, 128], BF16)
                rTb = [res("Tb0"), res("Tb1")]; rMb = [res("Mb0"), res("Mb1")]
                rtab = res("s5tab")
                tab_w = [rtab, res("diag"), res("lnst"), res("bufT"), res("xo")] + \
                        [res(f"ar{i}") for i in range(3)] + [res(f"ar2_{i}") for i in range(3)]
                DMA("sp", Rre, scrR[l, 0], [res("scr")], tab_w)
                DMA("sp", Rim, scrR[l, 1], [res("scr")], [rtab])
                DMA("sp", Pre, scrP[l, 0].rearrange("p (j s) -> p j s", s=128), [res("scr")], [rtab])
                DMA("sp", Pim, scrP[l, 1].rearrange("p (j s) -> p j s", s=128), [res("scr")], [rtab])
                DMA("sp", Bm, scrB[l].rearrange("p (k r c) -> p k r c", k=4, r=2), [res("scr")], [rtab])
                DMA("sp", Cm, scrC[l].rearrange("p (r j c) -> p r j c", r=2, j=16), [res("scr")], [rtab])
                gmix_sel = lambda kt, l=l: gmix[:, kt, l:l + 1]
                if l == 0:
                    rmsnorm_to_xn(TTL, rx, rxn, gmix_sel, tis=[0])
                cv32 = carve(0, [4, NTM], F32)

                def cons_u(ntl, ti, t0, n, bk):
                    CP("act", u[:, ntl, t0:t0 + n], ps[:, bk, :n], [RB[bk]], ru_tile(ti, ntl))

                def cons_cv(ntl, ti, t0, n, bk):
                    CP("act", cv32[:, ntl, t0:t0 + n], ps[:, bk, :n], [RB[bk]], [rar[ti]])

                def cons_cg(ntl, ti, t0, n, bk, l=l, half=half):
                    t = tmp()
                    ACT(tmpf[t][:, :n], ps[:, bk, :n], AF.Sigmoid, [RB[bk]], [RT[t]])
                    TT("dve", cv32[:, ntl, t0:t0 + n], cv32[:, ntl, t0:t0 + n], tmpf[t][:, :n], ALU.mult,
                       [rar[ti], RT[t]], [rar[ti]])
                    if ti < 2:
                        CP("act", cbuf[:, ntl, 30 + t0:30 + t0 + n], cv32[:, ntl, t0:t0 + n], [rar[ti]], [rcb[1 + ti]])
                        if ti == 1:
                            CP("dve", ctail32[:, ntl, :], cv32[:, ntl, 994:1024], [rar[ti]], [res("ctail32")])
                    else:
                        CP("dve", cnew32[:, ntl, :], cv32[:, ntl, t0:t0 + n], [rar[ti]], [res("cnew32")])

                xsrc = lambda kt, t0, n: xn[:, kt, t0:t0 + n]
                rxn_l = [[r] for r in rxn]
                CP("pool", cbuf[:, :, 0:30], ctail[:, l, :, :], [res("ctail")], [rcb[0]])
                tis0 = [0]
                tis1 = list(range(1, NTT))
                dense(W["u"], 8, 512, xsrc, rxn_l, TTL, cons_u, tis=tis0, la=2)
                rmsnorm_to_xn(TTL, rx, rxn, gmix_sel, tis=tis1)
                dense(W["cv"], 8, 512, xsrc, rxn_l, TTL, cons_cv, tis=tis0, la=1)
                dense(W["cg"], 8, 512, xsrc, rxn_l, TTL, cons_cg, tis=tis0, la=0)
                dense(W["u"], 8, 512, xsrc, rxn_l, TTL, cons_u, tis=tis1)
                dense(W["cv"], 8, 512, xsrc, rxn_l, TTL, cons_cv, tis=tis1)
                dense(W["cg"], 8, 512, xsrc, rxn_l, TTL, cons_cg, tis=tis1)
                if half == 0:
                    DMA("sp", D["cvs"][l, :, 0:29, :], D["scv"][l].rearrange("(b k) c -> b k c", k=30)[:, 1:30, :],
                        (), [res("cvs")], is_out=True)
                    t = tmp()
                    bk = bank()
                    for ct in range(4):
                        TR(ps[:16, bk, ct * 128:(ct + 1) * 128], cnew32[:, ct, :], ident_f[:], [res("cnew32"), rc], [RB[bk]])
                    CP("dve", tmpf[t][:16, :512], ps[:16, bk, :], [RB[bk]], [RT[t]])
                    DMA("sp", D["cvs"][l, :, 29, :], tmpf[t][:16, :512], [RT[t]], [res("cvs")], is_out=True)
                if half == n_halves - 1:
                    t = tmp()
                    bk = bank()
                    for ct in range(4):
                        TR(ps[:30, bk, ct * 128:(ct + 1) * 128], ctail32[:, ct, :], ident_f[:], [res("ctail32"), rc], [RB[bk]])
                    CP("dve", tmpf[t][:30, :512], ps[:30, bk, :], [RB[bk]], [RT[t]])
                    DMA("sp", D["cvp"][l], tmpf[t][:30, :512], [RT[t]], [res("cvp")], is_out=True)

                rH = res("Hst"); rcar = res("carry"); rgc = res("gcol")
                lre = sm["lre"][:, l, :]; lim = sm["lim"][:, l, :]
                gcol = hsm[1]

                def cmul_small(o_re, o_im, a_re_, a_im_, b_re_, b_im_, reads, writes):
                    t1 = hsm[0][:, 0, :]; t2 = hsm[0][:, 1, :]
                    TT("pool", t1, a_re_, b_re_, ALU.mult, reads, [res("hsm")])
                    TT("pool", t2, a_im_, b_im_, ALU.mult, reads, [res("hsm")])
                    TT("pool", o_re, t1, t2, ALU.subtract, [res("hsm")], writes)
                    TT("pool", t1, a_re_, b_im_, ALU.mult, reads, [res("hsm")])
                    TT("pool", t2, a_im_, b_re_, ALU.mult, reads, [res("hsm")])
                    TT("pool", o_im, t1, t2, ALU.add, [res("hsm")], writes)

                if half == 0:
                    MEMSET("pool", Hst[:, l, :, :], 0.0, [rH])
                v3 = lambda ap: ap.rearrange("p (j s) -> p j s", s=128)
                L128 = hsm[2]
                rL = res("L128")
                cmul_small(L128[:, 0, :], L128[:, 1, :], lre, lim, Pre[:, :, 127], Pim[:, :, 127], [rs, rtab], [rL])
                cmul_small(carry[:, 0, :], carry[:, 1, :], lre, lim, Hst[:, l, 0, :], Hst[:, l, 1, :], [rs, rH], [rcar])
                items = [(ck, kt) for ck in range(8) for kt in range(4)]
                Mviews = [[v3(Mb[p_][:, i, :]) for i in range(4)] for p_ in range(2)]

                def stage_A(it):
                    ck, kt = items[it]; pp = it % 2; c0 = ck * 128
                    T = Tb[pp]
                    ruk = [ru[ck][kt]]
                    bre, bim = 0, 1
                    MM(ps[:, bre, :], u[:, kt, c0:c0 + 128], Bm[:, kt, 0, :], True, True, ruk + [rtab], [RB[bre]])
                    MM(ps[:, bim, :], u[:, kt, c0:c0 + 128], Bm[:, kt, 1, :], True, True, ruk + [rtab], [RB[bim]])
                    cs = slice(kt * 512, (kt + 1) * 512)
                    TT("dve", T[:, 0, :], ps[:, bre, :], Rre[:, cs], ALU.mult, [RB[bre], rtab],
                       [rTb[pp]] + (list(rar) if it < 2 else []))
                    TT("dve", T[:, 2, :], ps[:, bre, :], Rim[:, cs], ALU.mult, [RB[bre], rtab], [rTb[pp]])
                    STT("dve", T[:, 1, :], ps[:, bim, :], -1.0, Rim[:, cs], ALU.mult, ALU.mult, [RB[bim], rtab], [rTb[pp]])
                    TT("dve", T[:, 3, :], ps[:, bim, :], Rre[:, cs], ALU.mult, [RB[bim], rtab], [rTb[pp]])

                GT = {}

                def stage_B1(it):
                    ck, kt = items[it]; pp = it % 2
                    T = Tb[pp]
                    gre, gim = (2, 3) if pp == 0 else (4, 5)
                    for jm in range(4):
                        MM(ps[:, gre, jm * 128:(jm + 1) * 128], T[:, 0, jm * 128:(jm + 1) * 128], tri_b[:],
                           True, False, [rTb[pp], rc], [RB[gre]])
                        MM(ps[:, gre, jm * 128:(jm + 1) * 128], T[:, 1, jm * 128:(jm + 1) * 128], tri_b[:],
                           False, True, [rTb[pp], rc], [RB[gre]])
                    for jm in range(4):
                        MM(ps[:, gim, jm * 128:(jm + 1) * 128], T[:, 2, jm * 128:(jm + 1) * 128], tri_b[:],
                           True, False, [rTb[pp], rc], [RB[gim]])
                        MM(ps[:, gim, jm * 128:(jm + 1) * 128], T[:, 3, jm * 128:(jm + 1) * 128], tri_b[:],
                           False, True, [rTb[pp], rc], [RB[gim]])
                    ta = tmp()
                    GT[it] = ta
                    gr = v3(tmpf[ta][:, 0:512]); gi = v3(tmpf[ta][:, 512:1024])
                    js = slice(4 * kt, 4 * kt + 4)
                    for jm in range(4):
                        j = 4 * kt + jm
                        ACT(gr[:, jm, :], ps[:, gre, jm * 128:(jm + 1) * 128], AF.Identity, [RB[gre], rcar], [RT[ta]],
                            bias=carry[:, 0, j:j + 1])
                    for jm in range(4):
                        j = 4 * kt + jm
                        ACT(gi[:, jm, :], ps[:, gim, jm * 128:(jm + 1) * 128], AF.Identity, [RB[gim], rcar], [RT[ta]],
                            bias=carry[:, 1, j:j + 1])
                    CP("act", gcol[:, 0, js], gr[:, :, 127], [RT[ta]], [rgc])
                    CP("act", gcol[:, 1, js], gi[:, :, 127], [RT[ta]], [rgc])
                    if kt == 3:
                        if ck < 7:
                            t1 = hsm[0][:, 0, :]; t2 = hsm[0][:, 1, :]; t3 = hsm[3][:, 0, :]; t4 = hsm[3][:, 1, :]
                            rh_ = res("hsm")
                            TT("pool", t1, L128[:, 0, :], gcol[:, 0, :], ALU.mult, [rL, rgc], [rh_])
                            TT("pool", t2, L128[:, 1, :], gcol[:, 1, :], ALU.mult, [rL, rgc], [rh_])
                            TT("pool", t3, L128[:, 0, :], gcol[:, 1, :], ALU.mult, [rL, rgc], [rh_])
                            TT("pool", t4, L128[:, 1, :], gcol[:, 0, :], ALU.mult, [rL, rgc], [rh_])
                            TT("pool", carry[:, 0, :], t1, t2, ALU.subtract, [rh_], [rcar])
                            TT("pool", carry[:, 1, :], t3, t4, ALU.add, [rh_], [rcar])
                        else:
                            cmul_small(Hst[:, l, 0, :], Hst[:, l, 1, :], gcol[:, 0, :], gcol[:, 1, :],
                                       Pre[:, :, 127], Pim[:, :, 127], [rgc, rtab], [rH])

                def stage_B2(it):
                    ck, kt = items[it]; pp = it % 2
                    Mv = Mviews[pp]
                    ta = GT.pop(it)
                    gr = v3(tmpf[ta][:, 0:512]); gi = v3(tmpf[ta][:, 512:1024])
                    js = slice(4 * kt, 4 * kt + 4)
                    TT("pool", Mv[0], gr, Pre[:, js, :], ALU.mult, [RT[ta], rtab],
                       [rMb[pp]] + (list(rar) if it < 2 else []))
                    STT("dve", Mv[1], gi, -1.0, Pim[:, js, :], ALU.mult, ALU.mult, [RT[ta], rtab], [rMb[pp]])
                    TT("pool", Mv[2], gr, Pim[:, js, :], ALU.mult, [RT[ta], rtab], [rMb[pp]])
                    TT("pool", Mv[3], gi, Pre[:, js, :], ALU.mult, [RT[ta], rtab], [rMb[pp]])

                def stage_C(it):
                    ck, kt = items[it]; pp = it % 2; c0 = ck * 128
                    Mv = Mviews[pp]
                    ruk = [ru[ck][kt]]
                    bk = 6 + pp
                    for jm in range(4):
                        j = 4 * kt + jm
                        MM(ps[:, bk, :128], Cm[:, 0, j, :], Mv[0][:, jm, :], jm == 0, False, [rtab, rMb[pp]], [RB[bk]])
                        MM(ps[:, bk, :128], Cm[:, 0, j, :], Mv[1][:, jm, :], False, False, [rtab, rMb[pp]], [RB[bk]])
                        MM(ps[:, bk, :128], Cm[:, 1, j, :], Mv[2][:, jm, :], False, False, [rtab, rMb[pp]], [RB[bk]])
                        MM(ps[:, bk, :128], Cm[:, 1, j, :], Mv[3][:, jm, :], False, jm == 3, [rtab, rMb[pp]], [RB[bk]])
                    t = tmp()
                    STT("dve", tmpf[t][:, :128], u[:, kt, c0:c0 + 128], dpar[:, kt, l:l + 1], ps[:, bk, :128],
                        ALU.mult, ALU.add, ruk + [rp, RB[bk]], [RT[t]])
                    ACT(u[:, kt, c0:c0 + 128], tmpf[t][:, :128], AF.Gelu_apprx_tanh, [RT[t]], ruk)

                segs = []
                if half == 0:
                    stgA = ya[:, 3:5, :].rearrange("p a b -> p (a b)").bitcast(F32)
                    sS = ya[:, 5:7, :].rearrange("p a b -> p (a b)").bitcast(F32)
                    sT = ya[:, 7, :].bitcast(F32)
                    rstg = res("s_stg"); rsS = res("s_sS"); rsT = res("s_sT")
                    rhsp = res("hsp"); rhsn = res("hsn"); rhsb = res("hsb"); rbus = res("sqb")
                    sbk = [0]

                    def sbank():
                        sbk[0] ^= 1
                        return 6 + sbk[0]

                    def seg_load(ri, nm, h2):
                        def f():
                            DMA("sp", stgA[:16, 0:1024], D[nm][l][:, h2 * 1024:(h2 + 1) * 1024], (), [rstg])
                            bk = sbank()
                            for b_ in range(8):
                                TR(ps[:, bk, b_ * 16:(b_ + 1) * 16], stgA[:16, b_ * 128:(b_ + 1) * 128], ident_f[:16, :16],
                                   [rstg, rc], [RB[bk]])
                            CP("act", hsp[:, ri, h2 * 8:(h2 + 1) * 8, :],
                               ps[:, bk, 0:128].rearrange("p (j s) -> p j s", s=16), [RB[bk]], [rhsp])
                        return f

                    for ri_, nm_ in enumerate(("sre", "sim")):
                        for h2_ in range(2):
                            segs.append(seg_load(ri_, nm_, h2_))

                    def seg_bu(kt):
                        def f():
                            for ri in range(2):
                                bk = sbank()
                                MM(ps[:16, bk, :], u[:, kt, 1024:1040], Bm[:, kt, ri, :], True, True, ru[8] + [rtab], [RB[bk]])
                                CP("act", bus[:, ri, kt * 512:(kt + 1) * 512], ps[:16, bk, :], [RB[bk]], [rbus])
                        return f

                    for kt_ in range(4):
                        segs.append(seg_bu(kt_))

                    def seg_step():
                        bk = sbank()
                        for ri in range(2):
                            for j in range(16):
                                MM(ps[:, bk, ri * 256 + j * 16:ri * 256 + (j + 1) * 16], bus[:, ri, j * 128:(j + 1) * 128],
                                   ident_b[:16, :16], True, True, [rbus, rc], [RB[bk]])
                        lre_bc = lre.unsqueeze(2).to_broadcast([128, 16, 16])
                        lim_bc = lim.unsqueeze(2).to_broadcast([128, 16, 16])
                        v16 = lambda ap: ap.rearrange("p (j s) -> p j s", s=16)
                        s1 = v16(sS[:, 0:256]); s2 = v16(sS[:, 256:512]); s3 = v16(sS[:, 512:768]); s4 = v16(sS[:, 768:1024])
                        TT("dve", s1, hsp[:, 0, :, :], lre_bc, ALU.mult, [rhsp, rs], [rsS])
                        TT("dve", s2, hsp[:, 1, :, :], lim_bc, ALU.mult, [rhsp, rs], [rsS])
                        TT("dve", s3, hsp[:, 0, :, :], lim_bc, ALU.mult, [rhsp, rs], [rsS])
                        TT("dve", s4, hsp[:, 1, :, :], lre_bc, ALU.mult, [rhsp, rs], [rsS])
                        TT("dve", s1, s1, s2, ALU.subtract, [rsS], [rsS])
                        TT("dve", s3, s3, s4, ALU.add, [rsS], [rsS])
                        TT("dve", hsn[:, 0, :, :], s1, v16(ps[:, bk, 0:256]), ALU.add, [rsS, RB[bk]], [rhsn])
                        TT("dve", hsn[:, 1, :, :], s3, v16(ps[:, bk, 256:512]), ALU.add, [rsS, RB[bk]], [rhsn])
                        CP("act", hsb, hsn, [rhsn], [rhsb])

                    segs.append(seg_step)

                    def seg_y(kt):
                        def f():
                            bk = sbank()
                            for jm in range(4):
                                j = 4 * kt + jm
                                MM(ps[:, bk, :16], Cm[:, 0, j, :], hsb[:, 0, j, :], jm == 0, False, [rtab, rhsb], [RB[bk]])
                                MM(ps[:, bk, :16], Cm[:, 1, j, :], hsb[:, 1, j, :], False, jm == 3, [rtab, rhsb], [RB[bk]])
                            STT("dve", sT[:, kt * 16:(kt + 1) * 16], u[:, kt, 1024:1040], dpar[:, kt, l:l + 1], ps[:, bk, :16],
                                ALU.mult, ALU.add, [ru[8][kt], rp, RB[bk]], [rsT])
                            ACT(u[:, kt, 1024:1040], sT[:, kt * 16:(kt + 1) * 16], AF.Gelu_apprx_tanh, [rsT], [ru[8][kt]])
                        return f

                    for kt_ in range(4):
                        segs.append(seg_y(kt_))

                    def seg_out(ri, nm, h2):
                        def f():
                            for q in range(2):
                                bk = sbank()
                                for jm in range(4):
                                    j = h2 * 8 + q * 4 + jm
                                    TR(ps[:16, bk, jm * 128:(jm + 1) * 128], hsn[:, ri, j, :], ident_f[:], [rhsn, rc], [RB[bk]])
                                CP("act", stgA[:16, q * 512:(q + 1) * 512], ps[:16, bk, :], [RB[bk]], [rstg])
                            DMA("sp", D[nm][l][:, h2 * 1024:(h2 + 1) * 1024], stgA[:16, 0:1024], [rstg], [res(nm)], is_out=True)
                        return f

                    for ri_, nm_ in enumerate(("res", "ims")):
                        for h2_ in range(2):
                            segs.append(seg_out(ri_, nm_, h2_))

                NI = len(items)
                for step in range(NI + 3):
                    if step < NI:
                        stage_A(step)
                    if 0 <= step - 2 < NI:
                        stage_B2(step - 2)
                    if 0 <= step - 1 < NI:
                        stage_B1(step - 1)
                    if 0 <= step - 3 < NI:
                        stage_C(step - 3)
                    if segs and step >= 3 and step % 2 == 1:
                        segs.pop(0)()
                while segs:
                    segs.pop(0)()
                if half == n_halves - 1:
                    for ri, nm in enumerate(("rep", "imp")):
                        bk = bank(); t = tmp()
                        TR(ps[:16, bk, :128], Hst[:, l, ri, :], ident_f[:], [rH, rc], [RB[bk]])
                        CP("dve", tmpf[t][:16, :128], ps[:16, bk, :128], [RB[bk]], [RT[t]])
                        DMA("sp", D[nm][l], tmpf[t][:16, :128], [RT[t]], [res(nm)], is_out=True)

                diag = carve(0, [4, 31, 128], BF16)
                conv32 = carve(31744, [4, NTM], F32)
                cbf = carve(48384, [4, 512], BF16)
                csq = carve(52480, [4, 512], BF16)
                bufT = carve(56576, [4, 16, 30], F32) if half == 0 else None
                rdiag = res("diag")
                s5_alias = [res("Tb0"), res("Tb1"), res("Mb0"), res("Mb1"), res("s5tab")]
                ya_alias = [res(n_) for n_ in ("hsp", "hsn", "hsb", "s_stg", "s_sS", "s_sT")] if half == 0 else []
                idb_bc = ident_b[:].unsqueeze(1).to_broadcast([128, 31, 128])
                for ct in range(4):
                    TT("pool", diag[:, ct, :, :], idb_bc,
                       convw[:, l, ct, :].unsqueeze(2).to_broadcast([128, 31, 128]), ALU.mult, [rc, rp],
                       [rdiag] + (s5_alias if ct == 0 else []))
                usrc = lambda kt, t0, n: u[:, kt, t0:t0 + n]
                ru_l = [ru_tile(ti) for ti in range(NTT)]

                def cons_g1(ntl, ti, t0, n, bk):
                    CP("act", ya[:, ntl, t0:t0 + n], ps[:, bk, :n], [RB[bk]], [rya[ti]] + ya_alias)

                def cons_g2(ntl, ti, t0, n, bk):
                    t = tmp()
                    ACT(tmpf[t][:, :n], ps[:, bk, :n], AF.Sigmoid, [RB[bk]], [RT[t]])
                    TT("dve", ya[:, ntl, t0:t0 + n], ya[:, ntl, t0:t0 + n], tmpf[t][:, :n], ALU.mult,
                       [rya[ti], RT[t]], [rya[ti]])

                dense(W["glu1"], 4, 1024, usrc, ru_l, TTL, cons_g1)
                dense(W["glu2"], 4, 1024, usrc, ru_l, TTL, cons_g2)
                if half == 0:
                    CP("dve", ctail[:, l, :, :], cbuf[:, :, 1024:1054], [rcb[2]], [res("ctail")])
                    for rg in range(4):
                        t = tmp()
                        DMA("sp", tmpf[t][:120, :512], D["scv"][l][rg * 120:(rg + 1) * 120, :], (), [RT[t]])
                        bk = bank()
                        for ct in range(4):
                            TR(ps[:, bk, ct * 120:(ct + 1) * 120], tmpf[t][:120, ct * 128:(ct + 1) * 128],
                               ident_f[:120, :120], [RT[t], rc], [RB[bk]])
                        CP("dve", bufT[:, :, 4 * rg:4 * rg + 4, :],
                           ps[:, bk, :480].rearrange("p (c b k) -> p c b k", c=4, b=4), [RB[bk]],
                           [res("bufT")] + s5_alias)
                    for ct in range(4):
                        t = tmp()
                        pv = tmpf[t][:, 0:480].rearrange("p (b k) -> p b k", k=30)
                        TT("dve", pv, bufT[:, ct, :, :], convw[:, l, ct, 0:30].unsqueeze(1).to_broadcast([128, 16, 30]),
                           ALU.mult, [res("bufT"), rp], [RT[t]])
                        P.op("dve", lambda e, o=tmpf[t][:, 512:528], i=pv: e.tensor_reduce(
                            out=o, in_=i, op=ALU.add, axis=mybir.AxisListType.X), [RT[t]], [RT[t]])
                        STT("dve", tmpf[t][:, 528:544], cnew32[:, ct, :], convw[:, l, ct, 30:31], tmpf[t][:, 512:528],
                            ALU.mult, ALU.add, [res("cnew32"), rp, RT[t]], [RT[t]])
                        ACT(conv32[:, ct, 1024:1040], tmpf[t][:, 528:544], AF.Identity, [RT[t], rp], [rar[2]] + s5_alias,
                            bias=convb[:, ct, l:l + 1])


                cact = u

                def ln_pre(ti):
                    t0, n = TTL[ti]
                    rtmp = res("lnst")
                    ACT(cbf[:, :, :n], conv32[:, :, t0:t0 + n], AF.Copy, [rar[ti]], [rtmp])
                    ACT(csq[:, :, :n], conv32[:, :, t0:t0 + n], AF.Square, [rar[ti]], [rtmp])

                def ln_rest(ti):
                    t0, n = TTL[ti]
                    rtmp = res("lnst")
                    bm = bank()
                    for ct in range(4):
                        MM(ps[:, bm, :n], ones512[:], cbf[:, ct, :n], ct == 0, ct == 3, [rtmp, rc], [RB[bm]])
                    bq = bank()
                    for ct in range(4):
                        MM(ps[:, bq, :n], ones512[:], csq[:, ct, :n], ct == 0, ct == 3, [rtmp, rc], [RB[bq]])
                    tm, tv = tmp(), tmp()
                    CP("act", tmpf[tm][:, :n], ps[:, bm, :n], [RB[bm]], [RT[tm]])
                    TT("dve", tmpf[tv][:, :n], tmpf[tm][:, :n], tmpf[tm][:, :n], ALU.mult, [RT[tm]], [RT[tv]])
                    TT("dve", tmpf[tv][:, :n], ps[:, bq, :n], tmpf[tv][:, :n], ALU.subtract, [RB[bq], RT[tv]], [RT[tv]])
                    TS("dve", tmpf[tv][:, :n], tmpf[tv][:, :n], 0.0, None, ALU.max, None, [RT[tv]], [RT[tv]])
                    ACT(tmpf[tv][:, :n], tmpf[tv][:, :n], AF.Sqrt, [RT[tv], rc], [RT[tv]], bias=epsc[:, 0:1])
                    RECIP(tmpf[tv][:, :n], tmpf[tv][:, :n], [RT[tv]], [RT[tv]])
                    for ct in range(4):
                        tx = tm if ct % 2 == 0 else tv
                        xc = tmpf[tx][:, 512:512 + n]
                        TT("dve", xc, conv32[:, ct, t0:t0 + n], tmpf[tm][:, :n], ALU.subtract,
                           [rar[ti], RT[tm]], [RT[tx]])
                        TT("dve", xc, xc, tmpf[tv][:, :n], ALU.mult, [RT[tx], RT[tv]], [RT[tx]])
                        ACT(cact[:, ct, t0:t0 + n], xc, AF.Silu, [RT[tx], rp], ru_tile(ti, ct),
                            scale=lng[:, ct, l:l + 1], bias=lnb[:, ct, l:l + 1])

                for ti in range(2):
                    t0 = ti * 512
                    for ct in range(4):
                        bk = bank()
                        for k in range(31):
                            MM(ps[:, bk, :], diag[:, ct, k, :], cbuf[:, ct, t0 + k:t0 + k + 512], k == 0, k == 30,
                               [rdiag, rcb[ti], rcb[ti + 1]], [RB[bk]])
                        ACT(conv32[:, ct, t0:t0 + 512], ps[:, bk, :], AF.Identity, [RB[bk], rp], [rar[ti]],
                            bias=convb[:, ct, l:l + 1])
                        if ti == 1 and ct == 1:
                            ln_rest(0)
                    if ti == 0:
                        ln_pre(0)

                ybuf = carve(0, [8, NTM], BF16)
                csrc = lambda kt, t0, n: cact[:, kt, t0:t0 + n]

                def cons_yb(ntl, ti, t0, n, bk):
                    CP("act", ybuf[:, ntl, t0:t0 + n], ps[:, bk, :n], [RB[bk]], [rar2[ti], rdiag])

                def cons_gate1(g):
                    def f(ntl, ti, t0, n, bk):
                        t = tmp()
                        ACT(tmpf[t][:, :n], ps[:, bk, :n], AF.Sigmoid, [RB[bk]], [RT[t]])
                        TT("dve", ya[:, ntl + 4 * g, t0:t0 + n], ya[:, ntl + 4 * g, t0:t0 + n], tmpf[t][:, :n], ALU.mult,
                           [rya[ti], RT[t]], [rya[ti]])
                    return f

                def cons_gate2(g):
                    def f(ntl, ti, t0, n, bk):
                        t = tmp()
                        nt_ = ntl + 4 * g
                        ACT(tmpf[t][:, :n], ps[:, bk, :n], AF.Sigmoid, [RB[bk]], [RT[t]])
                        TT("dve", ybuf[:, nt_, t0:t0 + n], ybuf[:, nt_, t0:t0 + n], tmpf[t][:, :n], ALU.mult,
                           [rar2[ti], RT[t]], [rar2[ti]])
                        TT("dve", ya[:, nt_, t0:t0 + n], ya[:, nt_, t0:t0 + n], ybuf[:, nt_, t0:t0 + n], ALU.add,
                           [rya[ti], rar2[ti]], [rya[ti]])
                    return f

                ln_pre(1)
                dense(W["gate"][0], 8, 512, xsrc, rxn_l, TTL, cons_gate1(0))
                ln_rest(1)
                dense(W["gate"][1], 8, 512, xsrc, rxn_l, TTL, cons_gate1(1))
                if half == 0:
                    ln_pre(2)
                    ln_rest(2)
                dense(W["pw"], 4, 1024, csrc, ru_l, TTL, cons_yb, tis=[0, 1])
                if half == 0:
                    dense(W["pw"], 4, 1024, csrc, ru_l, TTL, cons_yb, tis=[2])
                for g in range(2):
                    dense(W["gate"][2 + g], 8, 512, xsrc, rxn_l, TTL, cons_gate2(g))
                ysrc = lambda kt, t0, n: ya[:, kt, t0:t0 + n]
                rya_l = [[r] for r in rya]

                def cons_res(ntl, ti, t0, n, bk):
                    TT("dve", x[:, ntl, t0:t0 + n], x[:, ntl, t0:t0 + n], ps[:, bk, :n], ALU.add, [rx[ti], RB[bk]], [rx[ti]])

                tisA = [0]
                tisB = list(range(1, NTT))
                gffn_sel = lambda kt, l=l: gffn[:, kt, l:l + 1]
                dense(W["outA"], 8, 512, ysrc, rya_l, TTL, cons_res, tis=tisA)
                hk = {}

                def hk_stats(sel, hk=hk):
                    (hk["t"],) = rmsnorm_stats([TTL[0]], [rx[0]], sel)

                dense(W["outB"], 8, 512, ysrc, rya_l, TTL, cons_res, tis=tisB,
                      hooks={(0, 1): (lambda: hk_stats(gffn_sel)),
                             (1, 0): (lambda: rms_apply(TTL, rx, rxn, gffn_sel, 0, hk["t"]))})

                actb = carve(0, [22, NTM], BF16)
                ffin_passes = [(0, [0], 1), ("norm", None, None), (1, [0], 1), (0, tisB, 2), (1, tisB, 2)] + \
                              [(g, list(range(NTT)), 2) for g in range(2, 11)]
                for g, tis_g, la_g in ffin_passes:
                    if g == "norm":
                        rmsnorm_to_xn(TTL, rx, rxn, gffn_sel, tis=tisB)
                        continue
                    s = ws_use(W["ffin"][g], la_g)
                    wv = wview(s, 8, 512)
                    for ti in tis_g:
                        t0, n = TTL[ti]
                        for jj in range(2):
                            j = 2 * g + jj
                            b1 = bank()
                            for kt in range(8):
                                MM(ps[:, b1, :n], wv[:, kt, jj * 128:(jj + 1) * 128], xn[:, kt, t0:t0 + n], kt == 0, kt == 7,
                                   [RW[s], rxn[ti]], [RB[b1]])
                            b2 = bank()
                            for kt in range(8):
                                MM(ps[:, b2, :n], wv[:, kt, 256 + jj * 128:256 + (jj + 1) * 128], xn[:, kt, t0:t0 + n],
                                   kt == 0, kt == 7, [RW[s], rxn[ti]], [RB[b2]])
                            t = tmp()
                            ACT(tmpf[t][:, :n], ps[:, b1, :n], AF.Silu, [RB[b1]], [RT[t]])
                            TT("dve", actb[:, j, t0:t0 + n], tmpf[t][:, :n], ps[:, b2, :n], ALU.mult,
                               [RT[t], RB[b2]], [rar[ti], rar2[ti]])
                asrc = lambda kt, t0, n: actb[:, kt, t0:t0 + n]
                rar_l = [[r] for r in rar]
                dense(W["ffoutA"], 22, 128, asrc, rar_l, TTL, cons_res, tis=tisA)
                nxt_hooks = None
                if l + 1 < n_layers:
                    gnext = lambda kt, l=l: gmix[:, kt, l + 1:l + 2]
                    hk2 = {}

                    def hk2_stats(hk2=hk2, gnext=gnext):
                        (hk2["t"],) = rmsnorm_stats([TTL[0]], [rx[0]], gnext)

                    nxt_hooks = {(1, 0): hk2_stats,
                                 (4, 0): (lambda hk2=hk2, gnext=gnext: rms_apply(TTL, rx, rxn, gnext, 0, hk2["t"]))}
                dense(W["ffoutB"], 22, 128, asrc, rar_l, TTL, cons_res, tis=tisB, hooks=nxt_hooks)

            P.epoch += 1
            xo = ya[:, :, :].rearrange("p a b -> p (a b)").bitcast(F32)[:, 0:4096].rearrange("p (k t) -> p k t", t=512)
            rxo = [res("xo")] + list(rya)
            for ti, (t0, n) in enumerate(TTL):
                (t,) = rmsnorm_stats([TTL[ti]], [rx[ti]], None)
                for kt in range(8):
                    STT("dve", xo[:, kt, :n], x[:, kt, t0:t0 + n], gfin[:, kt, 0:1], tmpf[t][:, :n],
                        ALU.mult, ALU.mult, [rx[ti], RT[t], rp], rxo)
                nb = (n + 127) // 128
                for b in range(nb):
                    rows = min(128, n - b * 128)
                    to = tmp()
                    for q in range(2):
                        bk = bank()
                        for k4 in range(4):
                            kt = 4 * q + k4
                            TR(ps[:rows, bk, k4 * 128:(k4 + 1) * 128], xo[:, kt, b * 128:b * 128 + rows], ident_f[:],
                               rxo + [rc], [RB[bk]])
                        CP("act" if q == 0 else "dve", tmpf[to][:rows, q * 512:(q + 1) * 512], ps[:rows, bk, :], [RB[bk]], [RT[to]])
                    if ti < 2:
                        r0 = half * 1024 + t0 + b * 128
                        DMA("sp", D["yp"][r0:r0 + 128, :], tmpf[to][:rows, :], [RT[to]], [res("yp")], is_out=True)
                    else:
                        DMA("sp", D["ys"], tmpf[to][:rows, :], [RT[to]], [res("ys")], is_out=True)

        P.finish()
        P.emit(ctx)
    return nc


_NC_CACHE = {}


def _consts():
    k = {}
    k["k_ident"] = np.eye(128, dtype=np.float32)
    k["k_tri"] = np.triu(np.ones((128, 128), dtype=np.float32))
    k["k_svec"] = np.tile(np.arange(128, dtype=np.float32)[None, :], (128, 1))
    k["k_pidx"] = np.arange(128, dtype=np.float32)[:, None].copy()
    par = np.zeros((128, 2), dtype=np.float32)
    gl = (np.arange(128) // 16) % 2
    par[:, 0] = (gl == 0)
    par[:, 1] = (gl == 1)
    k["k_par"] = par
    return k


def kernel(x_prompt, x_sample, state_ssm_re, state_ssm_im, state_conv,
           g_mix, w_in, ssm_a_re, ssm_a_im, ssm_log_dt, ssm_b_re, ssm_b_im,
           ssm_c_re, ssm_c_im, ssm_d, w_glu, conv_w, conv_b, conv_ln_g, conv_ln_b,
           w_pw, w_out, g_ffn, w_ff_in, w_ff_out, g_final):
    f = lambda a: np.ascontiguousarray(np.asarray(a, dtype=np.float32))
    if "nc" not in _NC_CACHE:
        _NC_CACHE["nc"] = build()
    nc = _NC_CACHE["nc"]
    shared = dict(
        g_mix=f(g_mix), w_in=f(w_in), a_re=f(ssm_a_re).reshape(4, 2048), a_im=f(ssm_a_im).reshape(4, 2048),
        log_dt=f(ssm_log_dt), b_re=f(ssm_b_re).reshape(4, 2048, 16), b_im=f(ssm_b_im).reshape(4, 2048, 16),
        c_re=f(ssm_c_re).reshape(4, 512, 64), c_im=f(ssm_c_im).reshape(4, 512, 64), ssm_d=f(ssm_d),
        w_glu=f(w_glu), conv_w=f(conv_w), conv_b=f(conv_b), ln_g=f(conv_ln_g), ln_b=f(conv_ln_b),
        w_pw=f(w_pw), w_out=f(w_out), g_ffn=f(g_ffn), w_ff_in=f(w_ff_in), w_ff_out=f(w_ff_out),
        g_final=f(g_final).reshape(1, 1024),
    )
    shared.update(_consts())
    xp = f(x_prompt); xs = f(x_sample).reshape(128, 1024)
    sre = f(state_ssm_re).reshape(4, 128, 2048); sim = f(state_ssm_im).reshape(4, 128, 2048)
    scv = f(state_conv).reshape(4, 128, 30, 512)
    in_maps = []
    for c in range(8):
        m = dict(shared)
        m["xp"] = xp[c]
        m["xs"] = np.ascontiguousarray(xs[16 * c:16 * c + 16])
        m["sre"] = np.ascontiguousarray(sre[:, 16 * c:16 * c + 16])
        m["sim"] = np.ascontiguousarray(sim[:, 16 * c:16 * c + 16])
        m["scv"] = np.ascontiguousarray(scv[:, 16 * c:16 * c + 16].reshape(4, 480, 512))
        in_maps.append(m)
    res = run_bass_kernel_spmd(nc, in_maps, core_ids=list(range(8)))
    r = res.results
    y_prompt = np.stack([r[c]["yp"] for c in range(8)], 0).astype(np.float32)
    y_sample = np.concatenate([r[c]["ys"] for c in range(8)], 0).reshape(128, 1, 1024).astype(np.float32)
    re_p = np.stack([r[c]["rep"].reshape(4, 32, 64) for c in range(8)], 1).astype(np.float32)
    im_p = np.stack([r[c]["imp"].reshape(4, 32, 64) for c in range(8)], 1).astype(np.float32)
    conv_p = np.stack([r[c]["cvp"] for c in range(8)], 1).astype(np.float32)
    re_s = np.concatenate([r[c]["res"].reshape(4, 16, 32, 64) for c in range(8)], 1).astype(np.float32)
    im_s = np.concatenate([r[c]["ims"].reshape(4, 16, 32, 64) for c in range(8)], 1).astype(np.float32)
    conv_s = np.concatenate([r[c]["cvs"] for c in range(8)], 1).astype(np.float32)
    return (y_prompt, y_sample, re_p, im_p, conv_p, re_s, im_s, conv_s)
```

```python
import math
import os
from contextlib import ExitStack
import numpy as np
import concourse.bass as bass
import concourse.mybir as mybir
from concourse.bass_utils import run_bass_kernel_spmd

F32 = mybir.dt.float32
BF16 = mybir.dt.bfloat16
ALU = mybir.AluOpType
AF = mybir.ActivationFunctionType

ENGS = ("pe", "act", "dve", "pool", "sp")
DMA_Q = ("sp", "act", "pool")
NDMA_SEM = 16

DEPTH = 4
EPS = 1e-6
TWO_PI = 2.0 * math.pi
C1 = 6.28125
C2 = TWO_PI - 6.28125
MAGIC = 12582912.0


class Res:
    __slots__ = ("name", "lw", "rd")

    def __init__(self, name):
        self.name = name
        self.lw = None
        self.rd = {}


class Prog:
    def __init__(self, nc):
        self.nc = nc
        self.ins = {e: [] for e in ENGS}
        self.ndma = {q: 0 for q in DMA_Q}
        self.dmas = []
        self.epoch = 0
        self.all_out_dmas = []
        self._dma_id_of = {}

    def _deps(self, reads, writes):
        deps = set()
        for r in reads:
            if r.lw is not None:
                deps.add(r.lw)
        for w in writes:
            if w.lw is not None:
                deps.add(w.lw)
            for k, v in w.rd.items():
                if isinstance(k, tuple):
                    deps.add(k)
                else:
                    deps.add((k, v))
        return deps

    def op(self, eng, fn, reads=(), writes=()):
        idx = len(self.ins[eng])
        deps = self._deps(reads, writes)
        self.ins[eng].append(dict(fn=fn, deps=deps, dma=None, epoch=self.epoch))
        for r in reads:
            r.rd[eng] = idx
        for w in writes:
            w.lw = (eng, idx)
            w.rd = {}
        return (eng, idx)

    def dma(self, q, fn, reads=(), writes=(), is_out=False):
        deps = self._deps(reads, writes)
        n = self.ndma[q]
        self.ndma[q] += 1
        did = len(self.dmas)
        self.dmas.append((q, n))
        if n >= NDMA_SEM:
            deps.add(("dma", self._dma_id_of[(q, n - NDMA_SEM)]))
        self._dma_id_of[(q, n)] = did
        self.ins[q].append(dict(fn=fn, deps=deps, dma=did, epoch=self.epoch))
        key = ("dma", did)
        for r in reads:
            r.rd[key] = True
        for w in writes:
            w.lw = key
            w.rd = {}
        if is_out:
            self.all_out_dmas.append(key)
        return key

    def barrier(self):
        deps = set()
        for e in ENGS:
            if self.ins[e]:
                deps.add((e, len(self.ins[e]) - 1))
        for q in DMA_Q:
            for n in range(max(0, self.ndma[q] - NDMA_SEM), self.ndma[q]):
                deps.add(("dma", self._dma_id_of[(q, n)]))
        for e in ENGS:
            self.ins[e].append(dict(fn=None, deps=set(deps), dma=None, epoch=self.epoch))

    def finish(self, eng="sp"):
        self.ins[eng].append(dict(fn=None, deps=set(self.all_out_dmas), dma=None, epoch=self.epoch))

    def emit(self, ctx):
        nc = self.nc
        signal = {e: set() for e in ENGS}
        for e in ENGS:
            for i, ins in enumerate(self.ins[e]):
                for d in ins["deps"]:
                    if d[0] != "dma":
                        if d[0] == "pe" and e == "pe":
                            continue
                        if d[0] == e and d[1] >= i:
                            continue
                        signal[d[0]].add(d[1])
        for e in ENGS:
            fixed = set()
            for i in signal[e]:
                j = i
                while j >= 0 and (self.ins[e][j]["fn"] is None or self.ins[e][j]["dma"] is not None):
                    j -= 1
                fixed.add((i, j))
            signal[e] = fixed
        rank = {}
        sigidx = {e: {} for e in ENGS}
        for e in ENGS:
            cnt = {}
            seen = {}
            for i, j in sorted(signal[e], key=lambda t: (t[1], t[0])):
                if j < 0:
                    rank[(e, i)] = None
                    continue
                ep = self.ins[e][j]["epoch"]
                if j not in seen:
                    cnt[ep] = cnt.get(ep, 0) + 1
                    seen[j] = (ep, cnt[ep])
                    sigidx[e][j] = ep
                rank[(e, i)] = seen[j]
        sems = {}
        for e in ENGS:
            for ep in sorted(set(sigidx[e].values())):
                sems[(e, ep)] = ctx.enter_context(nc.semaphore(f"s_{e}_{ep}"))
        dsem = {}
        for q in DMA_Q:
            for k in range(min(NDMA_SEM, self.ndma[q])):
                dsem[(q, k)] = ctx.enter_context(nc.semaphore(f"d_{q}_{k}"))
        prog = self

        def run_engine(e, eng):
            waited = {}
            dwaited = set()
            for i, ins in enumerate(prog.ins[e]):
                need = {}
                dneed = []
                for d in ins["deps"]:
                    if d[0] == "dma":
                        if d[1] not in dwaited:
                            dneed.append(d[1])
                    else:
                        te, ti = d
                        if te == "pe" and e == "pe":
                            continue
                        if te == e and ti >= i:
                            continue
                        if waited.get(te, -1) >= ti:
                            continue
                        need[te] = max(need.get(te, -1), ti)
                for te, ti in need.items():
                    rk = rank[(te, ti)]
                    if rk is not None:
                        eng.wait_ge(sems[(te, rk[0])], rk[1])
                    waited[te] = ti
                for did in sorted(dneed):
                    q, n = prog.dmas[did]
                    eng.wait_ge(dsem[(q, n % NDMA_SEM)], 16 * (n // NDMA_SEM + 1))
                    dwaited.add(did)
                if ins["fn"] is None:
                    continue
                r = ins["fn"](eng)
                if ins["dma"] is not None:
                    q, n = prog.dmas[ins["dma"]]
                    r.then_inc(dsem[(q, n % NDMA_SEM)], 16)
                elif i in sigidx[e]:
                    r.then_inc(sems[(e, sigidx[e][i])], 1)

        with nc.Block() as block:
            @block.tensor
            def _(eng):
                run_engine("pe", eng)

            @block.scalar
            def _(eng):
                run_engine("act", eng)

            @block.vector
            def _(eng):
                run_engine("dve", eng)

            @block.gpsimd
            def _(eng):
                run_engine("pool", eng)

            @block.sync
            def _(eng):
                run_engine("sp", eng)


def build(n_layers=DEPTH, n_halves=2):
    nc = bass.Bass("TRN2", target_bir_lowering=False)
    D = {}

    def inp(name, shape):
        D[name] = nc.dram_tensor(name, shape, F32, kind="ExternalInput").ap()

    def outp(name, shape):
        D[name] = nc.dram_tensor(name, shape, F32, kind="ExternalOutput").ap()

    inp("xp", [2048, 1024]); inp("xs", [16, 1024])
    inp("sre", [4, 16, 2048]); inp("sim", [4, 16, 2048]); inp("scv", [4, 480, 512])
    inp("g_mix", [4, 1024]); inp("w_in", [4, 1024, 3584])
    inp("a_re", [4, 2048]); inp("a_im", [4, 2048]); inp("log_dt", [4, 32])
    inp("b_re", [4, 2048, 16]); inp("b_im", [4, 2048, 16])
    inp("c_re", [4, 512, 64]); inp("c_im", [4, 512, 64]); inp("ssm_d", [4, 512])
    inp("w_glu", [4, 512, 2048]); inp("conv_w", [4, 31, 512]); inp("conv_b", [4, 512])
    inp("ln_g", [4, 512]); inp("ln_b", [4, 512]); inp("w_pw", [4, 512, 1024])
    inp("w_out", [4, 1024, 1024]); inp("g_ffn", [4, 1024]); inp("w_ff_in", [4, 1024, 5632])
    inp("w_ff_out", [4, 2816, 1024]); inp("g_final", [1, 1024])
    inp("k_ident", [128, 128]); inp("k_tri", [128, 128]); inp("k_svec", [128, 128])
    inp("k_pidx", [128, 1]); inp("k_par", [128, 2])
    outp("yp", [2048, 1024]); outp("ys", [16, 1024])
    outp("rep", [4, 16, 128]); outp("imp", [4, 16, 128]); outp("cvp", [4, 30, 512])
    outp("res", [4, 16, 2048]); outp("ims", [4, 16, 2048]); outp("cvs", [4, 16, 30, 512])
    scrB = nc.dram_tensor("scrB", [4, 128, 4096], BF16).ap()
    scrC = nc.dram_tensor("scrC", [4, 128, 4096], BF16).ap()
    scrR = nc.dram_tensor("scrR", [4, 2, 128, 2048], F32).ap()
    scrP = nc.dram_tensor("scrP", [4, 2, 128, 2048], F32).ap()

    with ExitStack() as ctx:
        def sb(name, shape, dt):
            return ctx.enter_context(nc.sbuf_tensor(name, shape, dt))

        P = Prog(nc)
        NTM = 1040
        x = sb("x", [128, 8, NTM], F32)
        xn = sb("xn", [128, 8, NTM], BF16)
        u = sb("u", [128, 4, NTM], BF16)
        cbuf = sb("cbuf", [128, 4, 1054], BF16)
        ya = sb("ya", [128, 8, NTM], BF16)
        arena = sb("arena", [128, 16448], F32)
        wsl = [sb(f"wsl{i}", [128, 4096], BF16) for i in range(3)]
        tmpf = [sb(f"tmpf{i}", [128, 1024], F32) for i in range(4)]
        sqb = sb("sqb", [128, 8, 512], BF16)
        ps = ctx.enter_context(nc.psum_tensor("ps", [128, 8, 512], F32))
        ident_f = sb("ident_f", [128, 128], F32)
        ident_b = sb("ident_b", [128, 128], BF16)
        tri_b = sb("tri_b", [128, 128], BF16)
        ones1024 = sb("ones1024", [128, 128], BF16)
        ones512 = sb("ones512", [128, 128], BF16)
        svec = sb("svec", [128, 128], F32)
        pidx = sb("pidx", [128, 1], F32)
        par = sb("par", [128, 2], F32)
        gmix = sb("gmix", [128, 8, 4], F32)
        gffn = sb("gffn", [128, 8, 4], F32)
        gfin = sb("gfin", [128, 8, 1], F32)
        dpar = sb("dpar", [128, 4, 4], F32)
        convb = sb("convb", [128, 4, 4], F32)
        lng = sb("lng", [128, 4, 4], F32)
        lnb = sb("lnb", [128, 4, 4], F32)
        convw = sb("convw", [128, 4, 4, 31], F32)
        are = sb("are", [128, 16, 4], F32)
        aim = sb("aim", [128, 16, 4], F32)
        ldt = sb("ldt", [128, 4, 16], F32)
        sm = {n: sb("sm_" + n, [128, 4, 16], F32) for n in
              ["dt", "dr", "th", "mag", "cs", "sn", "lre", "lim", "t1", "t2", "t3", "t4", "qre", "qim", "den"]}
        Hst = sb("Hst", [128, 4, 2, 16], F32)
        carry = sb("carry", [128, 2, 16], F32)
        hsm = [sb(f"hsm{i}", [128, 2, 16], F32) for i in range(5)]
        ctail = sb("ctail", [128, 4, 4, 30], BF16)
        cnew32 = sb("cnew32", [128, 4, 16], F32)
        ctail32 = sb("ctail32", [128, 4, 30], F32)
        hsp = ya[:, 0, 0:1024].bitcast(F32).rearrange("p (r j s) -> p r j s", r=2, j=16)
        hsn = ya[:, 1, 0:1024].bitcast(F32).rearrange("p (r j s) -> p r j s", r=2, j=16)
        hsb = ya[:, 2, 0:512].rearrange("p (r j s) -> p r j s", r=2, j=16)
        bus = sqb[:16, :, :].rearrange("p a b -> p (a b)").rearrange("p (r c) -> p r c", r=2)

        R = {}

        def res(name):
            if name not in R:
                R[name] = Res(name)
            return R[name]

        RB = [res(f"bank{i}") for i in range(8)]
        RW = [res(f"wsl{i}") for i in range(3)]
        RT = [res(f"tmpf{i}") for i in range(4)]
        RA = res("arena_generic")
        st = dict(bank=0, wsl=0, tmp=0)

        def bank():
            b = st["bank"]; st["bank"] = (b + 1) % 8
            return b

        def tmp():
            t = st["tmp"]; st["tmp"] = (t + 1) % 4
            return t

        def MM(out, lhsT, rhs, start, stop, reads, writes):
            P.op("pe", lambda e, o=out, l=lhsT, r=rhs, s=start, t=stop:
                 e.matmul(o, lhsT=l, rhs=r, start=s, stop=t), reads, writes)

        def TR(out, in_, idn, reads, writes):
            P.op("pe", lambda e, o=out, i=in_, d=idn: e.transpose(o, i, d), reads, writes)

        def ACT(out, in_, func, reads, writes, **kw):
            P.op("act", lambda e, o=out, i=in_, f=func, k=kw: e.activation(out=o, in_=i, func=f, **k), reads, writes)

        def TT(eng, out, in0, in1, op, reads, writes):
            P.op(eng, lambda e, o=out, a=in0, b=in1, p=op: e.tensor_tensor(out=o, in0=a, in1=b, op=p), reads, writes)

        def TS(eng, out, in0, s1, s2, op0, op1, reads, writes):
            if s2 is None:
                P.op(eng, lambda e, o=out, a=in0, x1=s1, p0=op0:
                     e.tensor_scalar(out=o, in0=a, scalar1=x1, scalar2=None, op0=p0), reads, writes)
            else:
                P.op(eng, lambda e, o=out, a=in0, x1=s1, x2=s2, p0=op0, p1=op1:
                     e.tensor_scalar(out=o, in0=a, scalar1=x1, scalar2=x2, op0=p0, op1=p1), reads, writes)

        def STT(eng, out, in0, scalar, in1, op0, op1, reads, writes):
            P.op(eng, lambda e, o=out, a=in0, s=scalar, b=in1, p0=op0, p1=op1:
                 e.scalar_tensor_tensor(out=o, in0=a, scalar=s, in1=b, op0=p0, op1=p1), reads, writes)

        def CP(eng, out, in_, reads, writes):
            if eng == "act":
                P.op("act", lambda e, o=out, i=in_: e.copy(out=o, in_=i), reads, writes)
            else:
                P.op(eng, lambda e, o=out, i=in_: e.tensor_copy(out=o, in_=i), reads, writes)

        def MEMSET(eng, ap, val, writes):
            P.op(eng, lambda e, a=ap, v=val: e.memset(a, v), (), writes)

        def DMA(q, out, in_, reads, writes, is_out=False, slow=False):
            def fn(e, o=out, i=in_, s=slow):
                if s:
                    with nc.allow_non_contiguous_dma(reason="small strided parameter/state transfer"):
                        return e.dma_start(out=o, in_=i)
                return e.dma_start(out=o, in_=i)
            P.dma(q, fn, reads, writes, is_out=is_out)

        def RECIP(out, in_, reads, writes):
            P.op("dve", lambda e, o=out, i=in_: e.reciprocal(out=o, in_=i), reads, writes)

        def carve(off, shape, dt):
            n = int(np.prod(shape))
            nb = n * (4 if dt == F32 else 2)
            assert off % 4 == 0 and nb % 4 == 0 and off + nb <= 65792, (off, shape)
            a = arena[:, off // 4:(off + nb) // 4]
            if dt != F32:
                a = a.bitcast(dt)
            if len(shape) == 2:
                a = a.rearrange("p (a b) -> p a b", b=shape[1])
            elif len(shape) == 3:
                a = a.rearrange("p (a b c) -> p a b c", b=shape[1], c=shape[2])
            return a

        rc = res("consts")
        DMA("sp", ident_f[:], D["k_ident"], (), [rc])
        DMA("sp", svec[:], D["k_svec"], (), [rc])
        DMA("sp", pidx[:], D["k_pidx"], (), [rc])
        DMA("sp", par[:], D["k_par"], (), [rc])
        DMA("pool", ident_b[:], D["k_ident"], (), [rc])
        DMA("pool", tri_b[:], D["k_tri"], (), [rc])
        MEMSET("dve", ones1024[:], 1.0 / 1024.0, [rc])
        MEMSET("dve", ones512[:], 1.0 / 512.0, [rc])
        MEMSET("dve", Hst[:], 0.0, [res("Hst")])
        MEMSET("dve", ctail[:], 0.0, [res("ctail")])

        def load_T(src, rows, ncols, dst_fn, wres, eng_alt=[0]):
            t = tmp()
            DMA("sp", tmpf[t][:rows, :ncols], src, (), [RT[t]])
            for b in range(ncols // 128):
                bk = bank()
                TR(ps[:, bk, :rows], tmpf[t][:rows, b * 128:(b + 1) * 128], ident_f[:rows, :rows],
                   [RT[t], rc], [RB[bk]])
                eng = "act" if (eng_alt[0] % 2 == 0) else "dve"
                eng_alt[0] += 1
                CP(eng, dst_fn(b), ps[:, bk, :rows], [RB[bk]], wres)

        KSTOP = int(os.environ.get("KSTOP", "99"))
        if KSTOP == 1:
            P.finish(); P.emit(ctx); return nc
        rp = res("params")
        load_T(D["g_mix"], 4, 1024, lambda b: gmix[:, b, :], [rp])
        load_T(D["g_ffn"], 4, 1024, lambda b: gffn[:, b, :], [rp])
        load_T(D["g_final"], 1, 1024, lambda b: gfin[:, b, :], [rp])
        load_T(D["ssm_d"], 4, 512, lambda b: dpar[:, b, :], [rp])
        load_T(D["conv_b"], 4, 512, lambda b: convb[:, b, :], [rp])
        load_T(D["ln_g"], 4, 512, lambda b: lng[:, b, :], [rp])
        load_T(D["ln_b"], 4, 512, lambda b: lnb[:, b, :], [rp])
        for l in range(4):
            load_T(D["conv_w"][l], 31, 512, lambda b, l=l: convw[:, l, b, :], [rp])
        for h2 in range(2):
            load_T(D["a_re"][:, h2 * 1024:(h2 + 1) * 1024], 4, 1024, lambda b, h2=h2: are[:, h2 * 8 + b, :], [rp])
            load_T(D["a_im"][:, h2 * 1024:(h2 + 1) * 1024], 4, 1024, lambda b, h2=h2: aim[:, h2 * 8 + b, :], [rp])
        if KSTOP == 2:
            P.finish(); P.emit(ctx); return nc
        t_ld = tmp()
        DMA("sp", tmpf[t_ld][:, 0:128], D["log_dt"].rearrange("l g -> (l g)").partition_broadcast(128), (), [RT[t_ld]])
        for g2 in range(2):
            srcv = tmpf[t_ld][64 * g2:64 * g2 + 64, 0:128].rearrange("p (l j two) -> p l j two", l=4, two=2)[:, :, :, g2]
            CP("dve", ldt[64 * g2:64 * g2 + 64, :, :], srcv, [RT[t_ld]], [rp])

        def load_x(half, rx):
            blocks = [(D["xp"][half * 1024 + b * 128: half * 1024 + (b + 1) * 128, :], 128, b * 128) for b in range(8)]
            if half == 0:
                blocks.append((D["xs"], 16, 1024))
            for bi, (src, rows, c0) in enumerate(blocks):
                t = tmp()
                DMA("sp", tmpf[t][:rows, :], src, (), [RT[t]])
                ti = c0 // 512
                for q in range(2):
                    bk = bank()
                    for k4 in range(4):
                        kt = 4 * q + k4
                        TR(ps[:, bk, k4 * rows:(k4 + 1) * rows], tmpf[t][:rows, kt * 128:(kt + 1) * 128],
                           ident_f[:rows, :rows], [RT[t], rc], [RB[bk]])
                    CP("act" if q == 0 else "dve", x[:, 4 * q:4 * q + 4, c0:c0 + rows],
                       ps[:, bk, :4 * rows].rearrange("p (k t) -> p k t", t=rows), [RB[bk]], [rx[ti]])


        rx0 = [res(f"x{ti}") for ti in range(3)]
        if n_halves > 0:
            load_x(0, rx0)
        if KSTOP == 3:
            P.finish(); P.emit(ctx); return nc
        rs = res("s5small")

        def sincos(eng, A, n, cs_out, sn_out, t_a, t_b, rA, rO):
            TS(eng, t_a, A, 1.0 / TWO_PI, MAGIC, ALU.mult, ALU.add, rA, rO)
            TS(eng, t_a, t_a, -MAGIC, None, ALU.add, None, rO, rO)
            STT(eng, t_b, t_a, -C1, A, ALU.mult, ALU.add, rA + rO, rO)
            STT(eng, t_b, t_a, -C2, t_b, ALU.mult, ALU.add, rO, rO)
            TS(eng, t_b, t_b, -math.pi, math.pi, ALU.max, ALU.min, rO, rO)
            ACT(sn_out, t_b, AF.Sin, rO, rO)
            ACT(t_a, t_b, AF.Sin, rO, rO, scale=0.5)
            TT(eng, t_a, t_a, t_a, ALU.mult, rO, rO)
            TS(eng, cs_out, t_a, -2.0, 1.0, ALU.mult, ALU.add, rO, rO)

        halfpi = sb("halfpi", [128, 1], F32)
        epsc = sb("epsc", [128, 1], F32)
        MEMSET("dve", halfpi[:], math.pi / 2.0, [rc])
        MEMSET("dve", epsc[:], EPS, [rc])

        S = {k: v[:] for k, v in sm.items()}
        are_v = are[:].rearrange("p j l -> p l j")
        aim_v = aim[:].rearrange("p j l -> p l j")
        ACT(S["dt"], ldt[:], AF.Exp, [rp], [rs])
        TT("dve", S["dr"], S["dt"], are_v, ALU.mult, [rs, rp], [rs])
        TT("dve", S["th"], S["dt"], aim_v, ALU.mult, [rs, rp], [rs])
        ACT(S["mag"], S["dr"], AF.Exp, [rs], [rs])
        sincos("dve", S["th"], 64, S["cs"], S["sn"], S["t1"], S["t2"], [rs], [rs])
        TT("dve", S["lre"], S["mag"], S["cs"], ALU.mult, [rs], [rs])
        TT("dve", S["lim"], S["mag"], S["sn"], ALU.mult, [rs], [rs])
        TS("dve", S["t1"], S["lre"], -1.0, None, ALU.add, None, [rs], [rs])
        TT("dve", S["t2"], are_v, are_v, ALU.mult, [rs, rp], [rs])
        TT("dve", S["t3"], aim_v, aim_v, ALU.mult, [rs, rp], [rs])
        TT("dve", S["den"], S["t2"], S["t3"], ALU.add, [rs], [rs])
        RECIP(S["den"], S["den"], [rs], [rs])
        TT("dve", S["t2"], S["t1"], are_v, ALU.mult, [rs, rp], [rs])
        TT("dve", S["t3"], S["lim"], aim_v, ALU.mult, [rs, rp], [rs])
        TT("dve", S["t2"], S["t2"], S["t3"], ALU.add, [rs], [rs])
        TT("dve", S["qre"], S["t2"], S["den"], ALU.mult, [rs], [rs])
        TT("dve", S["t2"], S["lim"], are_v, ALU.mult, [rs, rp], [rs])
        TT("dve", S["t3"], S["t1"], aim_v, ALU.mult, [rs, rp], [rs])
        TT("dve", S["t2"], S["t2"], S["t3"], ALU.subtract, [rs], [rs])
        TT("dve", S["qim"], S["t2"], S["den"], ALU.mult, [rs], [rs])

        A_ang = carve(0, [2048], F32)
        A_ta = carve(8192, [2048], F32)
        A_tb = carve(16384, [2048], F32)
        A_cs = carve(24576, [2048], F32)
        A_sn = carve(32768, [2048], F32)
        A_mg = carve(40960, [2048], F32)
        BpadRe = carve(49152, [16, 128], F32)
        BpadIm = carve(57344, [16, 128], F32)
        rAng, rTa, rTb, rCs, rSn, rMg, rBpR, rBpI = [res("prep_" + n_) for n_ in ("ang", "ta", "tb", "cs", "sn", "mg", "bpr", "bpi")]
        rSC = [rTa, rTb, rCs, rSn]
        MEMSET("pool", BpadRe, 0.0, [rBpR])
        MEMSET("pool", BpadIm, 0.0, [rBpI])
        n_prep = n_layers
        for l in range(n_prep):
            rl = [RA]
            ang3 = A_ang.rearrange("p (j s) -> p j s", s=128)
            sv_bc = svec[:].unsqueeze(1).to_broadcast([128, 16, 128])
            th_bc = sm["th"][:, l, :].unsqueeze(2).to_broadcast([128, 16, 128])
            dr_bc = sm["dr"][:, l, :].unsqueeze(2).to_broadcast([128, 16, 128])
            TT("dve", ang3, sv_bc, th_bc, ALU.mult, [rc, rs], [rAng])
            sincos("dve", A_ang, 2048, A_cs, A_sn, A_ta, A_tb, [rAng], rSC)
            TT("dve", A_ta.rearrange("p (j s) -> p j s", s=128), sv_bc, dr_bc, ALU.mult, [rc, rs], [rTa])
            ACT(A_mg, A_ta, AF.Exp, [rTa], [rMg])
            TT("dve", A_cs, A_cs, A_mg, ALU.mult, [rCs, rMg], [rCs])
            TT("dve", A_sn, A_sn, A_mg, ALU.mult, [rSn, rMg], [rSn])
            DMA("sp", scrP[l, 0], A_cs, [rCs], [res("scr")])
            DMA("sp", scrP[l, 1], A_sn, [rSn], [res("scr")])
            ACT(A_ang, A_mg, AF.Copy, [rMg], [rAng])
            P.op("dve", lambda e, o=A_mg, i=A_ang: e.reciprocal(out=o, in_=i), [rAng], [rMg])
            TT("dve", A_mg, A_mg, A_mg, ALU.mult, [rMg], [rMg])
            TT("dve", A_ta, A_cs, A_mg, ALU.mult, [rCs, rMg], [rTa])
            STT("dve", A_tb, A_sn, -1.0, A_mg, ALU.mult, ALU.mult, [rSn, rMg], [rTb])
            for (Qsrc, rQ, ri_) in ((A_ta, rTa, 0), (A_tb, rTb, 1)):
                for q4 in range(4):
                    bk = bank()
                    for jm in range(4):
                        j = 4 * q4 + jm
                        TR(ps[:, bk, jm * 128:(jm + 1) * 128], Qsrc[:, j * 128:(j + 1) * 128], ident_f[:], [rQ, rc], [RB[bk]])
                    CP("act" if q4 % 2 == 0 else "dve", A_ang[:, q4 * 512:(q4 + 1) * 512], ps[:, bk, :], [RB[bk]], [rAng])
                DMA("sp", scrR[l, ri_], A_ang, [rAng], [res("scr")])
            Braw_re = A_ta.rearrange("p (a b) -> p a b", b=128)[:, :, 0:16]
            Braw_im = A_tb.rearrange("p (a b) -> p a b", b=128)[:, :, 0:16]
            Bb_re = A_ta.rearrange("p (a b) -> p a b", b=128)[:, :, 16:32]
            Bb_im = A_tb.rearrange("p (a b) -> p a b", b=128)[:, :, 16:32]
            Bt1 = A_ta.rearrange("p (a b) -> p a b", b=128)[:, :, 32:48]
            Bt2 = A_tb.rearrange("p (a b) -> p a b", b=128)[:, :, 32:48]
            DMA("sp", Braw_re, D["b_re"][l].rearrange("(j i) c -> i j c", i=128), (), [rTa])
            DMA("sp", Braw_im, D["b_im"][l].rearrange("(j i) c -> i j c", i=128), (), [rTb])
            qre_bc = sm["qre"][:, l, :].unsqueeze(2).to_broadcast([128, 16, 16])
            qim_bc = sm["qim"][:, l, :].unsqueeze(2).to_broadcast([128, 16, 16])
            TT("dve", Bt1, Braw_re, qre_bc, ALU.mult, [rTa, rTb] + [rs], [rTa, rTb])
            TT("dve", Bt2, Braw_im, qim_bc, ALU.mult, [rTa, rTb] + [rs], [rTa, rTb])
            TT("dve", Bb_re, Bt1, Bt2, ALU.subtract, [rTa, rTb], [rTa, rTb])
            TT("dve", Bt1, Braw_im, qre_bc, ALU.mult, [rTa, rTb] + [rs], [rTa, rTb])
            TT("dve", Bt2, Braw_re, qim_bc, ALU.mult, [rTa, rTb] + [rs], [rTa, rTb])
            TT("dve", Bb_im, Bt1, Bt2, ALU.add, [rTa, rTb], [rTa, rTb])
            for (Bb, Bpad, rBp) in ((Bb_re, BpadRe, rBpR), (Bb_im, BpadIm, rBpI)):
                for g2 in range(2):
                    for jm in range(4):
                        c0 = 16 * (2 * jm + g2)
                        CP("dve", Bpad[64 * g2:64 * g2 + 64, jm::4, c0:c0 + 16],
                           Bb[64 * g2:64 * g2 + 64, jm::4, :], [rTa, rTb], [rBp])
            Bm_sb = A_cs.bitcast(BF16).rearrange("p (k r c) -> p k r c", k=4, r=2)
            for ri, (Bpad, rBp) in enumerate(((BpadRe, rBpR), (BpadIm, rBpI))):
                for kt in range(4):
                    bk = bank()
                    for jm in range(4):
                        TR(ps[:, bk, jm * 128:(jm + 1) * 128], Bpad[:, 4 * kt + jm, :], ident_f[:], [rBp, rc], [RB[bk]])
                    CP("act", Bm_sb[:, kt, ri, :], ps[:, bk, :], [RB[bk]], [rCs])
            DMA("sp", scrB[l], A_cs.bitcast(BF16), [rCs], [res("scr")])
            Craw = A_sn.rearrange("p (k q) -> p k q", q=512)
            Cm_sb = A_mg.bitcast(BF16).rearrange("p (r j c) -> p r j c", r=2, j=16)
            MEMSET("pool", A_mg, 0.0, [rMg])
            for ri, nm in enumerate(("c_re", "c_im")):
                DMA("sp", Craw[:, :, 0:64], D[nm][l].rearrange("(k r) q -> r k q", r=128), (), [rSn])
                for g2 in range(2):
                    TS("dve", Craw[:, :, 128 + 64 * g2:128 + 64 * g2 + 64], Craw[:, :, 0:64], par[:, g2:g2 + 1], None,
                       ALU.mult, None, [rSn, rc], [rSn])
                bk = bank()
                for kt in range(4):
                    TR(ps[:, bk, kt * 128:(kt + 1) * 128], Craw[:, kt, 128:256], ident_f[:], [rSn, rc], [RB[bk]])
                for kt in range(4):
                    for jm in range(4):
                        src = ps[:, bk, kt * 128 + 32 * jm:kt * 128 + 32 * jm + 32]
                        dst = Cm_sb[:, ri, 4 * kt + jm, 32 * jm:32 * jm + 32]
                        if ri == 0:
                            CP("act", dst, src, [RB[bk]], [rMg])
                        else:
                            P.op("act", lambda e, o=dst, i=src: e.mul(out=o, in_=i, mul=-1.0), [RB[bk]], [rMg])
            DMA("sp", scrC[l], A_mg.bitcast(BF16), [rMg], [res("scr")])
        P.barrier()

        def load_w(dst_slot, src_ap, kt_n, ncols, col_off=0, slot_cols=None):
            sc = slot_cols if slot_cols is not None else ncols
            dst = wsl[dst_slot][:, 0:kt_n * sc].rearrange("p (k c) -> p k c", c=sc)[:, :, col_off:col_off + ncols]
            DMA("pool", dst, src_ap.rearrange("(k p) c -> p k c", p=128), (), [RW[dst_slot]])

        def next_slot():
            s = st["wsl"]; st["wsl"] = (s + 1) % 3
            return s

        def wview(slot, kt_n, sc):
            return wsl[slot][:, 0:kt_n * sc].rearrange("p (k c) -> p k c", c=sc)

        def rmsnorm_stats(TTL, rx, gcol):
            out = []
            for ti, (t0, n) in enumerate(TTL):
                ACT(sqb[:, :, :n], x[:, :, t0:t0 + n], AF.Square, [rx[ti]], [res("sqb")])
                bk = bank()
                for kt in range(8):
                    MM(ps[:, bk, :n], ones1024[:], sqb[:, kt, :n], kt == 0, kt == 7, [res("sqb"), rc], [RB[bk]])
                t = tmp()
                ACT(tmpf[t][:, :n], ps[:, bk, :n], AF.Sqrt, [RB[bk], rc], [RT[t]], bias=epsc[:, 0:1])
                RECIP(tmpf[t][:, :n], tmpf[t][:, :n], [RT[t]], [RT[t]])
                out.append(t)
            return out

        def rms_apply(TTL, rx, rxn, gsel, ti, t):
            t0, n = TTL[ti]
            for kt in range(8):
                STT("dve", xn[:, kt, t0:t0 + n], x[:, kt, t0:t0 + n], gsel(kt), tmpf[t][:, :n],
                    ALU.mult, ALU.mult, [rx[ti], RT[t], rp], [rxn[ti]])

        def rmsnorm_to_xn(TTL, rx, rxn, gsel, tis=None):
            for ti, (t0, n) in enumerate(TTL):
                if tis is not None and ti not in tis:
                    continue
                (t,) = rmsnorm_stats([TTL[ti]], [rx[ti]], gsel)
                rms_apply(TTL, rx, rxn, gsel, ti, t)

        WS = dict(q=[])

        def ws_add(fn):
            WS["q"].append([fn, None])
            return len(WS["q"]) - 1

        def ws_use(i, la=2):
            for k in range(i, min(i + 1 + la, len(WS["q"]))):
                if WS["q"][k][1] is None:
                    sl = next_slot()
                    WS["q"][k][0](sl)
                    WS["q"][k][1] = sl
            return WS["q"][i][1]

        def decl_dense(wsrc, kt_n, col0, ncols, grp):
            ids = []
            for g0 in range(0, ncols, grp):
                ids.append(ws_add(lambda sl, a=wsrc[:, col0 + g0:col0 + g0 + grp], k=kt_n, g=grp: load_w(sl, a, k, g)))
            return ids

        def dense(gids, kt_n, grp, src_fn, rsrc, TTL, consumer, tis=None, mid_hook=None, la=2, hooks=None):
            if tis is None:
                tis = list(range(len(TTL)))
            for gi, gid in enumerate(gids):
                s = ws_use(gid, la)
                wv = wview(s, kt_n, grp)
                for nt in range(grp // 128):
                    for ti in tis:
                        t0, n = TTL[ti]
                        bk = bank()
                        for kt in range(kt_n):
                            MM(ps[:, bk, :n], wv[:, kt, nt * 128:(nt + 1) * 128], src_fn(kt, t0, n),
                               kt == 0, kt == kt_n - 1, [RW[s]] + rsrc[ti], [RB[bk]])
                        consumer(gi * (grp // 128) + nt, ti, t0, n, bk)
                    if hooks is not None and (gi, nt) in hooks:
                        hooks[(gi, nt)]()
                if gi == 0 and mid_hook is not None:
                    mid_hook()

        def ffin_load(l, g):
            def fn(sl):
                load_w(sl, D["w_ff_in"][l][:, 256 * g:256 * g + 256], 8, 256, col_off=0, slot_cols=512)
                load_w(sl, D["w_ff_in"][l][:, 2816 + 256 * g:2816 + 256 * g + 256], 8, 256, col_off=256, slot_cols=512)
            return fn

        WD = {}
        for half_ in range(n_halves):
            for l_ in range(n_layers):
                w = {}
                w["u"] = decl_dense(D["w_in"][l_], 8, 0, 512, 512)
                w["cv"] = decl_dense(D["w_in"][l_], 8, 512, 512, 512)
                w["cg"] = decl_dense(D["w_in"][l_], 8, 1024, 512, 512)
                w["glu1"] = decl_dense(D["w_glu"][l_], 4, 0, 1024, 1024)
                w["glu2"] = decl_dense(D["w_glu"][l_], 4, 1024, 1024, 1024)
                w["gate"] = [None] * 4
                w["gate"][0] = decl_dense(D["w_in"][l_], 8, 1536, 512, 512)
                w["gate"][1] = decl_dense(D["w_in"][l_], 8, 2048, 512, 512)
                w["pw"] = decl_dense(D["w_pw"][l_], 4, 0, 1024, 1024)
                w["gate"][2] = decl_dense(D["w_in"][l_], 8, 2560, 512, 512)
                w["gate"][3] = decl_dense(D["w_in"][l_], 8, 3072, 512, 512)
                w["outA"] = decl_dense(D["w_out"][l_], 8, 0, 1024, 512)
                w["outB"] = decl_dense(D["w_out"][l_], 8, 0, 1024, 512)
                w["ffin"] = [ws_add(ffin_load(l_, g)) for g in range(11)]
                w["ffoutA"] = decl_dense(D["w_ff_out"][l_], 22, 0, 1024, 128)
                w["ffoutB"] = decl_dense(D["w_ff_out"][l_], 22, 0, 1024, 128)
                WD[(half_, l_)] = w

        for half in range(n_halves):
            TTL = [(0, 512), (512, 512)] + ([(1024, 16)] if half == 0 else [])
            NTT = len(TTL)
            rx = [res(f"x{ti}") for ti in range(NTT)]
            rxn = [res(f"xn{ti}") for ti in range(NTT)]
            ru = [[res(f"u{c}_{k}") for k in range(4)] for c in range(9)]
            rcb = [res(f"cb{i}") for i in range(3)]
            rya = [res(f"ya{ti}") for ti in range(NTT)]
            rar = [res(f"ar{ti}") for ti in range(NTT)]
            rar2 = [res(f"ar2_{ti}") for ti in range(NTT)]

            def ru_tile(ti, kt=None):
                cks = [8] if ti == 2 else list(range(4 * ti, 4 * ti + 4))
                if kt is None:
                    return [ru[c][k] for c in cks for k in range(4)]
                return [ru[c][kt] for c in cks]

            if half > 0:
                load_x(half, rx)
            if KSTOP == 4:
                P.finish(); P.emit(ctx); return nc
            for l in range(n_layers):
                P.epoch += 1
                W = WD[(half, l)]
                Tb = [carve(4096 * i, [4, 512], BF16) for i in range(2)]
                Mb = [carve(8192 + 4096 * i, [4, 512], BF16) for i in range(2)]
                Rre = carve(16640, [2048], F32); Rim = carve(24832, [2048], F32)
                Pre = carve(33024, [16, 128], F32); Pim = carve(41216, [16, 128], F32)
                Bm = carve(49408, [4, 2, 512], BF16)
                Cm = carve(57600, [2, 16, 128], BF16)
                rTb = [res("Tb0"), res("Tb1")]; rMb = [res("Mb0"), res("Mb1")]
                rtab = res("s5tab")
                tab_w = [rtab, res("diag"), res("lnst"), res("bufT"), res("xo")] + \
                        [res(f"ar{i}") for i in range(3)] + [res(f"ar2_{i}") for i in range(3)]
                DMA("sp", Rre, scrR[l, 0], [res("scr")], tab_w)
                DMA("sp", Rim, scrR[l, 1], [res("scr")], [rtab])
                DMA("sp", Pre, scrP[l, 0].rearrange("p (j s) -> p j s", s=128), [res("scr")], [rtab])
                DMA("sp", Pim, scrP[l, 1].rearrange("p (j s) -> p j s", s=128), [res("scr")], [rtab])
                DMA("sp", Bm, scrB[l].rearrange("p (k r c) -> p k r c", k=4, r=2), [res("scr")], [rtab])
                DMA("sp", Cm, scrC[l].rearrange("p (r j c) -> p r j c", r=2, j=16), [res("scr")], [rtab])
                gmix_sel = lambda kt, l=l: gmix[:, kt, l:l + 1]
                if l == 0:
                    rmsnorm_to_xn(TTL, rx, rxn, gmix_sel, tis=[0])
                cv32 = carve(0, [4, NTM], F32)

                def cons_u(ntl, ti, t0, n, bk):
                    CP("act", u[:, ntl, t0:t0 + n], ps[:, bk, :n], [RB[bk]], ru_tile(ti, ntl))

                def cons_cv(ntl, ti, t0, n, bk):
                    CP("act", cv32[:, ntl, t0:t0 + n], ps[:, bk, :n], [RB[bk]], [rar[ti]])

                def cons_cg(ntl, ti, t0, n, bk, l=l, half=half):
                    t = tmp()
                    ACT(tmpf[t][:, :n], ps[:, bk, :n], AF.Sigmoid, [RB[bk]], [RT[t]])
                    TT("dve", cv32[:, ntl, t0:t0 + n], cv32[:, ntl, t0:t0 + n], tmpf[t][:, :n], ALU.mult,
                       [rar[ti], RT[t]], [rar[ti]])
                    if ti < 2:
                        CP("act", cbuf[:, ntl, 30 + t0:30 + t0 + n], cv32[:, ntl, t0:t0 + n], [rar[ti]], [rcb[1 + ti]])
                        if ti == 1:
                            CP("dve", ctail32[:, ntl, :], cv32[:, ntl, 994:1024], [rar[ti]], [res("ctail32")])
                    else:
                        CP("dve", cnew32[:, ntl, :], cv32[:, ntl, t0:t0 + n], [rar[ti]], [res("cnew32")])

                xsrc = lambda kt, t0, n: xn[:, kt, t0:t0 + n]
                rxn_l = [[r] for r in rxn]
                CP("pool", cbuf[:, :, 0:30], ctail[:, l, :, :], [res("ctail")], [rcb[0]])
                tis0 = [0]
                tis1 = list(range(1, NTT))
                dense(W["u"], 8, 512, xsrc, rxn_l, TTL, cons_u, tis=tis0, la=2)
                rmsnorm_to_xn(TTL, rx, rxn, gmix_sel, tis=tis1)
                dense(W["cv"], 8, 512, xsrc, rxn_l, TTL, cons_cv, tis=tis0, la=1)
                dense(W["cg"], 8, 512, xsrc, rxn_l, TTL, cons_cg, tis=tis0, la=0)
                dense(W["u"], 8, 512, xsrc, rxn_l, TTL, cons_u, tis=tis1)
                dense(W["cv"], 8, 512, xsrc, rxn_l, TTL, cons_cv, tis=tis1)
                dense(W["cg"], 8, 512, xsrc, rxn_l, TTL, cons_cg, tis=tis1)
                if half == 0:
                    DMA("sp", D["cvs"][l, :, 0:29, :], D["scv"][l].rearrange("(b k) c -> b k c", k=30)[:, 1:30, :],
                        (), [res("cvs")], is_out=True)
                    t = tmp()
                    bk = bank()
                    for ct in range(4):
                        TR(ps[:16, bk, ct * 128:(ct + 1) * 128], cnew32[:, ct, :], ident_f[:], [res("cnew32"), rc], [RB[bk]])
                    CP("dve", tmpf[t][:16, :512], ps[:16, bk, :], [RB[bk]], [RT[t]])
                    DMA("sp", D["cvs"][l, :, 29, :], tmpf[t][:16, :512], [RT[t]], [res("cvs")], is_out=True)
                if half == n_halves - 1:
                    t = tmp()
                    bk = bank()
                    for ct in range(4):
                        TR(ps[:30, bk, ct * 128:(ct + 1) * 128], ctail32[:, ct, :], ident_f[:], [res("ctail32"), rc], [RB[bk]])
                    CP("dve", tmpf[t][:30, :512], ps[:30, bk, :], [RB[bk]], [RT[t]])
                    DMA("sp", D["cvp"][l], tmpf[t][:30, :512], [RT[t]], [res("cvp")], is_out=True)
                P.barrier()

                rH = res("Hst"); rcar = res("carry"); rgc = res("gcol")
                lre = sm["lre"][:, l, :]; lim = sm["lim"][:, l, :]
                gcol = hsm[1]

                def cmul_small(o_re, o_im, a_re_, a_im_, b_re_, b_im_, reads, writes):
                    t1 = hsm[0][:, 0, :]; t2 = hsm[0][:, 1, :]
                    TT("pool", t1, a_re_, b_re_, ALU.mult, reads, [res("hsm")])
                    TT("pool", t2, a_im_, b_im_, ALU.mult, reads, [res("hsm")])
                    TT("pool", o_re, t1, t2, ALU.subtract, [res("hsm")], writes)
                    TT("pool", t1, a_re_, b_im_, ALU.mult, reads, [res("hsm")])
                    TT("pool", t2, a_im_, b_re_, ALU.mult, reads, [res("hsm")])
                    TT("pool", o_im, t1, t2, ALU.add, [res("hsm")], writes)

                if half == 0:
                    MEMSET("pool", Hst[:, l, :, :], 0.0, [rH])
                v3 = lambda ap: ap.rearrange("p (j s) -> p j s", s=128)
                L128 = hsm[2]
                rL = res("L128")
                cmul_small(L128[:, 0, :], L128[:, 1, :], lre, lim, Pre[:, :, 127], Pim[:, :, 127], [rs, rtab], [rL])
                cmul_small(carry[:, 0, :], carry[:, 1, :], lre, lim, Hst[:, l, 0, :], Hst[:, l, 1, :], [rs, rH], [rcar])
                items = [(ck, kt) for ck in range(8) for kt in range(4)]
                Mviews = [[v3(Mb[p_][:, i, :]) for i in range(4)] for p_ in range(2)]

                def stage_A(it):
                    ck, kt = items[it]; pp = it % 2; c0 = ck * 128
                    T = Tb[pp]
                    ruk = [ru[ck][kt]]
                    bre, bim = 0, 1
                    MM(ps[:, bre, :], u[:, kt, c0:c0 + 128], Bm[:, kt, 0, :], True, True, ruk + [rtab], [RB[bre]])
                    MM(ps[:, bim, :], u[:, kt, c0:c0 + 128], Bm[:, kt, 1, :], True, True, ruk + [rtab], [RB[bim]])
                    cs = slice(kt * 512, (kt + 1) * 512)
                    TT("dve", T[:, 0, :], ps[:, bre, :], Rre[:, cs], ALU.mult, [RB[bre], rtab], [rTb[pp]])
                    TT("dve", T[:, 2, :], ps[:, bre, :], Rim[:, cs], ALU.mult, [RB[bre], rtab], [rTb[pp]])
                    STT("dve", T[:, 1, :], ps[:, bim, :], -1.0, Rim[:, cs], ALU.mult, ALU.mult, [RB[bim], rtab], [rTb[pp]])
                    TT("dve", T[:, 3, :], ps[:, bim, :], Rre[:, cs], ALU.mult, [RB[bim], rtab], [rTb[pp]])

                GT = {}

                def stage_B1(it):
                    ck, kt = items[it]; pp = it % 2
                    T = Tb[pp]
                    gre, gim = (2, 3) if pp == 0 else (4, 5)
                    for jm in range(4):
                        MM(ps[:, gre, jm * 128:(jm + 1) * 128], T[:, 0, jm * 128:(jm + 1) * 128], tri_b[:],
                           True, False, [rTb[pp], rc], [RB[gre]])
                        MM(ps[:, gre, jm * 128:(jm + 1) * 128], T[:, 1, jm * 128:(jm + 1) * 128], tri_b[:],
                           False, True, [rTb[pp], rc], [RB[gre]])
                    for jm in range(4):
                        MM(ps[:, gim, jm * 128:(jm + 1) * 128], T[:, 2, jm * 128:(jm + 1) * 128], tri_b[:],
                           True, False, [rTb[pp], rc], [RB[gim]])
                        MM(ps[:, gim, jm * 128:(jm + 1) * 128], T[:, 3, jm * 128:(jm + 1) * 128], tri_b[:],
                           False, True, [rTb[pp], rc], [RB[gim]])
                    ta = tmp()
                    GT[it] = ta
                    gr = v3(tmpf[ta][:, 0:512]); gi = v3(tmpf[ta][:, 512:1024])
                    js = slice(4 * kt, 4 * kt + 4)
                    for jm in range(4):
                        j = 4 * kt + jm
                        ACT(gr[:, jm, :], ps[:, gre, jm * 128:(jm + 1) * 128], AF.Identity, [RB[gre], rcar], [RT[ta]],
                            bias=carry[:, 0, j:j + 1])
                    for jm in range(4):
                        j = 4 * kt + jm
                        ACT(gi[:, jm, :], ps[:, gim, jm * 128:(jm + 1) * 128], AF.Identity, [RB[gim], rcar], [RT[ta]],
                            bias=carry[:, 1, j:j + 1])
                    CP("act", gcol[:, 0, js], gr[:, :, 127], [RT[ta]], [rgc])
                    CP("act", gcol[:, 1, js], gi[:, :, 127], [RT[ta]], [rgc])
                    if kt == 3:
                        if ck < 7:
                            t1 = hsm[0][:, 0, :]; t2 = hsm[0][:, 1, :]; t3 = hsm[3][:, 0, :]; t4 = hsm[3][:, 1, :]
                            rh_ = res("hsm")
                            TT("pool", t1, L128[:, 0, :], gcol[:, 0, :], ALU.mult, [rL, rgc], [rh_])
                            TT("pool", t2, L128[:, 1, :], gcol[:, 1, :], ALU.mult, [rL, rgc], [rh_])
                            TT("pool", t3, L128[:, 0, :], gcol[:, 1, :], ALU.mult, [rL, rgc], [rh_])
                            TT("pool", t4, L128[:, 1, :], gcol[:, 0, :], ALU.mult, [rL, rgc], [rh_])
                            TT("pool", carry[:, 0, :], t1, t2, ALU.subtract, [rh_], [rcar])
                            TT("pool", carry[:, 1, :], t3, t4, ALU.add, [rh_], [rcar])
                        else:
                            cmul_small(Hst[:, l, 0, :], Hst[:, l, 1, :], gcol[:, 0, :], gcol[:, 1, :],
                                       Pre[:, :, 127], Pim[:, :, 127], [rgc, rtab], [rH])

                def stage_B2(it):
                    ck, kt = items[it]; pp = it % 2
                    Mv = Mviews[pp]
                    ta = GT.pop(it)
                    gr = v3(tmpf[ta][:, 0:512]); gi = v3(tmpf[ta][:, 512:1024])
                    js = slice(4 * kt, 4 * kt + 4)
                    TT("pool", Mv[0], gr, Pre[:, js, :], ALU.mult, [RT[ta], rtab], [rMb[pp]])
                    STT("dve", Mv[1], gi, -1.0, Pim[:, js, :], ALU.mult, ALU.mult, [RT[ta], rtab], [rMb[pp]])
                    TT("pool", Mv[2], gr, Pim[:, js, :], ALU.mult, [RT[ta], rtab], [rMb[pp]])
                    TT("pool", Mv[3], gi, Pre[:, js, :], ALU.mult, [RT[ta], rtab], [rMb[pp]])

                def stage_C(it):
                    ck, kt = items[it]; pp = it % 2; c0 = ck * 128
                    Mv = Mviews[pp]
                    ruk = [ru[ck][kt]]
                    bk = 6 + pp
                    for jm in range(4):
                        j = 4 * kt + jm
                        MM(ps[:, bk, :128], Cm[:, 0, j, :], Mv[0][:, jm, :], jm == 0, False, [rtab, rMb[pp]], [RB[bk]])
                        MM(ps[:, bk, :128], Cm[:, 0, j, :], Mv[1][:, jm, :], False, False, [rtab, rMb[pp]], [RB[bk]])
                        MM(ps[:, bk, :128], Cm[:, 1, j, :], Mv[2][:, jm, :], False, False, [rtab, rMb[pp]], [RB[bk]])
                        MM(ps[:, bk, :128], Cm[:, 1, j, :], Mv[3][:, jm, :], False, jm == 3, [rtab, rMb[pp]], [RB[bk]])
                    t = tmp()
                    STT("dve", tmpf[t][:, :128], u[:, kt, c0:c0 + 128], dpar[:, kt, l:l + 1], ps[:, bk, :128],
                        ALU.mult, ALU.add, ruk + [rp, RB[bk]], [RT[t]])
                    ACT(u[:, kt, c0:c0 + 128], tmpf[t][:, :128], AF.Gelu_apprx_tanh, [RT[t]], ruk)

                segs = []
                if half == 0:
                    stgA = ya[:, 3:5, :].rearrange("p a b -> p (a b)").bitcast(F32)
                    sS = ya[:, 5:7, :].rearrange("p a b -> p (a b)").bitcast(F32)
                    sT = ya[:, 7, :].bitcast(F32)
                    rstg = res("s_stg"); rsS = res("s_sS"); rsT = res("s_sT")
                    rhsp = res("hsp"); rhsn = res("hsn"); rhsb = res("hsb"); rbus = res("sqb")
                    sbk = [0]

                    def sbank():
                        sbk[0] ^= 1
                        return 6 + sbk[0]

                    def seg_load(ri, nm, h2):
                        def f():
                            DMA("sp", stgA[:16, 0:1024], D[nm][l][:, h2 * 1024:(h2 + 1) * 1024], (), [rstg])
                            bk = sbank()
                            for b_ in range(8):
                                TR(ps[:, bk, b_ * 16:(b_ + 1) * 16], stgA[:16, b_ * 128:(b_ + 1) * 128], ident_f[:16, :16],
                                   [rstg, rc], [RB[bk]])
                            CP("act", hsp[:, ri, h2 * 8:(h2 + 1) * 8, :],
                               ps[:, bk, 0:128].rearrange("p (j s) -> p j s", s=16), [RB[bk]], [rhsp])
                        return f

                    for ri_, nm_ in enumerate(("sre", "sim")):
                        for h2_ in range(2):
                            segs.append(seg_load(ri_, nm_, h2_))

                    def seg_bu(kt):
                        def f():
                            for ri in range(2):
                                bk = sbank()
                                MM(ps[:16, bk, :], u[:, kt, 1024:1040], Bm[:, kt, ri, :], True, True, ru[8] + [rtab], [RB[bk]])
                                CP("act", bus[:, ri, kt * 512:(kt + 1) * 512], ps[:16, bk, :], [RB[bk]], [rbus])
                        return f

                    for kt_ in range(4):
                        segs.append(seg_bu(kt_))

                    def seg_step():
                        bk = sbank()
                        for ri in range(2):
                            for j in range(16):
                                MM(ps[:, bk, ri * 256 + j * 16:ri * 256 + (j + 1) * 16], bus[:, ri, j * 128:(j + 1) * 128],
                                   ident_b[:16, :16], True, True, [rbus, rc], [RB[bk]])
                        lre_bc = lre.unsqueeze(2).to_broadcast([128, 16, 16])
                        lim_bc = lim.unsqueeze(2).to_broadcast([128, 16, 16])
                        v16 = lambda ap: ap.rearrange("p (j s) -> p j s", s=16)
                        s1 = v16(sS[:, 0:256]); s2 = v16(sS[:, 256:512]); s3 = v16(sS[:, 512:768]); s4 = v16(sS[:, 768:1024])
                        TT("dve", s1, hsp[:, 0, :, :], lre_bc, ALU.mult, [rhsp, rs], [rsS])
                        TT("dve", s2, hsp[:, 1, :, :], lim_bc, ALU.mult, [rhsp, rs], [rsS])
                        TT("dve", s3, hsp[:, 0, :, :], lim_bc, ALU.mult, [rhsp, rs], [rsS])
                        TT("dve", s4, hsp[:, 1, :, :], lre_bc, ALU.mult, [rhsp, rs], [rsS])
                        TT("dve", s1, s1, s2, ALU.subtract, [rsS], [rsS])
                        TT("dve", s3, s3, s4, ALU.add, [rsS], [rsS])
                        TT("dve", hsn[:, 0, :, :], s1, v16(ps[:, bk, 0:256]), ALU.add, [rsS, RB[bk]], [rhsn])
                        TT("dve", hsn[:, 1, :, :], s3, v16(ps[:, bk, 256:512]), ALU.add, [rsS, RB[bk]], [rhsn])
                        CP("act", hsb, hsn, [rhsn], [rhsb])

                    segs.append(seg_step)

                    def seg_y(kt):
                        def f():
                            bk = sbank()
                            for jm in range(4):
                                j = 4 * kt + jm
                                MM(ps[:, bk, :16], Cm[:, 0, j, :], hsb[:, 0, j, :], jm == 0, False, [rtab, rhsb], [RB[bk]])
                                MM(ps[:, bk, :16], Cm[:, 1, j, :], hsb[:, 1, j, :], False, jm == 3, [rtab, rhsb], [RB[bk]])
                            STT("dve", sT[:, kt * 16:(kt + 1) * 16], u[:, kt, 1024:1040], dpar[:, kt, l:l + 1], ps[:, bk, :16],
                                ALU.mult, ALU.add, [ru[8][kt], rp, RB[bk]], [rsT])
                            ACT(u[:, kt, 1024:1040], sT[:, kt * 16:(kt + 1) * 16], AF.Gelu_apprx_tanh, [rsT], [ru[8][kt]])
                        return f

                    for kt_ in range(4):
                        segs.append(seg_y(kt_))

                    def seg_out(ri, nm, h2):
                        def f():
                            for q in range(2):
                                bk = sbank()
                                for jm in range(4):
                                    j = h2 * 8 + q * 4 + jm
                                    TR(ps[:16, bk, jm * 128:(jm + 1) * 128], hsn[:, ri, j, :], ident_f[:], [rhsn, rc], [RB[bk]])
                                CP("act", stgA[:16, q * 512:(q + 1) * 512], ps[:16, bk, :], [RB[bk]], [rstg])
                            DMA("sp", D[nm][l][:, h2 * 1024:(h2 + 1) * 1024], stgA[:16, 0:1024], [rstg], [res(nm)], is_out=True)
                        return f

                    for ri_, nm_ in enumerate(("res", "ims")):
                        for h2_ in range(2):
                            segs.append(seg_out(ri_, nm_, h2_))

                NI = len(items)
                for step in range(NI + 3):
                    if step < NI:
                        stage_A(step)
                    if 0 <= step - 2 < NI:
                        stage_B2(step - 2)
                    if 0 <= step - 1 < NI:
                        stage_B1(step - 1)
                    if 0 <= step - 3 < NI:
                        stage_C(step - 3)
                    if segs and step >= 3 and step % 2 == 1:
                        segs.pop(0)()
                while segs:
                    segs.pop(0)()
                if half == n_halves - 1:
                    for ri, nm in enumerate(("rep", "imp")):
                        bk = bank(); t = tmp()
                        TR(ps[:16, bk, :128], Hst[:, l, ri, :], ident_f[:], [rH, rc], [RB[bk]])
                        CP("dve", tmpf[t][:16, :128], ps[:16, bk, :128], [RB[bk]], [RT[t]])
                        DMA("sp", D[nm][l], tmpf[t][:16, :128], [RT[t]], [res(nm)], is_out=True)

                diag = carve(0, [4, 31, 128], BF16)
                conv32 = carve(31744, [4, NTM], F32)
                cbf = carve(48384, [4, 512], BF16)
                csq = carve(52480, [4, 512], BF16)
                bufT = carve(56576, [4, 16, 30], F32) if half == 0 else None
                rdiag = res("diag")
                s5_alias = [res("Tb0"), res("Tb1"), res("Mb0"), res("Mb1"), res("s5tab")]
                ya_alias = [res(n_) for n_ in ("hsp", "hsn", "hsb", "s_stg", "s_sS", "s_sT")] if half == 0 else []
                idb_bc = ident_b[:].unsqueeze(1).to_broadcast([128, 31, 128])
                for ct in range(4):
                    TT("pool", diag[:, ct, :, :], idb_bc,
                       convw[:, l, ct, :].unsqueeze(2).to_broadcast([128, 31, 128]), ALU.mult, [rc, rp],
                       [rdiag] + (s5_alias if ct == 0 else []))
                usrc = lambda kt, t0, n: u[:, kt, t0:t0 + n]
                ru_l = [ru_tile(ti) for ti in range(NTT)]

                def cons_g1(ntl, ti, t0, n, bk):
                    CP("act", ya[:, ntl, t0:t0 + n], ps[:, bk, :n], [RB[bk]], [rya[ti]] + ya_alias)

                def cons_g2(ntl, ti, t0, n, bk):
                    t = tmp()
                    ACT(tmpf[t][:, :n], ps[:, bk, :n], AF.Sigmoid, [RB[bk]], [RT[t]])
                    TT("dve", ya[:, ntl, t0:t0 + n], ya[:, ntl, t0:t0 + n], tmpf[t][:, :n], ALU.mult,
                       [rya[ti], RT[t]], [rya[ti]])

                dense(W["glu1"], 4, 1024, usrc, ru_l, TTL, cons_g1)
                dense(W["glu2"], 4, 1024, usrc, ru_l, TTL, cons_g2)
                if half == 0:
                    CP("dve", ctail[:, l, :, :], cbuf[:, :, 1024:1054], [rcb[2]], [res("ctail")])
                    for rg in range(4):
                        t = tmp()
                        DMA("sp", tmpf[t][:120, :512], D["scv"][l][rg * 120:(rg + 1) * 120, :], (), [RT[t]])
                        bk = bank()
                        for ct in range(4):
                            TR(ps[:, bk, ct * 120:(ct + 1) * 120], tmpf[t][:120, ct * 128:(ct + 1) * 128],
                               ident_f[:120, :120], [RT[t], rc], [RB[bk]])
                        CP("dve", bufT[:, :, 4 * rg:4 * rg + 4, :],
                           ps[:, bk, :480].rearrange("p (c b k) -> p c b k", c=4, b=4), [RB[bk]],
                           [res("bufT")] + s5_alias)
                    for ct in range(4):
                        t = tmp()
                        pv = tmpf[t][:, 0:480].rearrange("p (b k) -> p b k", k=30)
                        TT("dve", pv, bufT[:, ct, :, :], convw[:, l, ct, 0:30].unsqueeze(1).to_broadcast([128, 16, 30]),
                           ALU.mult, [res("bufT"), rp], [RT[t]])
                        P.op("dve", lambda e, o=tmpf[t][:, 512:528], i=pv: e.tensor_reduce(
                            out=o, in_=i, op=ALU.add, axis=mybir.AxisListType.X), [RT[t]], [RT[t]])
                        STT("dve", tmpf[t][:, 528:544], cnew32[:, ct, :], convw[:, l, ct, 30:31], tmpf[t][:, 512:528],
                            ALU.mult, ALU.add, [res("cnew32"), rp, RT[t]], [RT[t]])
                        ACT(conv32[:, ct, 1024:1040], tmpf[t][:, 528:544], AF.Identity, [RT[t], rp], [rar[2]] + s5_alias,
                            bias=convb[:, ct, l:l + 1])


                cact = u

                def ln_pre(ti):
                    t0, n = TTL[ti]
                    rtmp = res("lnst")
                    ACT(cbf[:, :, :n], conv32[:, :, t0:t0 + n], AF.Copy, [rar[ti]], [rtmp])
                    ACT(csq[:, :, :n], conv32[:, :, t0:t0 + n], AF.Square, [rar[ti]], [rtmp])

                def ln_rest(ti):
                    t0, n = TTL[ti]
                    rtmp = res("lnst")
                    bm = bank()
                    for ct in range(4):
                        MM(ps[:, bm, :n], ones512[:], cbf[:, ct, :n], ct == 0, ct == 3, [rtmp, rc], [RB[bm]])
                    bq = bank()
                    for ct in range(4):
                        MM(ps[:, bq, :n], ones512[:], csq[:, ct, :n], ct == 0, ct == 3, [rtmp, rc], [RB[bq]])
                    tm, tv = tmp(), tmp()
                    CP("act", tmpf[tm][:, :n], ps[:, bm, :n], [RB[bm]], [RT[tm]])
                    TT("dve", tmpf[tv][:, :n], tmpf[tm][:, :n], tmpf[tm][:, :n], ALU.mult, [RT[tm]], [RT[tv]])
                    TT("dve", tmpf[tv][:, :n], ps[:, bq, :n], tmpf[tv][:, :n], ALU.subtract, [RB[bq], RT[tv]], [RT[tv]])
                    TS("dve", tmpf[tv][:, :n], tmpf[tv][:, :n], 0.0, None, ALU.max, None, [RT[tv]], [RT[tv]])
                    ACT(tmpf[tv][:, :n], tmpf[tv][:, :n], AF.Sqrt, [RT[tv], rc], [RT[tv]], bias=epsc[:, 0:1])
                    RECIP(tmpf[tv][:, :n], tmpf[tv][:, :n], [RT[tv]], [RT[tv]])
                    for ct in range(4):
                        tx = tm if ct % 2 == 0 else tv
                        xc = tmpf[tx][:, 512:512 + n]
                        TT("dve", xc, conv32[:, ct, t0:t0 + n], tmpf[tm][:, :n], ALU.subtract,
                           [rar[ti], RT[tm]], [RT[tx]])
                        TT("dve", xc, xc, tmpf[tv][:, :n], ALU.mult, [RT[tx], RT[tv]], [RT[tx]])
                        ACT(cact[:, ct, t0:t0 + n], xc, AF.Silu, [RT[tx], rp], ru_tile(ti, ct),
                            scale=lng[:, ct, l:l + 1], bias=lnb[:, ct, l:l + 1])

                for ti in range(2):
                    t0 = ti * 512
                    for ct in range(4):
                        bk = bank()
                        for k in range(31):
                            MM(ps[:, bk, :], diag[:, ct, k, :], cbuf[:, ct, t0 + k:t0 + k + 512], k == 0, k == 30,
                               [rdiag, rcb[ti], rcb[ti + 1]], [RB[bk]])
                        ACT(conv32[:, ct, t0:t0 + 512], ps[:, bk, :], AF.Identity, [RB[bk], rp], [rar[ti]],
                            bias=convb[:, ct, l:l + 1])
                        if ti == 1 and ct == 1:
                            ln_rest(0)
                    if ti == 0:
                        ln_pre(0)

                ybuf = carve(0, [8, NTM], BF16)
                csrc = lambda kt, t0, n: cact[:, kt, t0:t0 + n]

                def cons_yb(ntl, ti, t0, n, bk):
                    CP("act", ybuf[:, ntl, t0:t0 + n], ps[:, bk, :n], [RB[bk]], [rar2[ti], rdiag])

                def cons_gate1(g):
                    def f(ntl, ti, t0, n, bk):
                        t = tmp()
                        ACT(tmpf[t][:, :n], ps[:, bk, :n], AF.Sigmoid, [RB[bk]], [RT[t]])
                        TT("dve", ya[:, ntl + 4 * g, t0:t0 + n], ya[:, ntl + 4 * g, t0:t0 + n], tmpf[t][:, :n], ALU.mult,
                           [rya[ti], RT[t]], [rya[ti]])
                    return f

                def cons_gate2(g):
                    def f(ntl, ti, t0, n, bk):
                        t = tmp()
                        nt_ = ntl + 4 * g
                        ACT(tmpf[t][:, :n], ps[:, bk, :n], AF.Sigmoid, [RB[bk]], [RT[t]])
                        TT("dve", ybuf[:, nt_, t0:t0 + n], ybuf[:, nt_, t0:t0 + n], tmpf[t][:, :n], ALU.mult,
                           [rar2[ti], RT[t]], [rar2[ti]])
                        TT("dve", ya[:, nt_, t0:t0 + n], ya[:, nt_, t0:t0 + n], ybuf[:, nt_, t0:t0 + n], ALU.add,
                           [rya[ti], rar2[ti]], [rya[ti]])
                    return f

                ln_pre(1)
                dense(W["gate"][0], 8, 512, xsrc, rxn_l, TTL, cons_gate1(0))
                ln_rest(1)
                dense(W["gate"][1], 8, 512, xsrc, rxn_l, TTL, cons_gate1(1))
                if half == 0:
                    ln_pre(2)
                    ln_rest(2)
                dense(W["pw"], 4, 1024, csrc, ru_l, TTL, cons_yb, tis=[0, 1])
                if half == 0:
                    dense(W["pw"], 4, 1024, csrc, ru_l, TTL, cons_yb, tis=[2])
                for g in range(2):
                    dense(W["gate"][2 + g], 8, 512, xsrc, rxn_l, TTL, cons_gate2(g))
                ysrc = lambda kt, t0, n: ya[:, kt, t0:t0 + n]
                rya_l = [[r] for r in rya]

                def cons_res(ntl, ti, t0, n, bk):
                    TT("dve", x[:, ntl, t0:t0 + n], x[:, ntl, t0:t0 + n], ps[:, bk, :n], ALU.add, [rx[ti], RB[bk]], [rx[ti]])

                tisA = [0]
                tisB = list(range(1, NTT))
                gffn_sel = lambda kt, l=l: gffn[:, kt, l:l + 1]
                dense(W["outA"], 8, 512, ysrc, rya_l, TTL, cons_res, tis=tisA)
                hk = {}

                def hk_stats(sel, hk=hk):
                    (hk["t"],) = rmsnorm_stats([TTL[0]], [rx[0]], sel)

                dense(W["outB"], 8, 512, ysrc, rya_l, TTL, cons_res, tis=tisB,
                      hooks={(0, 1): (lambda: hk_stats(gffn_sel)),
                             (1, 0): (lambda: rms_apply(TTL, rx, rxn, gffn_sel, 0, hk["t"]))})

                actb = carve(0, [22, NTM], BF16)
                ffin_passes = [(0, [0], 1), ("norm", None, None), (1, [0], 1), (0, tisB, 2), (1, tisB, 2)] + \
                              [(g, list(range(NTT)), 2) for g in range(2, 11)]
                for g, tis_g, la_g in ffin_passes:
                    if g == "norm":
                        rmsnorm_to_xn(TTL, rx, rxn, gffn_sel, tis=tisB)
                        continue
                    s = ws_use(W["ffin"][g], la_g)
                    wv = wview(s, 8, 512)
                    for ti in tis_g:
                        t0, n = TTL[ti]
                        for jj in range(2):
                            j = 2 * g + jj
                            b1 = bank()
                            for kt in range(8):
                                MM(ps[:, b1, :n], wv[:, kt, jj * 128:(jj + 1) * 128], xn[:, kt, t0:t0 + n], kt == 0, kt == 7,
                                   [RW[s], rxn[ti]], [RB[b1]])
                            b2 = bank()
                            for kt in range(8):
                                MM(ps[:, b2, :n], wv[:, kt, 256 + jj * 128:256 + (jj + 1) * 128], xn[:, kt, t0:t0 + n],
                                   kt == 0, kt == 7, [RW[s], rxn[ti]], [RB[b2]])
                            t = tmp()
                            ACT(tmpf[t][:, :n], ps[:, b1, :n], AF.Silu, [RB[b1]], [RT[t]])
                            TT("dve", actb[:, j, t0:t0 + n], tmpf[t][:, :n], ps[:, b2, :n], ALU.mult,
                               [RT[t], RB[b2]], [rar[ti], rar2[ti]])
                asrc = lambda kt, t0, n: actb[:, kt, t0:t0 + n]
                rar_l = [[r] for r in rar]
                dense(W["ffoutA"], 22, 128, asrc, rar_l, TTL, cons_res, tis=tisA)
                nxt_hooks = None
                if l + 1 < n_layers:
                    gnext = lambda kt, l=l: gmix[:, kt, l + 1:l + 2]
                    hk2 = {}

                    def hk2_stats(hk2=hk2, gnext=gnext):
                        (hk2["t"],) = rmsnorm_stats([TTL[0]], [rx[0]], gnext)

                    nxt_hooks = {(1, 0): hk2_stats,
                                 (4, 0): (lambda hk2=hk2, gnext=gnext: rms_apply(TTL, rx, rxn, gnext, 0, hk2["t"]))}
                dense(W["ffoutB"], 22, 128, asrc, rar_l, TTL, cons_res, tis=tisB, hooks=nxt_hooks)

            P.epoch += 1
            xo = ya[:, :, :].rearrange("p a b -> p (a b)").bitcast(F32)[:, 0:4096].rearrange("p (k t) -> p k t", t=512)
            rxo = [res("xo")] + list(rya)
            for ti, (t0, n) in enumerate(TTL):
                (t,) = rmsnorm_stats([TTL[ti]], [rx[ti]], None)
                for kt in range(8):
                    STT("dve", xo[:, kt, :n], x[:, kt, t0:t0 + n], gfin[:, kt, 0:1], tmpf[t][:, :n],
                        ALU.mult, ALU.mult, [rx[ti], RT[t], rp], rxo)
                nb = (n + 127) // 128
                for b in range(nb):
                    rows = min(128, n - b * 128)
                    to = tmp()
                    for q in range(2):
                        bk = bank()
                        for k4 in range(4):
                            kt = 4 * q + k4
                            TR(ps[:rows, bk, k4 * 128:(k4 + 1) * 128], xo[:, kt, b * 128:b * 128 + rows], ident_f[:],
                               rxo + [rc], [RB[bk]])
                        CP("act" if q == 0 else "dve", tmpf[to][:rows, q * 512:(q + 1) * 512], ps[:rows, bk, :], [RB[bk]], [RT[to]])
                    if ti < 2:
                        r0 = half * 1024 + t0 + b * 128
                        DMA("sp", D["yp"][r0:r0 + 128, :], tmpf[to][:rows, :], [RT[to]], [res("yp")], is_out=True)
                    else:
                        DMA("sp", D["ys"], tmpf[to][:rows, :], [RT[to]], [res("ys")], is_out=True)

        P.finish()
        P.emit(ctx)
    return nc


_NC_CACHE = {}


def _consts():
    k = {}
    k["k_ident"] = np.eye(128, dtype=np.float32)
    k["k_tri"] = np.triu(np.ones((128, 128), dtype=np.float32))
    k["k_svec"] = np.tile(np.arange(128, dtype=np.float32)[None, :], (128, 1))
    k["k_pidx"] = np.arange(128, dtype=np.float32)[:, None].copy()
    par = np.zeros((128, 2), dtype=np.float32)
    gl = (np.arange(128) // 16) % 2
    par[:, 0] = (gl == 0)
    par[:, 1] = (gl == 1)
    k["k_par"] = par
    return k


def kernel(x_prompt, x_sample, state_ssm_re, state_ssm_im, state_conv,
           g_mix, w_in, ssm_a_re, ssm_a_im, ssm_log_dt, ssm_b_re, ssm_b_im,
           ssm_c_re, ssm_c_im, ssm_d, w_glu, conv_w, conv_b, conv_ln_g, conv_ln_b,
           w_pw, w_out, g_ffn, w_ff_in, w_ff_out, g_final):
    f = lambda a: np.ascontiguousarray(np.asarray(a, dtype=np.float32))
    if "nc" not in _NC_CACHE:
        _NC_CACHE["nc"] = build()
    nc = _NC_CACHE["nc"]
    shared = dict(
        g_mix=f(g_mix), w_in=f(w_in), a_re=f(ssm_a_re).reshape(4, 2048), a_im=f(ssm_a_im).reshape(4, 2048),
        log_dt=f(ssm_log_dt), b_re=f(ssm_b_re).reshape(4, 2048, 16), b_im=f(ssm_b_im).reshape(4, 2048, 16),
        c_re=f(ssm_c_re).reshape(4, 512, 64), c_im=f(ssm_c_im).reshape(4, 512, 64), ssm_d=f(ssm_d),
        w_glu=f(w_glu), conv_w=f(conv_w), conv_b=f(conv_b), ln_g=f(conv_ln_g), ln_b=f(conv_ln_b),
        w_pw=f(w_pw), w_out=f(w_out), g_ffn=f(g_ffn), w_ff_in=f(w_ff_in), w_ff_out=f(w_ff_out),
        g_final=f(g_final).reshape(1, 1024),
    )
    shared.update(_consts())
    xp = f(x_prompt); xs = f(x_sample).reshape(128, 1024)
    sre = f(state_ssm_re).reshape(4, 128, 2048); sim = f(state_ssm_im).reshape(4, 128, 2048)
    scv = f(state_conv).reshape(4, 128, 30, 512)
    in_maps = []
    for c in range(8):
        m = dict(shared)
        m["xp"] = xp[c]
        m["xs"] = np.ascontiguousarray(xs[16 * c:16 * c + 16])
        m["sre"] = np.ascontiguousarray(sre[:, 16 * c:16 * c + 16])
        m["sim"] = np.ascontiguousarray(sim[:, 16 * c:16 * c + 16])
        m["scv"] = np.ascontiguousarray(scv[:, 16 * c:16 * c + 16].reshape(4, 480, 512))
        in_maps.append(m)
    res = run_bass_kernel_spmd(nc, in_maps, core_ids=list(range(8)))
    r = res.results
    y_prompt = np.stack([r[c]["yp"] for c in range(8)], 0).astype(np.float32)
    y_sample = np.concatenate([r[c]["ys"] for c in range(8)], 0).reshape(128, 1, 1024).astype(np.float32)
    re_p = np.stack([r[c]["rep"].reshape(4, 32, 64) for c in range(8)], 1).astype(np.float32)
    im_p = np.stack([r[c]["imp"].reshape(4, 32, 64) for c in range(8)], 1).astype(np.float32)
    conv_p = np.stack([r[c]["cvp"] for c in range(8)], 1).astype(np.float32)
    re_s = np.concatenate([r[c]["res"].reshape(4, 16, 32, 64) for c in range(8)], 1).astype(np.float32)
    im_s = np.concatenate([r[c]["ims"].reshape(4, 16, 32, 64) for c in range(8)], 1).astype(np.float32)
    conv_s = np.concatenate([r[c]["cvs"] for c in range(8)], 1).astype(np.float32)
    return (y_prompt, y_sample, re_p, im_p, conv_p, re_s, im_s, conv_s)
```
